# Optimizing a Trainium2 kernel written in Bass

```python
import math
import jax, jax.numpy as jnp
from jax import lax
import numpy as np

D_MODEL = 4096
BATCH = 4
SEQ = 2048
DEPTH = 2
DEC_BATCH = 128
DEC_SEQ = 1
PAST_LEN = 16384
PAGE_SIZE = 128

MIX_WIDTH = 2 * D_MODEL
SSD_WIDTH = MIX_WIDTH // 2
SSD_HEAD_DIM = 64
SSD_HEADS = SSD_WIDTH // SSD_HEAD_DIM
SSD_GROUPS = 8
SSD_HEADS_PER_GROUP = SSD_HEADS // SSD_GROUPS
SSD_STATE = 128
SSD_CONV = 4
SSD_CONV_DIM = SSD_WIDTH + 2 * SSD_GROUPS * SSD_STATE
RWKV_WIDTH = MIX_WIDTH - SSD_WIDTH
RWKV_HEAD_DIM = 64
RWKV_HEADS = RWKV_WIDTH // RWKV_HEAD_DIM
DECAY_LORA = 128
AAA_LORA = 128
SHIFT_DIM = 3 * RWKV_WIDTH + DECAY_LORA + AAA_LORA
AB_IN = SSD_WIDTH + SSD_CONV_DIM + SSD_HEADS + SHIFT_DIM + RWKV_WIDTH
RET_HEADS = 16
RET_QK_DIM = D_MODEL // RET_HEADS
RET_V_DIM = 2 * RET_QK_DIM
RET_QK_WIDTH = RET_HEADS * RET_QK_DIM
RET_WIDTH = RET_HEADS * RET_V_DIM
RET_IN = 2 * RET_QK_WIDTH + 2 * RET_WIDTH
ROPE_BASE = 10000.0
CHUNK = 128
N_AB_LAYERS = (DEPTH + 1) // 2
N_C_LAYERS = DEPTH // 2
ALPHA = (2 * DEPTH) ** 0.25
BETA = (8 * DEPTH) ** -0.25
LN_EPS = 1e-5
RMS_EPS = 1e-5
RWKV_GN_EPS = 64e-5
RET_GN_EPS = 1e-6

kernel_name = 'hybrid_ssd_rwkv7_retention_step'


def _chunk_len(l):
    return CHUNK if l % CHUNK == 0 else l


def _to_chunks(t, L):
    b, l = t.shape[:2]
    return jnp.moveaxis(t.reshape(b, l // L, L, *t.shape[2:]), 1, 0)


def _from_chunks(t):
    c, b, L = t.shape[:3]
    return jnp.moveaxis(t, 0, 1).reshape(b, c * L, *t.shape[3:])


def layer_norm(x, w, b):
    xf = x.astype(jnp.float32)
    mu = jnp.mean(xf, -1, keepdims=True)
    var = jnp.mean(jnp.square(xf - mu), -1, keepdims=True)
    return ((xf - mu) * lax.rsqrt(var + LN_EPS) * w + b).astype(x.dtype)


def head_norm(t, eps):
    mu = jnp.mean(t, -1, keepdims=True)
    var = jnp.mean(jnp.square(t - mu), -1, keepdims=True)
    return (t - mu) * lax.rsqrt(var + eps)


def group_rms_norm(y, groups, w):
    b, l, c = y.shape
    yg = y.reshape(b, l, groups, c // groups)
    yg = yg * lax.rsqrt(jnp.mean(jnp.square(yg), -1, keepdims=True) + RMS_EPS)
    return yg.reshape(b, l, c) * w


def causal_dwconv(u, buf, w, bias):
    l = u.shape[1]
    up = jnp.concatenate([buf.astype(u.dtype), u], axis=1)
    out = bias + sum(up[:, k:k + l] * w[k] for k in range(SSD_CONV))
    return out, up[:, l:]


def rotary(t, pos):
    half = t.shape[-1] // 2
    freq = ROPE_BASE ** (-jnp.arange(half, dtype=jnp.float32) / half)
    ang = pos.astype(jnp.float32)[:, None] * freq[None]
    cos = jnp.cos(ang)[None, :, None]
    sin = jnp.sin(ang)[None, :, None]
    t1, t2 = t[..., :half], t[..., half:]
    return jnp.concatenate([t1 * cos - t2 * sin, t1 * sin + t2 * cos], -1)


def ssd_chunked(xdt, adt, bm, cm, s0):
    L = _chunk_len(xdt.shape[1])
    causal = jnp.tril(jnp.ones((L, L), dtype=bool))[None, :, :, None, None]

    def step(s, inp):
        xc, ac, bc, cc = inp
        cum = jnp.cumsum(ac, axis=1)
        seg = cum[:, :, None] - cum[:, None, :]
        decay = jnp.exp(jnp.where(causal, seg, -jnp.inf))
        cb = jnp.einsum('blgn,bsgn->blsg', cc, bc)
        y = jnp.einsum('blsgr,bsgrp->blgrp', cb[..., None] * decay, xc)
        y = y + jnp.einsum('blgn,bgrpn->blgrp', cc, s) * jnp.exp(cum)[..., None]
        tail = jnp.exp(cum[:, -1:] - cum)
        s = s * jnp.exp(cum[:, -1])[..., None, None] + jnp.einsum('blgn,blgrp->bgrpn', bc, xc * tail[..., None])
        return s, y

    s, ys = lax.scan(step, s0, tuple(_to_chunks(t, L) for t in (xdt, adt, bm, cm)))
    return _from_chunks(ys), s


def wkv7_scan(r, w, k, v, kk, a, s0):
    def step(s, inp):
        rt, wt, kt, vt, kkt, at = inp
        sk = jnp.einsum('bhvk,bhk->bhv', s, kkt)
        s = s * wt[:, :, None, :] - sk[..., None] * (kkt * at)[:, :, None, :] + vt[..., None] * kt[:, :, None, :]
        return s, jnp.einsum('bhvk,bhk->bhv', s, rt)

    xs = tuple(jnp.moveaxis(t, 1, 0) for t in (r, w, k, v, kk, a))
    s, ys = lax.scan(step, s0, xs)
    return jnp.moveaxis(ys, 0, 1), s


def retention_chunked(q, k, v, s0):
    L = _chunk_len(q.shape[1])
    log_g = jnp.log1p(-jnp.exp2(-5.0 - jnp.arange(RET_HEADS, dtype=jnp.float32)))
    i = jnp.arange(L, dtype=jnp.float32)
    rel = (i[:, None] - i[None, :])[None]
    inner = jnp.exp(jnp.where(rel >= 0, rel * log_g[:, None, None], -jnp.inf))
    q_decay = jnp.exp((i[:, None] + 1.0) * log_g[None])
    k_decay = jnp.exp((L - 1.0 - i)[:, None] * log_g[None])
    chunk_decay = jnp.exp(L * log_g)

    def step(s, inp):
        qc, kc, vc = inp
        sc = jnp.einsum('blhd,bshd->bhls', qc, kc) * inner
        y = jnp.einsum('bhls,bshv->blhv', sc, vc) + jnp.einsum('blhd,bhdv->blhv', qc, s) * q_decay[None, :, :, None]
        s = s * chunk_decay[None, :, None, None] + jnp.einsum('blhd,blhv->bhdv', kc * k_decay[None, :, :, None], vc)
        return s, y

    s, ys = lax.scan(step, s0, tuple(_to_chunks(t, L) for t in (q, k, v)))
    return _from_chunks(ys), s


def ab_layer(x, conv_buf, ssm_s, shift_buf, wkv_s, w_in, conv_w, conv_b, dt_bias, a_log, d_skip, norm_w,
             mu, w0, w_up, a0, a_up, k_k, k_a, r_k, lnx_w, lnx_b, w_out, ln_w, ln_b):
    f32 = jnp.float32
    b, l, _ = x.shape
    proj = jnp.einsum('bld,de->ble', x, w_in)
    z, xbc, dt, p, g = jnp.split(proj, np.cumsum([SSD_WIDTH, SSD_CONV_DIM, SSD_HEADS, SHIFT_DIM]).tolist(), axis=-1)

    xbc, new_conv = causal_dwconv(xbc, conv_buf, conv_w, conv_b)
    xbc = jax.nn.silu(xbc).astype(f32)
    xs, bm, cm = jnp.split(xbc, [SSD_WIDTH, SSD_WIDTH + SSD_GROUPS * SSD_STATE], axis=-1)
    xs = xs.reshape(b, l, SSD_GROUPS, SSD_HEADS_PER_GROUP, SSD_HEAD_DIM)
    bm = bm.reshape(b, l, SSD_GROUPS, SSD_STATE)
    cm = cm.reshape(b, l, SSD_GROUPS, SSD_STATE)
    dt = jax.nn.softplus(dt.astype(f32) + dt_bias).reshape(b, l, SSD_GROUPS, SSD_HEADS_PER_GROUP)
    a = -jnp.exp(a_log.astype(f32)).reshape(SSD_GROUPS, SSD_HEADS_PER_GROUP)
    s0 = ssm_s.astype(f32).reshape(b, SSD_GROUPS, SSD_HEADS_PER_GROUP, SSD_HEAD_DIM, SSD_STATE)
    y, s_ssm = ssd_chunked(xs * dt[..., None], dt * a, bm, cm, s0)
    y = y + xs * d_skip.reshape(SSD_GROUPS, SSD_HEADS_PER_GROUP)[..., None]
    y = y.reshape(b, l, SSD_WIDTH) * jax.nn.silu(z.astype(f32))
    y_a = group_rms_norm(y, SSD_GROUPS, norm_w)
    new_ssm = s_ssm.reshape(b, SSD_HEADS, SSD_HEAD_DIM, SSD_STATE)

    prev = jnp.concatenate([shift_buf.astype(p.dtype), p[:, :-1]], axis=1)
    new_shift = p[:, -1:]
    m = (p + (prev - p) * mu).astype(f32)
    r, k, v, wd, ad = jnp.split(m, np.cumsum([RWKV_WIDTH, RWKV_WIDTH, RWKV_WIDTH, DECAY_LORA]).tolist(), axis=-1)
    wlog = -jax.nn.softplus(-(w0 + jnp.tanh(wd) @ w_up)) - 0.5
    decay = jnp.exp(-jnp.exp(wlog))
    aa = jax.nn.sigmoid(a0 + ad @ a_up)
    heads = lambda t: t.reshape(b, l, RWKV_HEADS, RWKV_HEAD_DIM)
    kk = heads(k * k_k)
    kk = kk * lax.rsqrt(jnp.maximum(jnp.sum(kk * kk, -1, keepdims=True), 1e-24))
    k = k * (1.0 + (aa - 1.0) * k_a)
    r, decay, k, v, aa = heads(r), heads(decay), heads(k), heads(v), heads(aa)
    o, s_wkv = wkv7_scan(r, decay, k, v, kk, aa, wkv_s.astype(f32))
    o = head_norm(o, RWKV_GN_EPS).reshape(b, l, RWKV_WIDTH) * lnx_w + lnx_b
    o = o + (jnp.sum(r * k * r_k, -1, keepdims=True) * v).reshape(b, l, RWKV_WIDTH)
    y_b = o * jax.nn.silu(g.astype(f32))

    out = jnp.einsum('ble,ed->bld', jnp.concatenate([y_a, y_b], -1).astype(x.dtype), w_out)
    x = layer_norm(ALPHA * x + out, ln_w, ln_b)
    return x, new_conv, new_ssm, new_shift, s_wkv


def ret_layer(x, ret_s, pos, w_in, gn_w, w_out, ln_w, ln_b):
    f32 = jnp.float32
    b, l, _ = x.shape
    proj = jnp.einsum('bld,de->ble', x, w_in).astype(f32)
    q, k, v, g = jnp.split(proj, [RET_QK_WIDTH, 2 * RET_QK_WIDTH, 2 * RET_QK_WIDTH + RET_WIDTH], axis=-1)
    q = rotary(q.reshape(b, l, RET_HEADS, RET_QK_DIM), pos) * RET_QK_DIM ** -0.5
    k = rotary(k.reshape(b, l, RET_HEADS, RET_QK_DIM), pos)
    v = v.reshape(b, l, RET_HEADS, RET_V_DIM)
    o, s = retention_chunked(q, k, v, ret_s.astype(f32))
    o = head_norm(o, RET_GN_EPS).reshape(b, l, RET_WIDTH) * gn_w
    y = o * jax.nn.silu(g)
    out = jnp.einsum('ble,ed->bld', y.astype(x.dtype), w_out)
    x = layer_norm(ALPHA * x + out, ln_w, ln_b)
    return x, s


def setup_inputs(seed: int = 0) -> dict:
    key = jax.random.key(seed)
    ks = jax.random.split(key, 32)
    nab, nc = N_AB_LAYERS, N_C_LAYERS

    def nrm(k, shape, scale):
        return jax.random.normal(k, shape, jnp.float32) * scale

    def uni(k, shape, lo, hi):
        return jax.random.uniform(k, shape, jnp.float32, lo, hi)

    dt0 = jnp.exp(uni(ks[10], (nab, SSD_HEADS), math.log(1e-3), math.log(1e-1)))
    return {
        'x_prompt': nrm(ks[0], (BATCH, SEQ, D_MODEL), 1.0),
        'x_sample': nrm(ks[1], (DEC_BATCH, DEC_SEQ, D_MODEL), 1.0),
        'state_conv': nrm(ks[2], (nab, DEC_BATCH, SSD_CONV - 1, SSD_CONV_DIM), 1.0),
        'state_ssm': nrm(ks[3], (nab, DEC_BATCH, SSD_HEADS, SSD_HEAD_DIM, SSD_STATE), 0.5),
        'state_shift': nrm(ks[4], (nab, DEC_BATCH, 1, SHIFT_DIM), 1.0),
        'state_wkv': nrm(ks[5], (nab, DEC_BATCH, RWKV_HEADS, RWKV_HEAD_DIM, RWKV_HEAD_DIM), 0.5),
        'state_ret': nrm(ks[6], (nc, DEC_BATCH, RET_HEADS, RET_QK_DIM, RET_V_DIM), 1.0),
        'ab_w_in': nrm(ks[7], (nab, D_MODEL, AB_IN), D_MODEL ** -0.5),
        'ssd_conv_w': nrm(ks[8], (nab, SSD_CONV, SSD_CONV_DIM), SSD_CONV ** -0.5),
        'ssd_conv_b': nrm(ks[9], (nab, SSD_CONV_DIM), 0.02),
        'ssd_dt_bias': dt0 + jnp.log(-jnp.expm1(-dt0)),
        'ssd_a_log': jnp.log(uni(ks[11], (nab, SSD_HEADS), 1.0, 16.0)),
        'ssd_d': 1.0 + nrm(ks[12], (nab, SSD_HEADS), 0.1),
        'ssd_norm_w': 1.0 + nrm(ks[13], (nab, SSD_WIDTH), 0.02),
        'rwkv_mu': uni(ks[14], (nab, SHIFT_DIM), 0.0, 1.0),
        'rwkv_w0': uni(ks[15], (nab, RWKV_WIDTH), -6.0, 1.0),
        'rwkv_w_up': nrm(ks[16], (nab, DECAY_LORA, RWKV_WIDTH), 0.1 * DECAY_LORA ** -0.5),
        'rwkv_a0': nrm(ks[17], (nab, RWKV_WIDTH), 0.1),
        'rwkv_a_up': nrm(ks[18], (nab, AAA_LORA, RWKV_WIDTH), 0.1 * AAA_LORA ** -0.5),
        'rwkv_k_k': 0.85 + nrm(ks[19], (nab, RWKV_WIDTH), 0.02),
        'rwkv_k_a': 1.0 + nrm(ks[20], (nab, RWKV_WIDTH), 0.02),
        'rwkv_r_k': nrm(ks[21], (nab, RWKV_HEADS, RWKV_HEAD_DIM), 0.1),
        'rwkv_lnx_w': 1.0 + nrm(ks[22], (nab, RWKV_WIDTH), 0.02),
        'rwkv_lnx_b': nrm(ks[23], (nab, RWKV_WIDTH), 0.02),
        'ab_w_out': nrm(ks[24], (nab, MIX_WIDTH, D_MODEL), BETA * MIX_WIDTH ** -0.5),
        'ab_ln_w': 1.0 + nrm(ks[25], (nab, D_MODEL), 0.02),
        'ab_ln_b': nrm(ks[26], (nab, D_MODEL), 0.02),
        'ret_w_in': nrm(ks[27], (nc, D_MODEL, RET_IN), D_MODEL ** -0.5),
        'ret_gn_w': 1.0 + nrm(ks[28], (nc, RET_WIDTH), 0.02),
        'ret_w_out': nrm(ks[29], (nc, RET_WIDTH, D_MODEL), BETA * RET_WIDTH ** -0.5),
        'ret_ln_w': 1.0 + nrm(ks[30], (nc, D_MODEL), 0.02),
        'ret_ln_b': nrm(ks[31], (nc, D_MODEL), 0.02),
    }


def reference(x_prompt, x_sample, state_conv, state_ssm, state_shift, state_wkv, state_ret,
              ab_w_in, ssd_conv_w, ssd_conv_b, ssd_dt_bias, ssd_a_log, ssd_d, ssd_norm_w,
              rwkv_mu, rwkv_w0, rwkv_w_up, rwkv_a0, rwkv_a_up, rwkv_k_k, rwkv_k_a, rwkv_r_k,
              rwkv_lnx_w, rwkv_lnx_b, ab_w_out, ab_ln_w, ab_ln_b,
              ret_w_in, ret_gn_w, ret_w_out, ret_ln_w, ret_ln_b):
    bp = x_prompt.shape[0]
    pos_p = jnp.arange(x_prompt.shape[1])
    pos_s = PAST_LEN + jnp.arange(x_sample.shape[1])
    hp, hs = x_prompt, x_sample
    pc, pssm, pshift, pwkv, pret = [], [], [], [], []
    sc, sssm, sshift, swkv, sret = [], [], [], [], []
    for layer in range(DEPTH):
        j = layer // 2
        if layer % 2 == 0:
            wab = [t[j] for t in (ab_w_in, ssd_conv_w, ssd_conv_b, ssd_dt_bias, ssd_a_log, ssd_d, ssd_norm_w,
                                  rwkv_mu, rwkv_w0, rwkv_w_up, rwkv_a0, rwkv_a_up, rwkv_k_k, rwkv_k_a, rwkv_r_k,
                                  rwkv_lnx_w, rwkv_lnx_b, ab_w_out, ab_ln_w, ab_ln_b)]
            hp, c, s, sh, wk = ab_layer(
                hp,
                jnp.zeros((bp, SSD_CONV - 1, SSD_CONV_DIM), hp.dtype),
                jnp.zeros((bp, SSD_HEADS, SSD_HEAD_DIM, SSD_STATE), jnp.float32),
                jnp.zeros((bp, 1, SHIFT_DIM), hp.dtype),
                jnp.zeros((bp, RWKV_HEADS, RWKV_HEAD_DIM, RWKV_HEAD_DIM), jnp.float32),
                *wab)
            pc.append(c); pssm.append(s); pshift.append(sh); pwkv.append(wk)
            hs, c, s, sh, wk = ab_layer(hs, state_conv[j], state_ssm[j], state_shift[j], state_wkv[j], *wab)
            sc.append(c); sssm.append(s); sshift.append(sh); swkv.append(wk)
        else:
            wc = [t[j] for t in (ret_w_in, ret_gn_w, ret_w_out, ret_ln_w, ret_ln_b)]
            hp, s = ret_layer(hp, jnp.zeros((bp, RET_HEADS, RET_QK_DIM, RET_V_DIM), jnp.float32), pos_p, *wc)
            pret.append(s)
            hs, s = ret_layer(hs, state_ret[j], pos_s, *wc)
            sret.append(s)
    return (hp, hs,
            jnp.stack(pc), jnp.stack(pssm), jnp.stack(pshift), jnp.stack(pwkv), jnp.stack(pret),
            jnp.stack(sc), jnp.stack(sssm), jnp.stack(sshift), jnp.stack(swkv), jnp.stack(sret))
```

```python
import contextlib
import numpy as np
import concourse.bass as bass
import concourse.mybir as mybir
from concourse.bass_utils import run_bass_kernel_spmd

F32 = mybir.dt.float32
BF16 = mybir.dt.bfloat16
AF = mybir.ActivationFunctionType
ALU = mybir.AluOpType
AX = mybir.AxisListType

D = 4096
TP = 2048
NS = 16
T = TP + NS
KC = D // 128
AB_IN = 26944
RET_IN = 24576
NCORES = 8
SEM_LIMIT = 60000


class Buf:
    def __init__(self, t, name="", parent=None, excl=False):
        self.t = t
        self.name = name
        self.parent = parent
        self.excl = excl or (parent is not None and parent.excl)
        self._w = None
        self._r = {}

    @property
    def w(self):
        return self.parent.w if self.parent is not None else self._w

    @w.setter
    def w(self, v):
        if self.parent is not None:
            self.parent.w = v
        else:
            self._w = v

    @property
    def r(self):
        return self.parent.r if self.parent is not None else self._r

    @r.setter
    def r(self, v):
        if self.parent is not None:
            self.parent.r = v
        else:
            self._r = v

    def __getitem__(self, idx):
        return self.t[idx]


class KB:
    def __init__(self, nc, es, plan=None):
        self.nc = nc
        self.es = es
        self.sem_es = es
        self.plan = plan
        self.needed = set()
        self.engs = {"pe": nc.tensor, "dve": nc.vector, "act": nc.scalar,
                     "pool": nc.gpsimd, "sp": nc.sync}
        self.csem = {}
        self.shadow = {k: 0 for k in self.engs}
        self.real = {}
        self.waited = {k: {} for k in self.engs}
        self.dsem = {}
        self.dnext = {}
        self.nsem = 0
        self.all_sems = []
        for k in self.engs:
            self.csem[k] = [self._newsem(), 0]
        for k in ("sp", "act", "pool"):
            self.dsem[k] = [[self._newsem(), 0] for _ in range(8)]
            self.dnext[k] = 0
        self.ninstr = 0
        self.ninc = 0

    def _newsem(self):
        s = self.sem_es.enter_context(self.nc.semaphore(f"s{self.nsem}"))
        self.nsem += 1
        self.all_sems.append(s)
        return s

    def _uniq(self, name):
        self.nalloc = getattr(self, "nalloc", 0) + 1
        return f"{name}_{self.nalloc}"

    def sb(self, name, shape, dt):
        name = self._uniq(name)
        return Buf(self.es.enter_context(self.nc.sbuf_tensor(name, shape, dt)), name)

    def ps(self, name, shape, dt=F32):
        name = self._uniq(name)
        return Buf(self.es.enter_context(self.nc.psum_tensor(name, shape, dt)), name, excl=True)

    def last(self, en):
        return ("c", en, self.shadow[en])

    @staticmethod
    def _kv(ev):
        if ev[0] == "c":
            return ("c", ev[1]), ev[2]
        return ev[0], ev[1]

    def _wait(self, en, evs):
        need = {}
        for ev in evs:
            if ev is None:
                continue
            key, v = self._kv(ev)
            if v > need.get(key, 0):
                need[key] = v
        wd = self.waited[en]
        for key, v in need.items():
            if wd.get(key, 0) >= v:
                continue
            if isinstance(key, tuple):
                self.needed.add((key[1], v))
                s, rv = self.real[(key[1], v)]
            else:
                s, rv = key, v
            self.engs[en].wait_ge(s, rv)
            wd[key] = v
            self.ninstr += 1

    def _deps(self, en, reads, writes):
        evs = []
        own = ("c", en)
        for b in reads:
            evs.append(b.w)
            if b.excl:
                for key, v in b.r.items():
                    if key != own:
                        evs.append(key + (v,) if isinstance(key, tuple) else (key, v))
        for b in writes:
            evs.append(b.w)
            for key, v in b.r.items():
                if key == own:
                    continue
                evs.append(key + (v,) if isinstance(key, tuple) else (key, v))
        return evs

    def _record(self, ev, reads, writes):
        key, v = self._kv(ev)
        for b in reads:
            if v > b.r.get(key, 0):
                b.r[key] = v
        for b in writes:
            b.w = ev
            b.r = {}

    def _cinc(self, en, ins):
        self.shadow[en] += 1
        n = self.shadow[en]
        if self.plan is None or (en, n) in self.plan:
            cs = self.csem[en]
            if cs[1] >= SEM_LIMIT:
                cs[0] = self._newsem()
                cs[1] = 0
            cs[1] += 1
            ins.then_inc(cs[0], 1)
            self.real[(en, n)] = (cs[0], cs[1])
            self.ninc += 1
        return ("c", en, n)

    def op(self, en, fn, reads=(), writes=()):
        self._wait(en, self._deps(en, reads, writes))
        ins = fn(self.engs[en])
        ev = self._cinc(en, ins)
        self._record(ev, reads, writes)
        self.ninstr += 1
        return ev

    def mm(self, out_buf, out_ap, pairs, reads, transpose=False):
        return self.mm_multi(out_buf, [(out_ap, pairs)], reads, transpose)

    def mm_multi(self, out_buf, items, reads, transpose=False):
        self._wait("pe", self._deps("pe", reads, [out_buf]))
        ins = None
        for out_ap, pairs in items:
            n = len(pairs)
            for i, (l, r) in enumerate(pairs):
                if transpose:
                    ins = self.nc.tensor.transpose(out_ap, l, r)
                else:
                    ins = self.nc.tensor.matmul(out_ap, l, r, start=(i == 0), stop=(i == n - 1))
                self.ninstr += 1
        ev = self._cinc("pe", ins)
        self._record(ev, reads, [out_buf])
        return ev

    def dma(self, en, out_ap, in_ap, reads=(), writes=()):
        ring = self.dsem[en]
        i = self.dnext[en]
        self.dnext[en] = (i + 1) % len(ring)
        slot = ring[i]
        if slot[1] + 16 > SEM_LIMIT:
            old = (slot[0], slot[1])
            slot[0] = self._newsem()
            slot[1] = 0
            self._wait(en, [old])
        evs = self._deps(en, reads, writes)
        if slot[1] > 0:
            evs.append((slot[0], slot[1]))
        self._wait(en, evs)
        ins = self.engs[en].dma_start(out=out_ap, in_=in_ap)
        slot[1] += 16
        ins.then_inc(slot[0], 16)
        ev = (slot[0], slot[1])
        self._record(ev, reads, writes)
        self.ninstr += 1
        return ev

    def barrier(self):
        evs = []
        for en in self.engs:
            if self.shadow[en] > 0:
                evs.append(("c", en, self.shadow[en]))
        for k, ring in self.dsem.items():
            for slot in ring:
                if slot[1] > 0:
                    evs.append((slot[0], slot[1]))
        for en in self.engs:
            self._wait(en, evs)

    def finish(self):
        self.barrier()


OUT_SPECS = [
    ("y_prompt", [TP, D]),
    ("y_sample", [NS, D]),
    ("prompt_conv", [3, 6144]),
    ("prompt_ssm", [64 * 64, 128]),
    ("prompt_shift", [1, 12544]),
    ("prompt_wkv", [64 * 64, 64]),
    ("prompt_ret", [16 * 256, 512]),
    ("sample_conv", [NS, 3, 6144]),
    ("sample_ssm", [NS, 64 * 64, 128]),
    ("sample_shift", [NS, 12544]),
    ("sample_wkv", [NS, 64 * 64, 64]),
    ("sample_ret", [NS, 16 * 256, 512]),
]

IN_SPECS = [
    ("xp", [TP, D]), ("xs", [NS, D]),
    ("st_conv", [NS, 3, 6144]), ("st_ssm", [NS, 4096, 128]), ("st_shift", [NS, 12544]),
    ("st_wkv", [NS, 4096, 64]), ("st_ret", [NS, 4096, 512]),
    ("ab_w_in", [D, AB_IN]), ("ab_w_out", [8192, D]),
    ("ret_w_in", [D, RET_IN]), ("ret_w_out", [8192, D]),
    ("ident", [128, 128]),
]


def phase_xT(k, x_p, x_s, xT, ident):
    nc = k.nc
    xin = [k.sb(f"xin{i}", [128, D], F32) for i in range(2)]
    pst = [k.ps(f"pst{i}", [128, 512], F32) for i in range(2)]
    ntile = TP // 128 + 1
    q = 0
    for t in range(ntile):
        m = 128 if t < TP // 128 else NS
        xb = xin[t % 2]
        src = x_p[t * 128:(t + 1) * 128, :] if t < TP // 128 else x_s[:, :]
        k.dma("sp", xb.t[:m, :], src, writes=[xb])
        for g in range(KC // 4):
            pb = pst[q % 2]
            items = []
            for j in range(4):
                kc = g * 4 + j
                items.append((pb.t[:, j * 128:j * 128 + m],
                              [(xb.t[:m, kc * 128:(kc + 1) * 128], ident.t[:m, :m])]))
            k.mm_multi(pb, items, reads=[xb, ident], transpose=True)
            src_ap = pb.t[:, :].rearrange("p (j m) -> p j m", j=4)[:, :, :m]
            dst_ap = xT.t[:, g * 4:(g + 1) * 4, t * 128:t * 128 + m]
            en = "dve" if q % 2 == 0 else "act"
            if en == "dve":
                k.op("dve", lambda e: e.tensor_copy(out=dst_ap, in_=src_ap), reads=[pb], writes=[xT])
            else:
                k.op("act", lambda e: e.copy(out=dst_ap, in_=src_ap), reads=[pb], writes=[xT])
            q += 1


def phase_gemm(k, xT, W, ncols, P, wbufs, psb, stg):
    Wv = W.rearrange("(kc p) n -> p kc n", p=128)
    ngrp = (ncols + 511) // 512
    ntile = TP // 128 + 1
    q = 0

    def load(g):
        c0 = g * 512
        n = min(512, ncols - c0)
        wb = wbufs[g % 2]
        for h in range(4):
            k.dma("pool", wb.t[:, h * 8:(h + 1) * 8, :n], Wv[:, h * 8:(h + 1) * 8, c0:c0 + n], writes=[wb])

    load(0)
    for g in range(ngrp):
        if g + 1 < ngrp:
            load(g + 1)
        c0 = g * 512
        n = min(512, ncols - c0)
        wb = wbufs[g % 2]
        for t in range(ntile):
            m = 128 if t < TP // 128 else NS
            pb = psb[q % len(psb)]
            pairs = [(xT.t[:, kc, t * 128:t * 128 + m], wb.t[:, kc, :n]) for kc in range(KC)]
            k.mm(pb, pb.t[:m, :n], pairs, reads=[xT, wb])
            sg = stg[q % len(stg)]
            if q % 2 == 0:
                k.op("dve", lambda e: e.tensor_copy(out=sg.t[:m, :n], in_=pb.t[:m, :n]), reads=[pb], writes=[sg])
            else:
                k.op("act", lambda e: e.copy(out=sg.t[:m, :n], in_=pb.t[:m, :n]), reads=[pb], writes=[sg])
            k.dma("sp", P[t * 128:t * 128 + m, c0:c0 + n], sg.t[:m, :n], reads=[sg])
            q += 1


def load_bc(k, name, src_row_ap, n, dt=F32, en="sp"):
    t = k.sb(name, [128, n], dt)
    k.dma(en, t.t[:, :], src_row_ap.to_broadcast([128, n]), writes=[t])
    return t


def phase_ssd(k, P0, prm, C, YT, out_ssm, nchunks=TP // 128):
    ident, tri_le, tri_gt, ones, ident_bf = C["ident"], C["tri_le"], C["tri_gt"], C["ones"], C["ident_bf"]
    a_bc = load_bc(k, "a_bc", prm["ssd_a_log"][0:1, :], 64)
    k.op("act", lambda e: e.activation(out=a_bc.t[:, :], in_=a_bc.t[:, :], func=AF.Exp), reads=[a_bc], writes=[a_bc])
    k.op("dve", lambda e: e.tensor_scalar(out=a_bc.t[:, :], in0=a_bc.t[:, :], scalar1=-1.0, scalar2=None, op0=ALU.mult),
         reads=[a_bc], writes=[a_bc])
    d_bc = load_bc(k, "d_bc", prm["ssd_d"][0:1, :], 64)
    dtb_bc = load_bc(k, "dtb_bc", prm["ssd_dt_bias"][0:1, :], 64)
    cw = k.sb("cw", [128, 4, 768], F32)
    cb = k.sb("cb", [128, 768], F32)
    nw = k.sb("nw", [128, 512], F32)
    S = k.sb("S", [128, 512], F32)
    Sb = k.sb("Sb", [128, 512], BF16)
    yTs = k.sb("yTs", [128, 4, TP], BF16)
    U = [k.sb(f"U{i}", [128, 4, 768], F32) for i in range(2)]
    zt = [k.sb(f"zt{i}", [128, 512], F32) for i in range(2)]
    dtr = [k.sb(f"dtr{i}", [128, 8], F32) for i in range(2)]
    def db(name, shape, dt):
        return [k.sb(f"{name}{i}", shape, dt) for i in range(2)]
    DB = dict(
        a1=db("a1", [128, 768], F32), t1=db("t1c", [128, 768], F32), a2=db("a2", [128, 768], F32), t2=db("t2c", [128, 768], F32),
        xa=db("xa", [128, 768], F32), bc16=db("bc16", [128, 256], BF16), bcT=db("bcT", [128, 256], BF16),
        dt=db("dt", [128, 8], F32), adt=db("adt", [128, 8], F32), dtt=db("dtt", [128, 8], F32),
        ein=db("ein", [128, 24], F32), st=db("st", [128, 24], F32), rhsA=db("rhsA", [128, 8, 128], F32),
        eseg=db("eseg", [128, 8, 128], F32), cbm=db("cbm", [128, 128], F32), M=db("M", [128, 8, 128], BF16),
        xdt=db("xdt", [128, 8, 64], BF16), xdtt=db("xdtt", [128, 8, 64], BF16), yc=db("yc", [128, 512], F32),
        y3=db("y3", [128, 512], F32), sz=db("sz", [128, 512], F32), junk=db("junk", [128, 512], F32),
        ss=db("ss", [128, 1], F32), rstd=db("rstd", [128, 1], F32), ya=db("ya", [128, 512], BF16))
    sto = k.sb("sto", [128, 512], F32)
    psSeg = [k.ps(f"psSeg{i}", [128, 512], F32) for i in range(2)]
    psY = k.ps("psY", [128, 512], F32)
    psI = k.ps("psI", [128, 512], F32)
    psS = k.ps("psS", [128, 512], F32)
    psM = k.ps("psM", [128, 512], F32)
    psT = k.ps("psT", [128, 1024], BF16)
    XB = 4096
    u = 0
    for g in range(8):
        cols = [(XB + g * 512, 512, 0), (XB + 4096 + g * 128, 128, 512), (XB + 5120 + g * 128, 128, 640)]
        for (c0, n, o) in cols:
            cwsrc = prm["ssd_conv_w"][:, c0 - XB:c0 - XB + n]
            for kk in range(4):
                k.dma("act", cw.t[:, kk, o:o + n], cwsrc[kk:kk + 1, :].to_broadcast([128, n]), writes=[cw])
            k.dma("act", cb.t[:, o:o + n], prm["ssd_conv_b"][0:1, c0 - XB:c0 - XB + n].to_broadcast([128, n]), writes=[cb])
        k.dma("act", nw.t[:, :], prm["ssd_norm_w"][0:1, g * 512:(g + 1) * 512].to_broadcast([128, 512]), writes=[nw])
        k.op("pool", lambda e: e.memset(S.t[:, :], 0.0), writes=[S])
        k.op("pool", lambda e: e.memset(Sb.t[:, :], 0.0), writes=[Sb])
        for c in range(nchunks):
            t0 = c * 128
            Ub = U[u % 2]; zb = zt[u % 2]; db = dtr[u % 2]
            (a1, t1, a2, t2, xa, bc16, bcT, dt, adt, dtt, ein, st, rhsA, eseg, cbm, M, xdt, xdtt, yc, y3, sz, junk, ss, rstd, ya) = [
                DB[n][u % 2] for n in ("a1", "t1", "a2", "t2", "xa", "bc16", "bcT", "dt", "adt", "dtt", "ein", "st", "rhsA", "eseg", "cbm", "M",
                                       "xdt", "xdtt", "yc", "y3", "sz", "junk", "ss", "rstd", "ya")]
            u += 1
            if c == 0:
                k.op("pool", lambda e: e.memset(Ub.t[:, :, :], 0.0), writes=[Ub])
            for kk in range(4):
                r0 = t0 - 3 + kk
                p0 = max(0, -r0)
                for (c0, n, o) in cols:
                    k.dma("sp", Ub.t[p0:128, kk, o:o + n], P0[r0 + p0:r0 + 128, c0:c0 + n], writes=[Ub])
            k.dma("act", zb.t[:, :], P0[t0:t0 + 128, g * 512:(g + 1) * 512], writes=[zb])
            k.dma("act", db.t[:, :], P0[t0:t0 + 128, 10240 + g * 8:10240 + g * 8 + 8], writes=[db])
            k.op("dve", lambda e: e.tensor_tensor(out=a1.t[:, :], in0=Ub.t[:, 0, :], in1=cw.t[:, 0, :], op=ALU.mult), reads=[Ub, cw], writes=[a1])
            k.op("dve", lambda e: e.tensor_tensor(out=t1.t[:, :], in0=Ub.t[:, 1, :], in1=cw.t[:, 1, :], op=ALU.mult), reads=[Ub, cw], writes=[t1])
            k.op("pool", lambda e: e.tensor_tensor(out=a2.t[:, :], in0=Ub.t[:, 2, :], in1=cw.t[:, 2, :], op=ALU.mult), reads=[Ub, cw], writes=[a2])
            k.op("pool", lambda e: e.tensor_tensor(out=t2.t[:, :], in0=Ub.t[:, 3, :], in1=cw.t[:, 3, :], op=ALU.mult), reads=[Ub, cw], writes=[t2])
            k.op("pool", lambda e: e.tensor_tensor(out=a2.t[:, :], in0=a2.t[:, :], in1=t2.t[:, :], op=ALU.add), reads=[a2, t2], writes=[a2])
            k.op("pool", lambda e: e.tensor_tensor(out=a2.t[:, :], in0=a2.t[:, :], in1=cb.t[:, :], op=ALU.add), reads=[a2, cb], writes=[a2])
            k.op("dve", lambda e: e.tensor_tensor(out=a1.t[:, :], in0=a1.t[:, :], in1=t1.t[:, :], op=ALU.add), reads=[a1, t1], writes=[a1])
            k.op("dve", lambda e: e.tensor_tensor(out=a1.t[:, :], in0=a1.t[:, :], in1=a2.t[:, :], op=ALU.add), reads=[a1, a2], writes=[a1])
            k.op("act", lambda e: e.activation(out=xa.t[:, :], in_=a1.t[:, :], func=AF.Silu), reads=[a1], writes=[xa])
            k.op("dve", lambda e: e.tensor_tensor(out=dt.t[:, :], in0=db.t[:, :], in1=dtb_bc.t[:, g * 8:g * 8 + 8], op=ALU.add), reads=[db, dtb_bc], writes=[dt])
            k.op("act", lambda e: e.activation(out=dt.t[:, :], in_=dt.t[:, :], func=AF.Exp), reads=[dt], writes=[dt])
            k.op("act", lambda e: e.activation(out=dt.t[:, :], in_=dt.t[:, :], func=AF.Ln, bias=1.0), reads=[dt], writes=[dt])
            k.op("dve", lambda e: e.tensor_tensor(out=adt.t[:, :], in0=dt.t[:, :], in1=a_bc.t[:, g * 8:g * 8 + 8], op=ALU.mult), reads=[dt, a_bc], writes=[adt])
            k.op("dve", lambda e: e.tensor_tensor(out=rhsA.t[:, :, :], in0=tri_le.t[:, :].unsqueeze(1).to_broadcast([128, 8, 128]),
                                                  in1=adt.t[:, :].unsqueeze(2).to_broadcast([128, 8, 128]), op=ALU.mult),
                 reads=[tri_le, adt], writes=[rhsA])
            for hh in range(2):
                k.mm(psSeg[hh], psSeg[hh].t[:, :], [(tri_gt.t[:, :], rhsA.t[:, hh * 4:(hh + 1) * 4, :].rearrange("p h l -> p (h l)"))], reads=[tri_gt, rhsA])
            k.mm_multi(psM, [(psM.t[:, 128:136], [(tri_le.t[:, :], adt.t[:, :])]),
                             (psM.t[:, 136:144], [(ones.t[:, :], adt.t[:, :])])], reads=[tri_le, ones, adt])
            k.op("dve", lambda e: e.tensor_copy(out=ein.t[:, 0:16], in_=psM.t[:, 128:144]), reads=[psM], writes=[ein])
            k.op("dve", lambda e: e.tensor_tensor(out=ein.t[:, 16:24], in0=ein.t[:, 8:16], in1=ein.t[:, 0:8], op=ALU.subtract), reads=[ein], writes=[ein])
            k.op("act", lambda e: e.activation(out=st.t[:, :], in_=ein.t[:, :], func=AF.Exp), reads=[ein], writes=[st])
            k.op("dve", lambda e: e.tensor_tensor(out=dtt.t[:, :], in0=dt.t[:, :], in1=st.t[:, 16:24], op=ALU.mult), reads=[dt, st], writes=[dtt])
            k.op("act", lambda e: e.copy(out=bc16.t[:, :], in_=xa.t[:, 512:768]), reads=[xa], writes=[bc16])
            k.mm_multi(psT, [(psT.t[:, 0:128], [(bc16.t[:, 0:128], ident_bf.t[:, :])]),
                             (psT.t[:, 128:256], [(bc16.t[:, 128:256], ident_bf.t[:, :])])], reads=[bc16, ident_bf], transpose=True)
            k.op("act", lambda e: e.copy(out=bcT.t[:, :], in_=psT.t[:, 0:256]), reads=[psT], writes=[bcT])
            k.mm(psM, psM.t[:, 0:128], [(bcT.t[:, 0:128], bcT.t[:, 128:256])], reads=[bcT])
            k.op("dve", lambda e: e.tensor_tensor(out=cbm.t[:, :], in0=psM.t[:, 0:128], in1=tri_le.t[:, :], op=ALU.mult), reads=[psM, tri_le], writes=[cbm])
            for hh in range(2):
                k.op("act", lambda e: e.activation(out=eseg.t[:, hh * 4:(hh + 1) * 4, :].rearrange("p h l -> p (h l)"), in_=psSeg[hh].t[:, :], func=AF.Exp),
                     reads=[psSeg[hh]], writes=[eseg])
            k.op("dve", lambda e: e.tensor_tensor(out=M.t[:, :, :], in0=eseg.t[:, :, :], in1=cbm.t[:, :].unsqueeze(1).to_broadcast([128, 8, 128]), op=ALU.mult),
                 reads=[eseg, cbm], writes=[M])
            xs3 = xa.t[:, 0:512].rearrange("p (h d) -> p h d", h=8)
            k.op("dve", lambda e: e.tensor_tensor(out=xdt.t[:, :, :], in0=xs3, in1=dt.t[:, :].unsqueeze(2).to_broadcast([128, 8, 64]), op=ALU.mult),
                 reads=[xa, dt], writes=[xdt])
            k.op("pool", lambda e: e.tensor_tensor(out=xdtt.t[:, :, :], in0=xs3, in1=dtt.t[:, :].unsqueeze(2).to_broadcast([128, 8, 64]), op=ALU.mult),
                 reads=[xa, dtt], writes=[xdtt])
            k.mm_multi(psY, [(psY.t[:, h * 64:(h + 1) * 64], [(M.t[:, h, :], xdt.t[:, h, :])]) for h in range(8)], reads=[M, xdt])
            k.mm(psI, psI.t[:, :], [(bcT.t[:, 128:256], Sb.t[:, :])], reads=[bcT, Sb])
            k.op("dve", lambda e: e.tensor_tensor(out=yc.t[:, :].rearrange("p (h d) -> p h d", h=8), in0=psI.t[:, :].rearrange("p (h d) -> p h d", h=8),
                                                  in1=st.t[:, 0:8].unsqueeze(2).to_broadcast([128, 8, 64]), op=ALU.mult), reads=[psI, st], writes=[yc])
            k.op("dve", lambda e: e.tensor_tensor(out=yc.t[:, :], in0=yc.t[:, :], in1=psY.t[:, :], op=ALU.add), reads=[yc, psY], writes=[yc])
            k.op("pool", lambda e: e.tensor_tensor(out=y3.t[:, :].rearrange("p (h d) -> p h d", h=8), in0=xs3,
                                                   in1=d_bc.t[:, g * 8:g * 8 + 8].unsqueeze(2).to_broadcast([128, 8, 64]), op=ALU.mult), reads=[xa, d_bc], writes=[y3])
            k.op("dve", lambda e: e.tensor_tensor(out=yc.t[:, :], in0=yc.t[:, :], in1=y3.t[:, :], op=ALU.add), reads=[yc, y3], writes=[yc])
            k.op("act", lambda e: e.activation(out=sz.t[:, :], in_=zb.t[:, :], func=AF.Silu), reads=[zb], writes=[sz])
            k.op("dve", lambda e: e.tensor_tensor(out=yc.t[:, :], in0=yc.t[:, :], in1=sz.t[:, :], op=ALU.mult), reads=[yc, sz], writes=[yc])
            k.op("act", lambda e: e.activation(out=junk.t[:, :], in_=yc.t[:, :], func=AF.Square, accum_out=ss.t[:, :]), reads=[yc], writes=[junk, ss])
            k.op("dve", lambda e: e.tensor_scalar(out=rstd.t[:, :], in0=ss.t[:, :], scalar1=1.0 / 512, scalar2=1e-5, op0=ALU.mult, op1=ALU.add), reads=[ss], writes=[rstd])
            k.op("act", lambda e: e.activation(out=rstd.t[:, :], in_=rstd.t[:, :], func=AF.Sqrt), reads=[rstd], writes=[rstd])
            k.op("dve", lambda e: e.reciprocal(out=rstd.t[:, :], in_=rstd.t[:, :]), reads=[rstd], writes=[rstd])
            k.op("dve", lambda e: e.scalar_tensor_tensor(out=ya.t[:, :], in0=yc.t[:, :], scalar=rstd.t[:, 0:1], in1=nw.t[:, :], op0=ALU.mult, op1=ALU.mult),
                 reads=[yc, rstd, nw], writes=[ya])
            k.mm_multi(psT, [(psT.t[:, 256 + j * 128:256 + (j + 1) * 128], [(ya.t[:, j * 128:(j + 1) * 128], ident_bf.t[:, :])]) for j in range(4)],
                       reads=[ya, ident_bf], transpose=True)
            k.op("act", lambda e: e.copy(out=yTs.t[:, :, t0:t0 + 128], in_=psT.t[:, 256:768].rearrange("p (j t) -> p j t", j=4)), reads=[psT], writes=[yTs])
            k.mm(psS, psS.t[:, :], [(bc16.t[:, 0:128], xdtt.t[:, :, :].rearrange("p h d -> p (h d)"))], reads=[bc16, xdtt])
            k.op("dve", lambda e: e.tensor_tensor(out=S.t[:, :].rearrange("p (h d) -> p h d", h=8), in0=S.t[:, :].rearrange("p (h d) -> p h d", h=8),
                                                  in1=st.t[:, 8:16].unsqueeze(2).to_broadcast([128, 8, 64]), op=ALU.mult), reads=[S, st], writes=[S])
            k.op("dve", lambda e: e.tensor_tensor(out=S.t[:, :], in0=S.t[:, :], in1=psS.t[:, :], op=ALU.add), reads=[S, psS], writes=[S])
            k.op("act", lambda e: e.copy(out=Sb.t[:, :], in_=S.t[:, :]), reads=[S], writes=[Sb])
        for j in range(4):
            k.dma("sp", YT[0:nchunks, :, g * 4 + j, :].rearrange("t p k -> p t k"), yTs.t[:, j, 0:nchunks * 128].rearrange("p (t k) -> p t k", k=128), reads=[yTs])
        k.mm_multi(psI, [(psI.t[:, j * 128:(j + 1) * 128], [(S.t[:, j * 128:(j + 1) * 128], ident.t[:, :])]) for j in range(4)], reads=[S, ident], transpose=True)
        k.op("dve", lambda e: e.tensor_copy(out=sto.t[:, :], in_=psI.t[:, :]), reads=[psI], writes=[sto])
        k.dma("sp", out_ssm[g * 512:(g + 1) * 512, :].rearrange("(j p) n -> p j n", p=128), sto.t[:, :].rearrange("p (j n) -> p j n", j=4), reads=[sto])


PB = 10304
GB = 22848
C1 = 0.6065306597126334


def phase_rwkv(k, P0, prm, C, YT, out_wkv, nchunks=TP // 128, ngroups=8):
    ident, tri_le, tri_gt, ones, ident_bf = C["ident"], C["tri_le"], C["tri_gt"], C["ones"], C["ident_bf"]
    call = C["all"]
    maskA = Buf(call.t[:, 5:9, :].rearrange("p a b -> p (a b)"), "maskA", call)
    e127 = Buf(call.t[:, 0, 127:128], "e127", call)
    loraT = k.sb("loraT", [128, 2, TP], BF16)
    mu_wa = load_bc(k, "mu_wa", prm["rwkv_mu"][0:1, 12288:12544], 256)
    pw = [k.sb(f"pw{i}", [128, 256], F32) for i in range(2)]
    pwp = [k.sb(f"pwp{i}", [128, 256], F32) for i in range(2)]
    dd = k.sb("dd", [128, 256], F32)
    lw16 = k.sb("lw16", [128, 256], BF16)
    psTr = k.ps("psTr", [128, 1024], BF16)
    for c in range(nchunks):
        t0 = c * 128
        a = pw[c % 2]; b = pwp[c % 2]
        k.dma("sp", a.t[:, :], P0[t0:t0 + 128, PB + 12288:PB + 12544], writes=[a])
        if c == 0:
            k.op("pool", lambda e: e.memset(b.t[:, :], 0.0), writes=[b])
            k.dma("sp", b.t[1:128, :], P0[0:127, PB + 12288:PB + 12544], writes=[b])
        else:
            k.dma("sp", b.t[:, :], P0[t0 - 1:t0 + 127, PB + 12288:PB + 12544], writes=[b])
        k.op("dve", lambda e: e.tensor_tensor(out=dd.t[:, :], in0=b.t[:, :], in1=a.t[:, :], op=ALU.subtract), reads=[a, b], writes=[dd])
        k.op("dve", lambda e: e.tensor_tensor(out=dd.t[:, :], in0=dd.t[:, :], in1=mu_wa.t[:, :], op=ALU.mult), reads=[dd, mu_wa], writes=[dd])
        k.op("dve", lambda e: e.tensor_tensor(out=dd.t[:, :], in0=dd.t[:, :], in1=a.t[:, :], op=ALU.add), reads=[dd, a], writes=[dd])
        k.op("act", lambda e: e.activation(out=lw16.t[:, 0:128], in_=dd.t[:, 0:128], func=AF.Tanh), reads=[dd], writes=[lw16])
        k.op("act", lambda e: e.copy(out=lw16.t[:, 128:256], in_=dd.t[:, 128:256]), reads=[dd], writes=[lw16])
        k.mm_multi(psTr, [(psTr.t[:, j * 128:(j + 1) * 128], [(lw16.t[:, j * 128:(j + 1) * 128], ident_bf.t[:, :])]) for j in range(2)],
                   reads=[lw16, ident_bf], transpose=True)
        k.op("dve", lambda e: e.tensor_copy(out=loraT.t[:, :, t0:t0 + 128], in_=psTr.t[:, 0:256].rearrange("p (j t) -> p j t", j=2)), reads=[psTr], writes=[loraT])
    def sbf(name, dt=F32, n=512):
        return k.sb(name, [128, n], dt)
    mu_r = sbf("mu_r"); mu_k = sbf("mu_k"); mu_v = sbf("mu_v"); w0 = sbf("w0"); a0 = sbf("a0")
    k_k = sbf("k_k"); k_a = sbf("k_a"); r_k = sbf("r_k"); lnw = sbf("lnw"); lnb = sbf("lnb")
    wup = sbf("wup", BF16); aup = sbf("aup", BF16)
    ST = k.sb("ST", [64, 8, 64], F32); STb = k.sb("STb", [64, 8, 64], BF16)
    yTs = k.sb("yTsr", [128, 4, TP], BF16)
    ld = [[sbf(f"ld{n}{i}") for n in ("r", "k", "v", "rp", "kp", "vp", "g")] for i in range(2)]
    r = sbf("r_"); kt = sbf("k_"); v = sbf("v_"); tA = sbf("tA"); tB = sbf("tB")
    sig = sbf("sig"); aa = sbf("aa"); cumI = sbf("cumI"); G = sbf("G"); Ginv = sbf("Ginv"); GX = sbf("GX"); GLr = sbf("GLr")
    kk = sbf("kk"); kp = sbf("kprime"); kka = sbf("kka"); sg = sbf("sg")
    At = k.sb("At", [128, 4, 512], BF16)
    Bh = sbf("Bh", BF16); Kh = sbf("Kh", BF16); Vb = sbf("Vb", BF16)
    ssq = k.sb("ssq", [128, 8], F32); bsum = k.sb("bsum", [128, 8], F32)
    hs = k.sb("hs", [128, 8], F32); hq = k.sb("hq", [128, 8], F32); hm = k.sb("hm", [128, 8], F32); hr = k.sb("hr", [128, 8], F32)
    glT = k.sb("glT", [64, 8], F32)
    FT = k.sb("FT", [64, 8, 4, 128], BF16)
    Am = [k.sb(f"Am{i}", [128, 512], F32) for i in range(8)]
    Abf = [k.sb(f"Abf{i}", [128, 3, 128], BF16) for i in range(8)]
    Mb = [[k.sb(f"Mb{i}_{j}", [128, 256], F32) for j in range(2)] for i in range(8)]
    Qb = [[k.sb(f"Qb{i}_{j}", [128, 128], F32) for j in range(2)] for i in range(8)]
    Xs = k.sb("Xs", [128, 512], F32)
    Ub = k.sb("Ub", [128, 512], BF16)
    yo = sbf("yo"); yb = sbf("yb", BF16)
    sto = k.sb("stow", [64, 8, 64], F32)
    psL = [k.ps(f"psL{i}", [128, 512], F32) for i in range(2)]
    psA = k.ps("psA", [128, 512], F32)
    psInv = [k.ps(f"psInv{i}", [128, 512], F32) for i in range(2)]
    psY = k.ps("psYr", [128, 512], F32)
    psMt = k.ps("psMisc", [128, 512], F32)
    psN = Buf(psMt.t[:, 0:128], "psN", psMt); psX = Buf(psMt.t[:, 128:192], "psX", psMt); psU = Buf(psMt.t[:, 192:256], "psU", psMt)
    psSt = Buf(psMt.t[:64, 256:320], "psSt", psMt); psG = Buf(psMt.t[:64, 320:328], "psG", psMt)
    psW = Buf(psMt.t[:64, 0:512].rearrange("p (h v) -> p h v", h=8), "psW", psMt)
    sq_banks = [psInv[0], psInv[1], psL[0], psL[1]]
    sqv = [Buf(sq_banks[h // 2].t[:, (h % 2) * 256:(h % 2 + 1) * 256], f"sqv{h}", sq_banks[h // 2]) for h in range(8)]
    mq_banks = [psA, psY]
    mqv = [Buf(mq_banks[h // 4].t[:, (h % 4) * 128:(h % 4 + 1) * 128], f"mqv{h}", mq_banks[h // 4]) for h in range(8)]
    psNv = [Buf(psInv[i].t[:, 0:128], f"psNv{i}", psInv[i]) for i in range(2)]
    u = 0
    for hg in range(ngroups):
        cs = hg * 512
        for (tile, nm, off) in ((mu_r, "rwkv_mu", 0), (mu_k, "rwkv_mu", 4096), (mu_v, "rwkv_mu", 8192), (w0, "rwkv_w0", 0),
                                (a0, "rwkv_a0", 0), (k_k, "rwkv_k_k", 0), (k_a, "rwkv_k_a", 0), (r_k, "rwkv_r_k", 0),
                                (lnw, "rwkv_lnx_w", 0), (lnb, "rwkv_lnx_b", 0)):
            k.dma("act", tile.t[:, :], prm[nm][0:1, off + cs:off + cs + 512].to_broadcast([128, 512]), writes=[tile])
        k.dma("pool", wup.t[:, :], prm["rwkv_w_up"][:, cs:cs + 512], writes=[wup])
        k.dma("pool", aup.t[:, :], prm["rwkv_a_up"][:, cs:cs + 512], writes=[aup])
        k.op("pool", lambda e: e.memset(ST.t[:, :, :], 0.0), writes=[ST])
        k.op("pool", lambda e: e.memset(STb.t[:, :, :], 0.0), writes=[STb])
        for c in range(nchunks):
            t0 = c * 128
            L = ld[u % 2]
            u += 1
            for i, off in enumerate((0, 4096, 8192)):
                k.dma("sp", L[i].t[:, :], P0[t0:t0 + 128, PB + off + cs:PB + off + cs + 512], writes=[L[i]])
                if c == 0:
                    k.op("pool", lambda e: e.memset(L[3 + i].t[:, :], 0.0), writes=[L[3 + i]])
                    k.dma("act", L[3 + i].t[1:128, :], P0[0:127, PB + off + cs:PB + off + cs + 512], writes=[L[3 + i]])
                else:
                    k.dma("act", L[3 + i].t[:, :], P0[t0 - 1:t0 + 127, PB + off + cs:PB + off + cs + 512], writes=[L[3 + i]])
            k.dma("sp", L[6].t[:, :], P0[t0:t0 + 128, GB + cs:GB + cs + 512], writes=[L[6]])
            for i, (dst, mu) in enumerate(((r, mu_r), (kt, mu_k), (v, mu_v))):
                en = "dve" if i != 1 else "pool"
                k.op(en, lambda e: e.tensor_tensor(out=dst.t[:, :], in0=L[3 + i].t[:, :], in1=L[i].t[:, :], op=ALU.subtract), reads=[L[3 + i], L[i]], writes=[dst])
                k.op(en, lambda e: e.tensor_tensor(out=dst.t[:, :], in0=dst.t[:, :], in1=mu.t[:, :], op=ALU.mult), reads=[dst, mu], writes=[dst])
                k.op(en, lambda e: e.tensor_tensor(out=dst.t[:, :], in0=dst.t[:, :], in1=L[i].t[:, :], op=ALU.add), reads=[dst, L[i]], writes=[dst])
            k.mm(psL[0], psL[0].t[:, :], [(loraT.t[:, 0, t0:t0 + 128], wup.t[:, :])], reads=[loraT, wup])
            k.mm(psL[1], psL[1].t[:, :], [(loraT.t[:, 1, t0:t0 + 128], aup.t[:, :])], reads=[loraT, aup])
            k.op("dve", lambda e: e.tensor_tensor(out=tA.t[:, :], in0=psL[0].t[:, :], in1=w0.t[:, :], op=ALU.add), reads=[psL[0], w0], writes=[tA])
            k.op("act", lambda e: e.activation(out=sig.t[:, :], in_=tA.t[:, :], func=AF.Sigmoid), reads=[tA], writes=[sig])
            k.op("dve", lambda e: e.tensor_tensor(out=tB.t[:, :], in0=psL[1].t[:, :], in1=a0.t[:, :], op=ALU.add), reads=[psL[1], a0], writes=[tB])
            k.op("act", lambda e: e.activation(out=aa.t[:, :], in_=tB.t[:, :], func=AF.Sigmoid), reads=[tB], writes=[aa])
            k.mm(psL[0], psL[0].t[:, :], [(tri_le.t[:, :], sig.t[:, :])], reads=[tri_le, sig])
            k.mm(psL[1], psL[1].t[:, :], [(ones.t[:, :], sig.t[:, :])], reads=[ones, sig])
            k.op("act", lambda e: e.copy(out=cumI.t[:, :], in_=psL[0].t[:, :]), reads=[psL[0]], writes=[cumI])
            k.op("act", lambda e: e.activation(out=G.t[:, :], in_=cumI.t[:, :], func=AF.Exp, scale=-C1), reads=[cumI], writes=[G])
            k.op("act", lambda e: e.activation(out=Ginv.t[:, :], in_=cumI.t[:, :], func=AF.Exp, scale=C1), reads=[cumI], writes=[Ginv])
            k.op("dve", lambda e: e.tensor_tensor(out=tA.t[:, :], in0=cumI.t[:, :], in1=sig.t[:, :], op=ALU.subtract), reads=[cumI, sig], writes=[tA])
            k.op("act", lambda e: e.activation(out=GX.t[:, :], in_=tA.t[:, :], func=AF.Exp, scale=-C1), reads=[tA], writes=[GX])
            k.op("dve", lambda e: e.tensor_tensor(out=tB.t[:, :], in0=psL[1].t[:, :], in1=cumI.t[:, :], op=ALU.subtract), reads=[psL[1], cumI], writes=[tB])
            k.op("act", lambda e: e.activation(out=GLr.t[:, :], in_=tB.t[:, :], func=AF.Exp, scale=-C1), reads=[tB], writes=[GLr])
            k3 = lambda t_: t_.t[:, :].rearrange("p (h d) -> p h d", h=8)
            bc8 = lambda t_: t_.t[:, :].unsqueeze(2).to_broadcast([128, 8, 64])
            k.op("pool", lambda e: e.tensor_tensor(out=kk.t[:, :], in0=kt.t[:, :], in1=k_k.t[:, :], op=ALU.mult), reads=[kt, k_k], writes=[kk])
            k.op("pool", lambda e: e.tensor_tensor(out=tA.t[:, :], in0=kk.t[:, :], in1=kk.t[:, :], op=ALU.mult), reads=[kk, tA], writes=[tA])
            k.op("dve", lambda e: e.tensor_reduce(out=ssq.t[:, :], in_=k3(tA), axis=AX.X, op=ALU.add), reads=[tA], writes=[ssq])
            k.op("dve", lambda e: e.tensor_scalar(out=ssq.t[:, :], in0=ssq.t[:, :], scalar1=1e-24, scalar2=None, op0=ALU.max), reads=[ssq], writes=[ssq])
            k.op("act", lambda e: e.activation(out=ssq.t[:, :], in_=ssq.t[:, :], func=AF.Sqrt), reads=[ssq], writes=[ssq])
            k.op("dve", lambda e: e.reciprocal(out=ssq.t[:, :], in_=ssq.t[:, :]), reads=[ssq], writes=[ssq])
            k.op("dve", lambda e: e.tensor_tensor(out=k3(kk), in0=k3(kk), in1=bc8(ssq), op=ALU.mult), reads=[kk, ssq], writes=[kk])
            k.op("dve", lambda e: e.scalar_tensor_tensor(out=tB.t[:, :], in0=aa.t[:, :], scalar=-1.0, in1=k_a.t[:, :], op0=ALU.add, op1=ALU.mult), reads=[aa, k_a, tB], writes=[tB])
            k.op("dve", lambda e: e.scalar_tensor_tensor(out=kp.t[:, :], in0=tB.t[:, :], scalar=1.0, in1=kt.t[:, :], op0=ALU.add, op1=ALU.mult), reads=[tB, kt], writes=[kp])
            k.op("dve", lambda e: e.tensor_tensor(out=kka.t[:, :], in0=kk.t[:, :], in1=aa.t[:, :], op=ALU.mult), reads=[kk, aa], writes=[kka])
            k.op("dve", lambda e: e.tensor_tensor(out=At.t[:, 0, :], in0=kk.t[:, :], in1=GX.t[:, :], op=ALU.mult), reads=[kk, GX], writes=[At])
            k.op("pool", lambda e: e.tensor_tensor(out=At.t[:, 1, :], in0=r.t[:, :], in1=G.t[:, :], op=ALU.mult), reads=[r, G, At], writes=[At])
            k.op("dve", lambda e: e.tensor_tensor(out=At.t[:, 2, :], in0=kka.t[:, :], in1=Ginv.t[:, :], op=ALU.mult), reads=[kka, Ginv], writes=[At])
            k.op("pool", lambda e: e.tensor_tensor(out=At.t[:, 3, :], in0=kp.t[:, :], in1=Ginv.t[:, :], op=ALU.mult), reads=[kp, Ginv, At], writes=[At])
            k.op("dve", lambda e: e.tensor_tensor(out=Bh.t[:, :], in0=kka.t[:, :], in1=GLr.t[:, :], op=ALU.mult), reads=[kka, GLr], writes=[Bh])
            k.op("pool", lambda e: e.tensor_tensor(out=Kh.t[:, :], in0=kp.t[:, :], in1=GLr.t[:, :], op=ALU.mult), reads=[kp, GLr], writes=[Kh])
            k.op("act", lambda e: e.copy(out=Vb.t[:, :], in_=v.t[:, :]), reads=[v], writes=[Vb])
            k.op("pool", lambda e: e.tensor_tensor(out=tA.t[:, :], in0=r.t[:, :], in1=kp.t[:, :], op=ALU.mult), reads=[r, kp, tA], writes=[tA])
            k.op("pool", lambda e: e.tensor_tensor(out=tA.t[:, :], in0=tA.t[:, :], in1=r_k.t[:, :], op=ALU.mult), reads=[tA, r_k], writes=[tA])
            k.op("dve", lambda e: e.tensor_reduce(out=bsum.t[:, :], in_=k3(tA), axis=AX.X, op=ALU.add), reads=[tA], writes=[bsum])
            k.mm_multi(psG, [(psG.t[:, h:h + 1], [(G.t[:, h * 64:(h + 1) * 64], e127.t)]) for h in range(8)], reads=[G, e127])
            k.op("dve", lambda e: e.tensor_copy(out=glT.t[:, :], in_=psG.t[:, :]), reads=[psG], writes=[glT])
            for h in range(8):
                k.mm_multi(psTr, [(psTr.t[:64, q * 128:(q + 1) * 128], [(At.t[:, q, h * 64:(h + 1) * 64], ident_bf.t[:, :])]) for q in range(4)],
                           reads=[At, ident_bf], transpose=True)
                en = "act" if h % 2 == 0 else "dve"
                if en == "act":
                    k.op("act", lambda e: e.copy(out=FT.t[:, h, :, :], in_=psTr.t[:64, 0:512].rearrange("p (q t) -> p q t", q=4)), reads=[psTr], writes=[FT])
                else:
                    k.op("dve", lambda e: e.tensor_copy(out=FT.t[:, h, :, :], in_=psTr.t[:64, 0:512].rearrange("p (q t) -> p q t", q=4)), reads=[psTr], writes=[FT])
            for h in range(8):
                pa = psA if h % 2 == 0 else psY
                pn = psNv[h % 2]
                rhsAR = FT.t[:, h, 0:2, :].rearrange("p q t -> p (q t)")
                k.mm_multi(pa, [(pa.t[:, 0:256], [(FT.t[:, h, 2, :], rhsAR)]), (pa.t[:, 256:512], [(FT.t[:, h, 3, :], rhsAR)])], reads=[FT])
                k.mm(pn, pn.t, [(FT.t[:, h, 0, :], FT.t[:, h, 2, :])], reads=[FT])
                k.op("dve", lambda e: e.tensor_tensor(out=Am[h].t[:, :], in0=pa.t[:, :], in1=maskA.t, op=ALU.mult), reads=[pa, maskA], writes=[Am[h]])
                k.op("act", lambda e: e.copy(out=Abf[h].t[:, :, :].rearrange("p a b -> p (a b)"), in_=Am[h].t[:, 128:512]), reads=[Am[h]], writes=[Abf[h]])
                k.op("pool", lambda e: e.tensor_copy(out=Mb[h][0].t[:, 0:128], in_=Am[h].t[:, 0:128]), reads=[Am[h], Mb[h][0]], writes=[Mb[h][0]])
                k.op("dve", lambda e: e.tensor_tensor(out=Mb[h][0].t[:, 128:256], in0=pn.t, in1=tri_gt.t[:, :], op=ALU.mult), reads=[pn, tri_gt, Mb[h][0]], writes=[Mb[h][0]])
                k.op("pool", lambda e: e.tensor_tensor(out=Qb[h][0].t[:, :], in0=ident.t[:, :], in1=Am[h].t[:, 0:128], op=ALU.subtract), reads=[ident, Am[h]], writes=[Qb[h][0]])
            NR = 7
            for rd in range(NR):
                for h in range(8):
                    Mc = Mb[h][rd % 2]; Qc = Qb[h][rd % 2]
                    items = []
                    if rd < NR - 2:
                        items.append((sqv[h].t[:, 0:128], [(Mc.t[:, 128:256], Mc.t[:, 0:128])]))
                    if rd < NR - 1:
                        items.append((sqv[h].t[:, 128:256], [(Mc.t[:, 0:128], Mc.t[:, 128:256])]))
                    if items:
                        k.mm_multi(sqv[h], items, reads=[Mc])
                    if rd > 0:
                        k.mm(mqv[h], mqv[h].t, [(Mc.t[:, 128:256], Qc.t[:, :])], reads=[Mc, Qc])
                for h in range(8):
                    Qc = Qb[h][rd % 2]
                    Mn = Mb[h][(rd + 1) % 2]; Qn = Qb[h][(rd + 1) % 2]
                    if rd < NR - 1:
                        lo = 0 if rd < NR - 2 else 128
                        k.op("act", lambda e: e.copy(out=Mn.t[:, lo:256], in_=sqv[h].t[:, lo:256]), reads=[sqv[h]], writes=[Mn])
                    if rd > 0:
                        k.op("dve", lambda e: e.tensor_tensor(out=Qn.t[:, :], in0=Qc.t[:, :], in1=mqv[h].t, op=ALU.add), reads=[Qc, mqv[h]], writes=[Qn])
                    else:
                        k.op("pool", lambda e: e.tensor_copy(out=Qn.t[:, :], in_=Qc.t[:, :]), reads=[Qc], writes=[Qn])
            hcs = [slice(h * 64, (h + 1) * 64) for h in range(8)]
            k.mm_multi(psMt, [(psMt.t[:, hcs[h]], [(FT.t[:, h, 0, :], STb.t[:, h, :]), (Abf[h].t[:, 1, :], Vb.t[:, hcs[h]])]) for h in range(8)],
                       reads=[FT, STb, Vb] + Abf)
            k.op("dve", lambda e: e.tensor_copy(out=Xs.t[:, :], in_=psMt.t[:, :]), reads=[psMt], writes=[Xs])
            Ws = [Qb[h][NR % 2] for h in range(8)]
            k.mm_multi(psInv[0], [(psInv[0].t[:, hcs[h]], [(Ws[h].t[:, :], Xs.t[:, hcs[h]])]) for h in range(8)], reads=[Xs] + Ws)
            k.op("dve", lambda e: e.tensor_scalar(out=Ub.t[:, :], in0=psInv[0].t[:, :], scalar1=-1.0, scalar2=None, op0=ALU.mult), reads=[psInv[0]], writes=[Ub])
            k.mm_multi(psY, [(psY.t[:, hcs[h]], [(FT.t[:, h, 1, :], STb.t[:, h, :]), (Abf[h].t[:, 0, :], Ub.t[:, hcs[h]]), (Abf[h].t[:, 2, :], Vb.t[:, hcs[h]])]) for h in range(8)],
                       reads=[FT, STb, Ub, Vb] + Abf)
            k.mm_multi(psInv[1], [(psInv[1].t[:64, hcs[h]], [(Bh.t[:, hcs[h]], Ub.t[:, hcs[h]]), (Kh.t[:, hcs[h]], Vb.t[:, hcs[h]])]) for h in range(8)],
                       reads=[Bh, Kh, Ub, Vb])
            k.op("dve", lambda e: e.tensor_tensor(out=ST.t[:, :, :], in0=ST.t[:, :, :], in1=glT.t[:, :].unsqueeze(2).to_broadcast([64, 8, 64]), op=ALU.mult), reads=[ST, glT], writes=[ST])
            k.op("dve", lambda e: e.tensor_tensor(out=ST.t[:, :, :], in0=ST.t[:, :, :], in1=psInv[1].t[:64, :].rearrange("p (h v) -> p h v", h=8), op=ALU.add), reads=[ST, psInv[1]], writes=[ST])
            k.op("act", lambda e: e.copy(out=STb.t[:, :, :], in_=ST.t[:, :, :]), reads=[ST], writes=[STb])
            k.op("act", lambda e: e.copy(out=yo.t[:, :], in_=psY.t[:, :]), reads=[psY], writes=[yo])
            k.op("dve", lambda e: e.tensor_reduce(out=hs.t[:, :], in_=k3(yo), axis=AX.X, op=ALU.add), reads=[yo], writes=[hs])
            k.op("pool", lambda e: e.tensor_tensor(out=tA.t[:, :], in0=yo.t[:, :], in1=yo.t[:, :], op=ALU.mult), reads=[yo, tA], writes=[tA])
            k.op("dve", lambda e: e.tensor_reduce(out=hq.t[:, :], in_=k3(tA), axis=AX.X, op=ALU.add), reads=[tA], writes=[hq])
            k.op("dve", lambda e: e.tensor_scalar(out=hm.t[:, :], in0=hs.t[:, :], scalar1=1.0 / 64, scalar2=None, op0=ALU.mult), reads=[hs], writes=[hm])
            k.op("dve", lambda e: e.tensor_tensor(out=hs.t[:, :], in0=hm.t[:, :], in1=hm.t[:, :], op=ALU.mult), reads=[hm], writes=[hs])
            k.op("dve", lambda e: e.scalar_tensor_tensor(out=hr.t[:, :], in0=hq.t[:, :], scalar=1.0 / 64, in1=hs.t[:, :], op0=ALU.mult, op1=ALU.subtract), reads=[hq, hs], writes=[hr])
            k.op("dve", lambda e: e.tensor_scalar(out=hr.t[:, :], in0=hr.t[:, :], scalar1=64e-5, scalar2=None, op0=ALU.add), reads=[hr], writes=[hr])
            k.op("act", lambda e: e.activation(out=hr.t[:, :], in_=hr.t[:, :], func=AF.Sqrt), reads=[hr], writes=[hr])
            k.op("dve", lambda e: e.reciprocal(out=hr.t[:, :], in_=hr.t[:, :]), reads=[hr], writes=[hr])
            k.op("dve", lambda e: e.tensor_tensor(out=k3(yo), in0=k3(yo), in1=bc8(hm), op=ALU.subtract), reads=[yo, hm], writes=[yo])
            k.op("dve", lambda e: e.tensor_tensor(out=k3(yo), in0=k3(yo), in1=bc8(hr), op=ALU.mult), reads=[yo, hr], writes=[yo])
            k.op("pool", lambda e: e.tensor_tensor(out=yo.t[:, :], in0=yo.t[:, :], in1=lnw.t[:, :], op=ALU.mult), reads=[yo, lnw], writes=[yo])
            k.op("pool", lambda e: e.tensor_tensor(out=yo.t[:, :], in0=yo.t[:, :], in1=lnb.t[:, :], op=ALU.add), reads=[yo, lnb], writes=[yo])
            k.op("dve", lambda e: e.tensor_tensor(out=k3(tA), in0=k3(v), in1=bc8(bsum), op=ALU.mult), reads=[v, bsum, tA], writes=[tA])
            k.op("dve", lambda e: e.tensor_tensor(out=yo.t[:, :], in0=yo.t[:, :], in1=tA.t[:, :], op=ALU.add), reads=[yo, tA], writes=[yo])
            k.op("act", lambda e: e.activation(out=sg.t[:, :], in_=L[6].t[:, :], func=AF.Silu), reads=[L[6]], writes=[sg])
            k.op("dve", lambda e: e.tensor_tensor(out=yb.t[:, :], in0=yo.t[:, :], in1=sg.t[:, :], op=ALU.mult), reads=[yo, sg], writes=[yb])
            k.mm_multi(psTr, [(psTr.t[:, 512 + j * 128:512 + (j + 1) * 128], [(yb.t[:, j * 128:(j + 1) * 128], ident_bf.t[:, :])]) for j in range(4)],
                       reads=[yb, ident_bf], transpose=True)
            k.op("act", lambda e: e.copy(out=yTs.t[:, :, t0:t0 + 128], in_=psTr.t[:, 512:1024].rearrange("p (j t) -> p j t", j=4)), reads=[psTr], writes=[yTs])
        for j in range(4):
            k.dma("sp", YT[0:nchunks, :, 32 + hg * 4 + j, :].rearrange("t p k -> p t k"), yTs.t[:, j, 0:nchunks * 128].rearrange("p (t k) -> p t k", k=128), reads=[yTs])
        k.mm_multi(psW, [(psW.t[:, h, :], [(ST.t[:, h, :], ident.t[:64, :64])]) for h in range(8)], reads=[ST, ident], transpose=True)
        k.op("dve", lambda e: e.tensor_copy(out=sto.t[:, :, :], in_=psW.t), reads=[psW], writes=[sto])
        k.dma("sp", out_wkv[cs:cs + 512, :].rearrange("(h v) n -> v h n", v=64), sto.t[:, :, :], reads=[sto])


ALPHA = (2 * 2) ** 0.25


def phase_outproj(k, YT, W, H):
    EC = 64
    Wv = W.rearrange("(ec p) n -> p ec n", p=128)
    wb = [k.sb(f"wo{i}", [128, EC, 512], BF16) for i in range(2)]
    yt = [k.sb(f"yt{i}", [128, EC, 128], BF16) for i in range(2)]
    psb = [k.ps(f"pso{i}", [128, 512], F32) for i in range(4)]
    stg = [k.sb(f"stgo{i}", [128, 512], F32) for i in range(2)]
    ntile = TP // 128 + 1

    def loadw(g):
        for h in range(8):
            k.dma("pool", wb[g % 2].t[:, h * 8:(h + 1) * 8, :], Wv[:, h * 8:(h + 1) * 8, g * 512:(g + 1) * 512], writes=[wb[g % 2]])

    q = 0
    loadw(0)
    for g in range(8):
        if g + 1 < 8:
            loadw(g + 1)
        for t in range(ntile):
            m = 128 if t < TP // 128 else NS
            yb = yt[q % 2]
            for h in range(2):
                k.dma("sp" if h == 0 else "act", yb.t[:, h * 32:(h + 1) * 32, :m], YT[t, :, h * 32:(h + 1) * 32, 0:m], writes=[yb])
            pb = psb[q % 4]
            k.mm(pb, pb.t[:m, :], [(yb.t[:, ec, :m], wb[g % 2].t[:, ec, :]) for ec in range(EC)], reads=[yb, wb[g % 2]])
            sg = stg[q % 2]
            if q % 2 == 0:
                k.op("dve", lambda e: e.tensor_copy(out=sg.t[:m, :], in_=pb.t[:m, :]), reads=[pb], writes=[sg])
            else:
                k.op("act", lambda e: e.copy(out=sg.t[:m, :], in_=pb.t[:m, :]), reads=[pb], writes=[sg])
            k.dma("sp", H[t * 128:t * 128 + m, g * 512:(g + 1) * 512], sg.t[:m, :], reads=[sg])
            q += 1


def phase_ln(k, H, xin, ln_w, ln_b, C, dst, xT=None):
    ident_bf = C["ident_bf"]
    w_bc = load_bc(k, "lnw_bc", ln_w, D)
    b_bc = load_bc(k, "lnb_bc", ln_b, D)
    ht = k.sb("ht", [128, D], F32); xt = k.sb("xt_", [128, D], F32)
    y16 = k.sb("y16", [128, D], BF16)
    s1 = k.sb("s1", [128, 1], F32); s2 = k.sb("s2", [128, 1], F32); mean = k.sb("mean", [128, 1], F32); rstd = k.sb("rstd_", [128, 1], F32)
    pst = [k.ps(f"psl{i}", [128, 1024], BF16) for i in range(2)]
    ntile = TP // 128 + 1
    q = 0
    for t in range(ntile):
        m = 128 if t < TP // 128 else NS
        k.dma("sp", ht.t[:m, :], H[t * 128:t * 128 + m, :], writes=[ht])
        k.dma("act", xt.t[:m, :], xin(t), writes=[xt])
        k.op("dve", lambda e: e.scalar_tensor_tensor(out=ht.t[:m, :], in0=xt.t[:m, :], scalar=ALPHA, in1=ht.t[:m, :], op0=ALU.mult, op1=ALU.add), reads=[xt, ht], writes=[ht])
        k.op("act", lambda e: e.activation(out=xt.t[:m, :], in_=ht.t[:m, :], func=AF.Identity, accum_out=s1.t[:m, :]), reads=[ht], writes=[xt, s1])
        k.op("act", lambda e: e.activation(out=xt.t[:m, :], in_=ht.t[:m, :], func=AF.Square, accum_out=s2.t[:m, :]), reads=[ht], writes=[xt, s2])
        k.op("dve", lambda e: e.tensor_scalar(out=mean.t[:m, :], in0=s1.t[:m, :], scalar1=1.0 / D, scalar2=None, op0=ALU.mult), reads=[s1], writes=[mean])
        k.op("dve", lambda e: e.tensor_tensor(out=s1.t[:m, :], in0=mean.t[:m, :], in1=mean.t[:m, :], op=ALU.mult), reads=[mean], writes=[s1])
        k.op("dve", lambda e: e.scalar_tensor_tensor(out=rstd.t[:m, :], in0=s2.t[:m, :], scalar=1.0 / D, in1=s1.t[:m, :], op0=ALU.mult, op1=ALU.subtract), reads=[s2, s1], writes=[rstd])
        k.op("dve", lambda e: e.tensor_scalar(out=rstd.t[:m, :], in0=rstd.t[:m, :], scalar1=1e-5, scalar2=None, op0=ALU.add), reads=[rstd], writes=[rstd])
        k.op("act", lambda e: e.activation(out=rstd.t[:m, :], in_=rstd.t[:m, :], func=AF.Sqrt), reads=[rstd], writes=[rstd])
        k.op("dve", lambda e: e.reciprocal(out=rstd.t[:m, :], in_=rstd.t[:m, :]), reads=[rstd], writes=[rstd])
        k.op("dve", lambda e: e.tensor_scalar(out=ht.t[:m, :], in0=ht.t[:m, :], scalar1=mean.t[:m, 0:1], scalar2=rstd.t[:m, 0:1], op0=ALU.subtract, op1=ALU.mult),
             reads=[ht, mean, rstd], writes=[ht])
        k.op("pool", lambda e: e.tensor_tensor(out=ht.t[:m, :], in0=ht.t[:m, :], in1=w_bc.t[:m, :], op=ALU.mult), reads=[ht, w_bc], writes=[ht])
        k.op("dve", lambda e: e.tensor_tensor(out=ht.t[:m, :], in0=ht.t[:m, :], in1=b_bc.t[:m, :], op=ALU.add), reads=[ht, b_bc], writes=[ht])
        k.dma("sp", dst(t), ht.t[:m, :], reads=[ht])
        if xT is not None:
            k.op("act", lambda e: e.copy(out=y16.t[:m, :], in_=ht.t[:m, :]), reads=[ht], writes=[y16])
            for g in range(KC // 8):
                pb = pst[q % 2]
                k.mm_multi(pb, [(pb.t[:, j * 128:j * 128 + m], [(y16.t[:m, (g * 8 + j) * 128:(g * 8 + j + 1) * 128], ident_bf.t[:m, :m])]) for j in range(8)],
                           reads=[y16, ident_bf], transpose=True)
                src_ap = pb.t[:, :].rearrange("p (j m) -> p j m", j=8)[:, :, :m]
                dst_ap = xT.t[:, g * 8:(g + 1) * 8, t * 128:t * 128 + m]
                if q % 2 == 0:
                    k.op("dve", lambda e: e.tensor_copy(out=dst_ap, in_=src_ap), reads=[pb], writes=[xT])
                else:
                    k.op("act", lambda e: e.copy(out=dst_ap, in_=src_ap), reads=[pb], writes=[xT])
                q += 1


RET_LOG_G = [float(np.log1p(-np.exp2(-5.0 - h))) for h in range(16)]


def phase_ret(k, P1, prm, C, RC, YT, out_ret, nchunks=TP // 128, nheads=16):
    ident_bf = C["ident_bf"]
    rope = k.sb("rope_sb", [128, 16, 256], F32)
    k.dma("sp", rope.t[:, :, :], RC["rope"][0:TP, :].rearrange("(c p) f -> p c f", p=128), writes=[rope])
    qdec = k.sb("qdec_sb", [128, 16], F32); kdec = k.sb("kdec_sb", [128, 16], F32)
    k.dma("sp", qdec.t[:, :], RC["qdec"][:, :], writes=[qdec])
    k.dma("sp", kdec.t[:, :], RC["kdec"][:, :], writes=[kdec])
    innerT = k.sb("innerT_sb", [128, 128], F32)
    gnw = k.sb("gnw", [128, 512], F32)
    S = k.sb("Sr", [128, 2, 512], F32); Sb = k.sb("Srb", [128, 2, 512], BF16)
    yTs = k.sb("yTs1", [128, 4, TP], BF16)
    ld = [[k.sb(f"rl{n}{i}", [128, 512], F32) for n in ("qk", "v", "g")] for i in range(2)]
    def db(name, shape, dt):
        return [k.sb(f"{name}{i}", shape, dt) for i in range(2)]
    DB = dict(
        ro=db("ro", [128, 2, 256], F32), tmp=db("rtmp", [128, 2, 128], F32), tmp2=db("rtmp2", [128, 2, 128], F32),
        qk16=db("qk16", [128, 512], BF16), kd16=db("kd16", [128, 256], BF16), v16=db("v16", [128, 512], BF16),
        qkT=db("qkT", [128, 4, 128], BF16), scT=db("scT", [128, 128], BF16), yi=db("yi", [128, 512], F32),
        yo=db("yor", [128, 512], F32), sg=db("sgr", [128, 512], F32), junk=db("junkr", [128, 512], F32), yb=db("ybr", [128, 512], BF16),
        s1=db("rs1", [128, 1], F32), s2=db("rs2", [128, 1], F32), mean=db("rmean", [128, 1], F32), rstd=db("rrstd", [128, 1], F32))
    psT = k.ps("psTq", [128, 1024], BF16)
    psSc = k.ps("psSc", [128, 512], F32)
    psYi = k.ps("psYi", [128, 512], F32); psYe = k.ps("psYe", [128, 512], F32)
    psS = [k.ps(f"psSr{i}", [128, 512], F32) for i in range(2)]
    u = 0
    for h in range(nheads):
        k.dma("sp", innerT.t[:, :], RC["innerT"][h, :, :], writes=[innerT])
        k.dma("sp", gnw.t[:, :], prm["ret_gn_w"][0:1, h * 512:(h + 1) * 512].to_broadcast([128, 512]), writes=[gnw])
        k.op("pool", lambda e: e.memset(S.t[:, :, :], 0.0), writes=[S])
        k.op("pool", lambda e: e.memset(Sb.t[:, :, :], 0.0), writes=[Sb])
        cd = float(np.exp(128.0 * RET_LOG_G[h]))
        for c in range(nchunks):
            t0 = c * 128
            L = ld[u % 2]
            (ro, tmp, tmp2, qk16, kd16, v16, qkT, scT, yi, yo, sg, junk, yb, s1, s2, mean, rstd) = [
                DB[n][u % 2] for n in ("ro", "tmp", "tmp2", "qk16", "kd16", "v16", "qkT", "scT", "yi", "yo", "sg", "junk", "yb", "s1", "s2", "mean", "rstd")]
            u += 1
            k.dma("sp", L[0].t[:, 0:256], P1[t0:t0 + 128, h * 256:(h + 1) * 256], writes=[L[0]])
            k.dma("sp", L[0].t[:, 256:512], P1[t0:t0 + 128, 4096 + h * 256:4096 + (h + 1) * 256], writes=[L[0]])
            k.dma("act", L[1].t[:, :], P1[t0:t0 + 128, 8192 + h * 512:8192 + (h + 1) * 512], writes=[L[1]])
            k.dma("act", L[2].t[:, :], P1[t0:t0 + 128, 16384 + h * 512:16384 + (h + 1) * 512], writes=[L[2]])
            X = L[0].t[:, :].rearrange("p (a d) -> p a d", a=2)
            cosb = rope.t[:, c, 0:128].unsqueeze(1).to_broadcast([128, 2, 128])
            sinb = rope.t[:, c, 128:256].unsqueeze(1).to_broadcast([128, 2, 128])
            k.op("dve", lambda e: e.tensor_tensor(out=ro.t[:, :, 0:128], in0=X[:, :, 0:128], in1=cosb, op=ALU.mult), reads=[L[0], rope], writes=[ro])
            k.op("pool", lambda e: e.tensor_tensor(out=tmp.t[:, :, :], in0=X[:, :, 128:256], in1=sinb, op=ALU.mult), reads=[L[0], rope], writes=[tmp])
            k.op("dve", lambda e: e.tensor_tensor(out=ro.t[:, :, 0:128], in0=ro.t[:, :, 0:128], in1=tmp.t[:, :, :], op=ALU.subtract), reads=[ro, tmp], writes=[ro])
            k.op("dve", lambda e: e.tensor_tensor(out=ro.t[:, :, 128:256], in0=X[:, :, 0:128], in1=sinb, op=ALU.mult), reads=[L[0], rope], writes=[ro])
            k.op("pool", lambda e: e.tensor_tensor(out=tmp2.t[:, :, :], in0=X[:, :, 128:256], in1=cosb, op=ALU.mult), reads=[L[0], rope], writes=[tmp2])
            k.op("dve", lambda e: e.tensor_tensor(out=ro.t[:, :, 128:256], in0=ro.t[:, :, 128:256], in1=tmp2.t[:, :, :], op=ALU.add), reads=[ro, tmp2], writes=[ro])
            k.op("act", lambda e: e.copy(out=qk16.t[:, :], in_=ro.t[:, :, :].rearrange("p a d -> p (a d)")), reads=[ro], writes=[qk16])
            k.op("dve", lambda e: e.tensor_scalar(out=kd16.t[:, :], in0=ro.t[:, 1, :], scalar1=kdec.t[:, h:h + 1], scalar2=None, op0=ALU.mult), reads=[ro, kdec], writes=[kd16])
            k.op("pool", lambda e: e.tensor_copy(out=v16.t[:, :], in_=L[1].t[:, :]), reads=[L[1]], writes=[v16])
            k.mm_multi(psT, [(psT.t[:, j * 128:(j + 1) * 128], [(qk16.t[:, j * 128:(j + 1) * 128], ident_bf.t[:, :])]) for j in range(4)], reads=[qk16, ident_bf], transpose=True)
            k.op("act", lambda e: e.copy(out=qkT.t[:, :, :], in_=psT.t[:, 0:512].rearrange("p (j t) -> p j t", j=4)), reads=[psT], writes=[qkT])
            k.mm(psSc, psSc.t[:, 0:128], [(qkT.t[:, 2, :], qkT.t[:, 0, :]), (qkT.t[:, 3, :], qkT.t[:, 1, :])], reads=[qkT])
            k.op("dve", lambda e: e.tensor_tensor(out=scT.t[:, :], in0=psSc.t[:, 0:128], in1=innerT.t[:, :], op=ALU.mult), reads=[psSc, innerT], writes=[scT])
            k.mm(psYi, psYi.t[:, :], [(scT.t[:, :], v16.t[:, :])], reads=[scT, v16])
            k.mm(psYe, psYe.t[:, :], [(qkT.t[:, 0, :], Sb.t[:, 0, :]), (qkT.t[:, 1, :], Sb.t[:, 1, :])], reads=[qkT, Sb])
            k.op("act", lambda e: e.copy(out=yi.t[:, :], in_=psYi.t[:, :]), reads=[psYi], writes=[yi])
            k.op("dve", lambda e: e.scalar_tensor_tensor(out=yo.t[:, :], in0=psYe.t[:, :], scalar=qdec.t[:, h:h + 1], in1=yi.t[:, :], op0=ALU.mult, op1=ALU.add),
                 reads=[psYe, qdec, yi], writes=[yo])
            for dc in range(2):
                k.mm(psS[dc], psS[dc].t[:, :], [(kd16.t[:, dc * 128:(dc + 1) * 128], v16.t[:, :])], reads=[kd16, v16])
                k.op("dve", lambda e: e.scalar_tensor_tensor(out=S.t[:, dc, :], in0=S.t[:, dc, :], scalar=cd, in1=psS[dc].t[:, :], op0=ALU.mult, op1=ALU.add),
                     reads=[S, psS[dc]], writes=[S])
            k.op("act", lambda e: e.copy(out=Sb.t[:, :, :], in_=S.t[:, :, :]), reads=[S], writes=[Sb])
            k.op("act", lambda e: e.activation(out=junk.t[:, :], in_=yo.t[:, :], func=AF.Identity, accum_out=s1.t[:, :]), reads=[yo], writes=[junk, s1])
            k.op("act", lambda e: e.activation(out=junk.t[:, :], in_=yo.t[:, :], func=AF.Square, accum_out=s2.t[:, :]), reads=[yo], writes=[junk, s2])
            k.op("dve", lambda e: e.tensor_scalar(out=mean.t[:, :], in0=s1.t[:, :], scalar1=1.0 / 512, scalar2=None, op0=ALU.mult), reads=[s1], writes=[mean])
            k.op("dve", lambda e: e.tensor_tensor(out=s1.t[:, :], in0=mean.t[:, :], in1=mean.t[:, :], op=ALU.mult), reads=[mean], writes=[s1])
            k.op("dve", lambda e: e.scalar_tensor_tensor(out=rstd.t[:, :], in0=s2.t[:, :], scalar=1.0 / 512, in1=s1.t[:, :], op0=ALU.mult, op1=ALU.subtract), reads=[s2, s1], writes=[rstd])
            k.op("dve", lambda e: e.tensor_scalar(out=rstd.t[:, :], in0=rstd.t[:, :], scalar1=1e-6, scalar2=None, op0=ALU.add), reads=[rstd], writes=[rstd])
            k.op("act", lambda e: e.activation(out=rstd.t[:, :], in_=rstd.t[:, :], func=AF.Sqrt), reads=[rstd], writes=[rstd])
            k.op("dve", lambda e: e.reciprocal(out=rstd.t[:, :], in_=rstd.t[:, :]), reads=[rstd], writes=[rstd])
            k.op("dve", lambda e: e.tensor_scalar(out=yo.t[:, :], in0=yo.t[:, :], scalar1=mean.t[:, 0:1], scalar2=rstd.t[:, 0:1], op0=ALU.subtract, op1=ALU.mult),
                 reads=[yo, mean, rstd], writes=[yo])
            k.op("pool", lambda e: e.tensor_tensor(out=yo.t[:, :], in0=yo.t[:, :], in1=gnw.t[:, :], op=ALU.mult), reads=[yo, gnw], writes=[yo])
            k.op("act", lambda e: e.activation(out=sg.t[:, :], in_=L[2].t[:, :], func=AF.Silu), reads=[L[2]], writes=[sg])
            k.op("dve", lambda e: e.tensor_tensor(out=yb.t[:, :], in0=yo.t[:, :], in1=sg.t[:, :], op=ALU.mult), reads=[yo, sg], writes=[yb])
            k.mm_multi(psT, [(psT.t[:, 512 + j * 128:512 + (j + 1) * 128], [(yb.t[:, j * 128:(j + 1) * 128], ident_bf.t[:, :])]) for j in range(4)],
                       reads=[yb, ident_bf], transpose=True)
            k.op("act", lambda e: e.copy(out=yTs.t[:, :, t0:t0 + 128], in_=psT.t[:, 512:1024].rearrange("p (j t) -> p j t", j=4)), reads=[psT], writes=[yTs])
        for j in range(4):
            k.dma("sp", YT[0:nchunks, :, h * 4 + j, :].rearrange("t p k -> p t k"), yTs.t[:, j, 0:nchunks * 128].rearrange("p (t k) -> p t k", k=128), reads=[yTs])
        k.dma("sp", out_ret[h * 256:(h + 1) * 256, :].rearrange("(dc p) v -> p dc v", p=128), S.t[:, :, :], reads=[S])


def make_ret_consts():
    lg = np.array(RET_LOG_G, np.float64)
    i = np.arange(128, dtype=np.float64)
    rel_ = i[None, :] - i[:, None]
    innerT = np.where(rel_[None] >= 0, np.exp(rel_[None] * lg[:, None, None]), 0.0) / 16.0
    qdec = np.exp((i[:, None] + 1.0) * lg[None]) / 16.0
    kdec = np.exp((127.0 - i)[:, None] * lg[None])
    half = 128
    freq = (10000.0 ** (-np.arange(half, dtype=np.float32) / half)).astype(np.float32)
    pos = np.concatenate([np.arange(TP), np.full(NS, 16384)]).astype(np.float32)
    ang = pos[:, None] * freq[None]
    rope = np.concatenate([np.cos(ang), np.sin(ang)], 1)
    return {"innerT": innerT.astype(np.float32), "qdec": qdec.astype(np.float32), "kdec": kdec.astype(np.float32), "rope": rope.astype(np.float32)}


def to_fm(k, src, c0, n, dst, ps, ident, bf=False):
    idt = ident.t[:NS, :NS]
    k.mm_multi(ps, [(ps.t[:, j * NS:(j + 1) * NS], [(src.t[:NS, c0 + j * 128:c0 + (j + 1) * 128], idt)]) for j in range(n)],
               reads=[src, ident], transpose=True)
    k.op("dve", lambda e: e.tensor_copy(out=dst, in_=ps.t[:, 0:n * NS].rearrange("p (j s) -> p j s", s=NS)), reads=[ps], writes=[])


def to_tm(k, srcfm, n, dst, c0, pss, ident):
    for g in range((n + 3) // 4):
        ps = pss[g % len(pss)]
        k.mm_multi(ps, [(ps.t[:NS, jj * 128:(jj + 1) * 128], [(srcfm.t[:, g * 4 + jj, :], ident.t[:, :])]) for jj in range(4)],
                   reads=[srcfm, ident], transpose=True)
        k.op("act", lambda e: e.copy(out=dst.t[:NS, c0 + g * 512:c0 + (g + 1) * 512], in_=ps.t[:NS, :]), reads=[ps], writes=[dst])


def bc16(k, name, row_ap, n, en="act"):
    t = k.sb(name, [NS, n], F32)
    k.dma(en, t.t[:, :], row_ap.to_broadcast([NS, n]), writes=[t])
    return t


def phase_ssd_s(k, P0, prm, C, st_conv, st_ssm, YT, out_ssm, SBC):
    ident, ident_bf = C["ident"], C["ident_bf"]
    XB = 4096
    xa = k.sb("sxa", [NS, 6144], F32)
    outer_es = k.es
    with contextlib.ExitStack() as esc:
        k.es = esc
        CW = 1536
        cw = k.sb("scw", [NS, 4, CW], F32); cb = k.sb("scb", [NS, CW], F32)
        cbuf = k.sb("scbuf", [NS, 3, CW], F32); xr = k.sb("sxr", [NS, CW], F32)
        acc = k.sb("sacc", [NS, CW], F32); tmp = k.sb("stmp", [NS, CW], F32)
        for q in range(6144 // CW):
            c0 = q * CW
            for kk in range(4):
                k.dma("act", cw.t[:, kk, :], prm["ssd_conv_w"][kk:kk + 1, c0:c0 + CW].to_broadcast([NS, CW]), writes=[cw])
            k.dma("act", cb.t[:, :], prm["ssd_conv_b"][0:1, c0:c0 + CW].to_broadcast([NS, CW]), writes=[cb])
            k.dma("sp", cbuf.t[:, :, :], st_conv[:, :, c0:c0 + CW], writes=[cbuf])
            k.dma("sp", xr.t[:, :], P0[TP:T, XB + c0:XB + c0 + CW], writes=[xr])
            k.op("dve", lambda e: e.tensor_tensor(out=acc.t[:, :], in0=xr.t[:, :], in1=cw.t[:, 3, :], op=ALU.mult), reads=[xr, cw], writes=[acc])
            for kk in range(3):
                k.op("pool", lambda e: e.tensor_tensor(out=tmp.t[:, :], in0=cbuf.t[:, kk, :], in1=cw.t[:, kk, :], op=ALU.mult), reads=[cbuf, cw, tmp], writes=[tmp])
                k.op("dve", lambda e: e.tensor_tensor(out=acc.t[:, :], in0=acc.t[:, :], in1=tmp.t[:, :], op=ALU.add), reads=[acc, tmp], writes=[acc])
            k.op("dve", lambda e: e.tensor_tensor(out=acc.t[:, :], in0=acc.t[:, :], in1=cb.t[:, :], op=ALU.add), reads=[acc, cb], writes=[acc])
            k.op("act", lambda e: e.activation(out=xa.t[:, c0:c0 + CW], in_=acc.t[:, :], func=AF.Silu), reads=[acc], writes=[xa])
        k.barrier()
    k.es = outer_es
    k.dma("sp", SBC[:, :], xa.t[:, 4096:6144], reads=[xa])
    a_bc = bc16(k, "sa_bc", prm["ssd_a_log"][0:1, :], 64)
    k.op("act", lambda e: e.activation(out=a_bc.t[:, :], in_=a_bc.t[:, :], func=AF.Exp), reads=[a_bc], writes=[a_bc])
    d_bc = bc16(k, "sd_bc", prm["ssd_d"][0:1, :], 64)
    dtb = bc16(k, "sdtb", prm["ssd_dt_bias"][0:1, :], 64)
    dt = k.sb("sdt", [NS, 64], F32); ee = k.sb("see", [NS, 64], F32)
    k.dma("sp", dt.t[:, :], P0[TP:T, 10240:10304], writes=[dt])
    k.op("dve", lambda e: e.tensor_tensor(out=dt.t[:, :], in0=dt.t[:, :], in1=dtb.t[:, :], op=ALU.add), reads=[dt, dtb], writes=[dt])
    k.op("act", lambda e: e.activation(out=dt.t[:, :], in_=dt.t[:, :], func=AF.Exp), reads=[dt], writes=[dt])
    k.op("act", lambda e: e.activation(out=dt.t[:, :], in_=dt.t[:, :], func=AF.Ln, bias=1.0), reads=[dt], writes=[dt])
    k.op("dve", lambda e: e.tensor_tensor(out=ee.t[:, :], in0=dt.t[:, :], in1=a_bc.t[:, :], op=ALU.mult), reads=[dt, a_bc], writes=[ee])
    k.op("act", lambda e: e.activation(out=ee.t[:, :], in_=ee.t[:, :], func=AF.Exp, scale=-1.0), reads=[ee], writes=[ee])
    h3 = lambda t_: t_.t[:, 0:4096].rearrange("p (h d) -> p h d", h=64)
    b64 = lambda t_: t_.t[:, :].unsqueeze(2).to_broadcast([NS, 64, 64])
    xdt = k.sb("sxdt", [NS, 4096], F32); er = k.sb("ser", [NS, 4096], F32)
    k.op("dve", lambda e: e.tensor_tensor(out=h3(xdt), in0=h3(xa), in1=b64(dt), op=ALU.mult), reads=[xa, dt], writes=[xdt])
    k.op("dve", lambda e: e.tensor_copy(out=h3(er), in_=b64(ee)), reads=[ee], writes=[er])
    psA = k.ps("spsA", [128, 512], F32); psB = k.ps("spsB", [128, 512], F32)
    pstm = [k.ps(f"spstm{i}", [128, 512], F32) for i in range(2)]
    psbf = k.ps("spsbf", [128, 1024], BF16)
    xdtT = k.sb("sxdtT", [128, 32, NS], F32); eT = k.sb("seT", [128, 32, NS], F32)
    to_fm(k, xdt, 0, 32, xdtT.t[:, :, :], psA, ident); xdtT.w = k.last("dve")
    to_fm(k, er, 0, 32, eT.t[:, :, :], psB, ident); eT.w = k.last("dve")
    Yfm = k.sb("sYfm", [128, 32, NS], F32)
    St = [k.sb(f"sSt{i}", [128, 32, 128], F32) for i in range(2)]
    BCb = [k.sb(f"sBCb{i}", [128, 16, 128], F32) for i in range(2)]
    tm = k.sb("stm", [128, 32, 128], F32)
    g4 = lambda ap: ap.rearrange("p (g i) n -> p g i n", g=8)
    for s_ in range(NS):
        Sb_ = St[s_ % 2]; Bb = BCb[s_ % 2]
        k.dma("sp", Sb_.t[:, :, :], st_ssm[s_].rearrange("(j p) n -> p j n", p=128), writes=[Sb_])
        k.dma("act", Bb.t[:, :, :], SBC[s_:s_ + 1, :].rearrange("o (g n) -> o g n", n=128).to_broadcast([128, 16, 128]), writes=[Bb])
        xcol = xdtT.t[:, :, s_].rearrange("p (g i) -> p g i", g=8).unsqueeze(3).to_broadcast([128, 8, 4, 128])
        k.op("pool", lambda e: e.tensor_tensor(out=g4(tm.t[:, :, :]), in0=Bb.t[:, 0:8, :].unsqueeze(2).to_broadcast([128, 8, 4, 128]), in1=xcol, op=ALU.mult),
             reads=[Bb, xdtT, tm], writes=[tm])
        k.op("dve", lambda e: e.tensor_tensor(out=Sb_.t[:, :, :], in0=Sb_.t[:, :, :], in1=eT.t[:, :, s_].unsqueeze(2).to_broadcast([128, 32, 128]), op=ALU.mult),
             reads=[Sb_, eT], writes=[Sb_])
        k.op("dve", lambda e: e.tensor_tensor(out=Sb_.t[:, :, :], in0=Sb_.t[:, :, :], in1=tm.t[:, :, :], op=ALU.add), reads=[Sb_, tm], writes=[Sb_])
        k.dma("sp", out_ssm[s_].rearrange("(j p) n -> p j n", p=128), Sb_.t[:, :, :], reads=[Sb_])
        k.op("pool", lambda e: e.tensor_tensor(out=g4(tm.t[:, :, :]), in0=g4(Sb_.t[:, :, :]), in1=Bb.t[:, 8:16, :].unsqueeze(2).to_broadcast([128, 8, 4, 128]), op=ALU.mult),
             reads=[Sb_, Bb, tm], writes=[tm])
        k.op("dve", lambda e: e.tensor_reduce(out=Yfm.t[:, :, s_], in_=tm.t[:, :, :], axis=AX.X, op=ALU.add), reads=[tm], writes=[Yfm])
    y = k.sb("sy", [NS, 4096], F32)
    to_tm(k, Yfm, 32, y, 0, pstm, ident)
    k.op("dve", lambda e: e.tensor_tensor(out=h3(xdt), in0=h3(xa), in1=b64(d_bc), op=ALU.mult), reads=[xa, d_bc], writes=[xdt])
    k.op("dve", lambda e: e.tensor_tensor(out=y.t[:, :], in0=y.t[:, :], in1=xdt.t[:, :], op=ALU.add), reads=[y, xdt], writes=[y])
    z = k.sb("sz_", [NS, 4096], F32)
    k.dma("sp", z.t[:, :], P0[TP:T, 0:4096], writes=[z])
    k.op("act", lambda e: e.activation(out=z.t[:, :], in_=z.t[:, :], func=AF.Silu), reads=[z], writes=[z])
    k.op("dve", lambda e: e.tensor_tensor(out=y.t[:, :], in0=y.t[:, :], in1=z.t[:, :], op=ALU.mult), reads=[y, z], writes=[y])
    k.op("pool", lambda e: e.tensor_tensor(out=er.t[:, :], in0=y.t[:, :], in1=y.t[:, :], op=ALU.mult), reads=[y, er], writes=[er])
    ssq = k.sb("sssq", [NS, 8], F32)
    k.op("dve", lambda e: e.tensor_reduce(out=ssq.t[:, :], in_=er.t[:, :].rearrange("p (g c) -> p g c", g=8), axis=AX.X, op=ALU.add), reads=[er], writes=[ssq])
    k.op("dve", lambda e: e.tensor_scalar(out=ssq.t[:, :], in0=ssq.t[:, :], scalar1=1.0 / 512, scalar2=1e-5, op0=ALU.mult, op1=ALU.add), reads=[ssq], writes=[ssq])
    k.op("act", lambda e: e.activation(out=ssq.t[:, :], in_=ssq.t[:, :], func=AF.Sqrt), reads=[ssq], writes=[ssq])
    k.op("dve", lambda e: e.reciprocal(out=ssq.t[:, :], in_=ssq.t[:, :]), reads=[ssq], writes=[ssq])
    k.op("dve", lambda e: e.tensor_tensor(out=y.t[:, :].rearrange("p (g c) -> p g c", g=8), in0=y.t[:, :].rearrange("p (g c) -> p g c", g=8),
                                          in1=ssq.t[:, :].unsqueeze(2).to_broadcast([NS, 8, 512]), op=ALU.mult), reads=[y, ssq], writes=[y])
    nw = bc16(k, "snw", prm["ssd_norm_w"][0:1, :], 4096)
    y16 = k.sb("sy16", [NS, 4096], BF16)
    k.op("dve", lambda e: e.tensor_tensor(out=y16.t[:, :], in0=y.t[:, :], in1=nw.t[:, :], op=ALU.mult), reads=[y, nw], writes=[y16])
    yT = k.sb("syT", [128, 32, NS], BF16)
    to_fm(k, y16, 0, 32, yT.t[:, :, :], psbf, ident_bf); yT.w = k.last("dve")
    k.dma("sp", YT[TP // 128, :, 0:32, 0:NS], yT.t[:, :, :], reads=[yT])


def phase_rwkv_s(k, P0, prm, C, st_shift, st_wkv, YT, out_wkv, SV):
    ident, ident_bf = C["ident"], C["ident_bf"]
    true_outer = k.es
    v = k.sb("wv", [NS, 4096], F32); tA = k.sb("wtA", [NS, 4096], F32); bsum = k.sb("wbsum", [NS, 64], F32)
    psbf = k.ps("wpsbf", [128, 1024], BF16)
    psA = k.ps("wpsA", [128, 512], F32)
    pstm = [k.ps(f"wpstm{i}", [128, 512], F32) for i in range(2)]
    psL = [k.ps(f"wpsL{i}", [128, 512], F32) for i in range(2)]
    esA = contextlib.ExitStack()
    k.es = esA
    outer_es = esA
    r = k.sb("wr", [NS, 4096], F32); kt = k.sb("wk", [NS, 4096], F32)
    lw16 = k.sb("wlw16", [NS, 256], BF16)
    with contextlib.ExitStack() as esc:
        k.es = esc
        CW = 3136
        p = k.sb("wp", [NS, CW], F32); pp = k.sb("wpp", [NS, CW], F32); mu = k.sb("wmu", [NS, CW], F32)
        for q in range(4):
            c0 = q * CW
            k.dma("sp", p.t[:, :], P0[TP:T, PB + c0:PB + c0 + CW], writes=[p])
            k.dma("sp", pp.t[:, :], st_shift[:, c0:c0 + CW], writes=[pp])
            k.dma("act", mu.t[:, :], prm["rwkv_mu"][0:1, c0:c0 + CW].to_broadcast([NS, CW]), writes=[mu])
            k.op("dve", lambda e: e.tensor_tensor(out=pp.t[:, :], in0=pp.t[:, :], in1=p.t[:, :], op=ALU.subtract), reads=[pp, p], writes=[pp])
            k.op("dve", lambda e: e.tensor_tensor(out=pp.t[:, :], in0=pp.t[:, :], in1=mu.t[:, :], op=ALU.mult), reads=[pp, mu], writes=[pp])
            k.op("dve", lambda e: e.tensor_tensor(out=pp.t[:, :], in0=pp.t[:, :], in1=p.t[:, :], op=ALU.add), reads=[pp, p], writes=[pp])
            for (dst, d0) in ((r, 0), (kt, 4096), (v, 8192)):
                lo = max(c0, d0); hi = min(c0 + CW, d0 + 4096)
                if lo < hi:
                    k.op("act", lambda e: e.copy(out=dst.t[:, lo - d0:hi - d0], in_=pp.t[:, lo - c0:hi - c0]), reads=[pp], writes=[dst])
            if c0 + CW > 12288:
                o = 12288 - c0
                k.op("act", lambda e: e.activation(out=lw16.t[:, 0:128], in_=pp.t[:, o:o + 128], func=AF.Tanh), reads=[pp], writes=[lw16])
                k.op("act", lambda e: e.copy(out=lw16.t[:, 128:256], in_=pp.t[:, o + 128:o + 256]), reads=[pp], writes=[lw16])
        k.barrier()
    k.es = outer_es
    lT = k.sb("wlT", [128, 2, NS], BF16)
    to_fm(k, lw16, 0, 2, lT.t[:, :, :], psbf, ident_bf); lT.w = k.last("dve")
    sig = k.sb("wsig", [NS, 4096], F32); aa = k.sb("waa", [NS, 4096], F32)
    with contextlib.ExitStack() as esc:
        k.es = esc
        wup = k.sb("wwup", [128, 4096], BF16); aup = k.sb("waup", [128, 4096], BF16)
        k.dma("pool", wup.t[:, :], prm["rwkv_w_up"][:, :], writes=[wup])
        k.dma("pool", aup.t[:, :], prm["rwkv_a_up"][:, :], writes=[aup])
        w0 = bc16(k, "ww0", prm["rwkv_w0"][0:1, :], 4096); a0 = bc16(k, "wa0", prm["rwkv_a0"][0:1, :], 4096)
        for cg in range(8):
            cs = slice(cg * 512, (cg + 1) * 512)
            k.mm(psL[0], psL[0].t[:NS, :], [(lT.t[:, 0, :], wup.t[:, cs])], reads=[lT, wup])
            k.mm(psL[1], psL[1].t[:NS, :], [(lT.t[:, 1, :], aup.t[:, cs])], reads=[lT, aup])
            k.op("dve", lambda e: e.tensor_tensor(out=sig.t[:, cs], in0=psL[0].t[:NS, :], in1=w0.t[:, cs], op=ALU.add), reads=[psL[0], w0], writes=[sig])
            k.op("dve", lambda e: e.tensor_tensor(out=aa.t[:, cs], in0=psL[1].t[:NS, :], in1=a0.t[:, cs], op=ALU.add), reads=[psL[1], a0], writes=[aa])
        k.op("act", lambda e: e.activation(out=sig.t[:, :], in_=sig.t[:, :], func=AF.Sigmoid), reads=[sig], writes=[sig])
        k.op("act", lambda e: e.activation(out=aa.t[:, :], in_=aa.t[:, :], func=AF.Sigmoid), reads=[aa], writes=[aa])
        k.op("act", lambda e: e.activation(out=sig.t[:, :], in_=sig.t[:, :], func=AF.Exp, scale=-C1), reads=[sig], writes=[sig])
        k.barrier()
    k.es = outer_es
    h3 = lambda t_: t_.t[:, 0:4096].rearrange("p (h d) -> p h d", h=64)
    b64 = lambda t_: t_.t[:, :].unsqueeze(2).to_broadcast([NS, 64, 64])
    kk = k.sb("wkk", [NS, 4096], F32); kp = k.sb("wkp", [NS, 4096], F32)
    ssq = k.sb("wssq", [NS, 64], F32)
    with contextlib.ExitStack() as esc:
        k.es = esc
        k_k = bc16(k, "wk_k", prm["rwkv_k_k"][0:1, :], 4096); k_a = bc16(k, "wk_a", prm["rwkv_k_a"][0:1, :], 4096)
        r_k = bc16(k, "wr_k", prm["rwkv_r_k"][0:1, :], 4096)
        k.op("dve", lambda e: e.tensor_tensor(out=kk.t[:, :], in0=kt.t[:, :], in1=k_k.t[:, :], op=ALU.mult), reads=[kt, k_k], writes=[kk])
        k.op("pool", lambda e: e.tensor_tensor(out=tA.t[:, :], in0=kk.t[:, :], in1=kk.t[:, :], op=ALU.mult), reads=[kk, tA], writes=[tA])
        k.op("dve", lambda e: e.tensor_reduce(out=ssq.t[:, :], in_=h3(tA), axis=AX.X, op=ALU.add), reads=[tA], writes=[ssq])
        k.op("dve", lambda e: e.tensor_scalar(out=ssq.t[:, :], in0=ssq.t[:, :], scalar1=1e-24, scalar2=None, op0=ALU.max), reads=[ssq], writes=[ssq])
        k.op("act", lambda e: e.activation(out=ssq.t[:, :], in_=ssq.t[:, :], func=AF.Sqrt), reads=[ssq], writes=[ssq])
        k.op("dve", lambda e: e.reciprocal(out=ssq.t[:, :], in_=ssq.t[:, :]), reads=[ssq], writes=[ssq])
        k.op("dve", lambda e: e.tensor_tensor(out=h3(kk), in0=h3(kk), in1=b64(ssq), op=ALU.mult), reads=[kk, ssq], writes=[kk])
        k.op("dve", lambda e: e.scalar_tensor_tensor(out=tA.t[:, :], in0=aa.t[:, :], scalar=-1.0, in1=k_a.t[:, :], op0=ALU.add, op1=ALU.mult), reads=[aa, k_a, tA], writes=[tA])
        k.op("dve", lambda e: e.scalar_tensor_tensor(out=kp.t[:, :], in0=tA.t[:, :], scalar=1.0, in1=kt.t[:, :], op0=ALU.add, op1=ALU.mult), reads=[tA, kt], writes=[kp])
        k.op("dve", lambda e: e.tensor_tensor(out=aa.t[:, :], in0=kk.t[:, :], in1=aa.t[:, :], op=ALU.mult), reads=[kk, aa], writes=[aa])
        k.op("pool", lambda e: e.tensor_tensor(out=tA.t[:, :], in0=r.t[:, :], in1=kp.t[:, :], op=ALU.mult), reads=[r, kp, tA], writes=[tA])
        k.op("pool", lambda e: e.tensor_tensor(out=tA.t[:, :], in0=tA.t[:, :], in1=r_k.t[:, :], op=ALU.mult), reads=[tA, r_k], writes=[tA])
        k.op("dve", lambda e: e.tensor_reduce(out=bsum.t[:, :], in_=h3(tA), axis=AX.X, op=ALU.add), reads=[tA], writes=[bsum])
        k.barrier()
    k.es = outer_es
    for q, t_ in enumerate((kk, sig, aa, kp, r)):
        k.dma("sp", SV[q, :, :], t_.t[:, :], reads=[t_])
    k.barrier()
    esA.close()
    k.es = true_outer
    outer_es = true_outer
    vT = k.sb("wvT", [128, 32, NS], F32)
    to_fm(k, v, 0, 32, vT.t[:, :, :], psA, ident); vT.w = k.last("dve")
    Yfm = k.sb("wYfm", [128, 32, NS], F32)
    esB = contextlib.ExitStack()
    k.es = esB
    St = [k.sb(f"wSt{i}", [128, 32, 64], F32) for i in range(2)]
    BC = [k.sb(f"wBC{i}", [128, 5, 32, 64], F32) for i in range(2)]
    t1 = k.sb("wt1", [128, 32, 64], F32); t2 = k.sb("wt2", [128, 32, 64], F32)
    sk = k.sb("wsk", [128, 32], F32)
    for s_ in range(NS):
        S_ = St[s_ % 2]; B_ = BC[s_ % 2]
        k.dma("sp", S_.t[:, :, :], st_wkv[s_].rearrange("(j p) n -> p j n", p=128), writes=[S_])
        for q in range(5):
            src = SV[q, s_:s_ + 1, :].rearrange("o (j hh n) -> o hh j n", hh=2, n=64)
            for hh in range(2):
                k.dma("act" if hh == 0 else "sp", B_.t[hh * 64:(hh + 1) * 64, q, :, :], src[:, hh, :, :].to_broadcast([64, 32, 64]), writes=[B_])
        bcj = lambda col: col.unsqueeze(2).to_broadcast([128, 32, 64])
        k.op("dve", lambda e: e.tensor_tensor(out=t1.t[:, :, :], in0=S_.t[:, :, :], in1=B_.t[:, 0, :, :], op=ALU.mult), reads=[S_, B_], writes=[t1])
        k.op("dve", lambda e: e.tensor_reduce(out=sk.t[:, :], in_=t1.t[:, :, :], axis=AX.X, op=ALU.add), reads=[t1], writes=[sk])
        k.op("pool", lambda e: e.tensor_tensor(out=S_.t[:, :, :], in0=S_.t[:, :, :], in1=B_.t[:, 1, :, :], op=ALU.mult), reads=[S_, B_, t1], writes=[S_])
        k.op("dve", lambda e: e.tensor_tensor(out=t1.t[:, :, :], in0=B_.t[:, 2, :, :], in1=bcj(sk.t[:, :]), op=ALU.mult), reads=[B_, sk], writes=[t1])
        k.op("pool", lambda e: e.tensor_tensor(out=t2.t[:, :, :], in0=B_.t[:, 3, :, :], in1=bcj(vT.t[:, :, s_]), op=ALU.mult), reads=[B_, vT, t2], writes=[t2])
        k.op("dve", lambda e: e.tensor_tensor(out=S_.t[:, :, :], in0=S_.t[:, :, :], in1=t1.t[:, :, :], op=ALU.subtract), reads=[S_, t1], writes=[S_])
        k.op("dve", lambda e: e.tensor_tensor(out=S_.t[:, :, :], in0=S_.t[:, :, :], in1=t2.t[:, :, :], op=ALU.add), reads=[S_, t2], writes=[S_])
        k.dma("sp", out_wkv[s_].rearrange("(j p) n -> p j n", p=128), S_.t[:, :, :], reads=[S_])
        k.op("pool", lambda e: e.tensor_tensor(out=t2.t[:, :, :], in0=S_.t[:, :, :], in1=B_.t[:, 4, :, :], op=ALU.mult), reads=[S_, B_, t2], writes=[t2])
        k.op("dve", lambda e: e.tensor_reduce(out=Yfm.t[:, :, s_], in_=t2.t[:, :, :], axis=AX.X, op=ALU.add), reads=[t2], writes=[Yfm])
    k.barrier()
    esB.close()
    k.es = true_outer
    y = k.sb("wy", [NS, 4096], F32)
    to_tm(k, Yfm, 32, y, 0, pstm, ident)
    hs = k.sb("whs", [NS, 64], F32); hq = k.sb("whq", [NS, 64], F32); hm = k.sb("whm", [NS, 64], F32); hr = k.sb("whr", [NS, 64], F32)
    k.op("dve", lambda e: e.tensor_reduce(out=hs.t[:, :], in_=h3(y), axis=AX.X, op=ALU.add), reads=[y], writes=[hs])
    k.op("pool", lambda e: e.tensor_tensor(out=tA.t[:, :], in0=y.t[:, :], in1=y.t[:, :], op=ALU.mult), reads=[y, tA], writes=[tA])
    k.op("dve", lambda e: e.tensor_reduce(out=hq.t[:, :], in_=h3(tA), axis=AX.X, op=ALU.add), reads=[tA], writes=[hq])
    k.op("dve", lambda e: e.tensor_scalar(out=hm.t[:, :], in0=hs.t[:, :], scalar1=1.0 / 64, scalar2=None, op0=ALU.mult), reads=[hs], writes=[hm])
    k.op("dve", lambda e: e.tensor_tensor(out=hs.t[:, :], in0=hm.t[:, :], in1=hm.t[:, :], op=ALU.mult), reads=[hm], writes=[hs])
    k.op("dve", lambda e: e.scalar_tensor_tensor(out=hr.t[:, :], in0=hq.t[:, :], scalar=1.0 / 64, in1=hs.t[:, :], op0=ALU.mult, op1=ALU.subtract), reads=[hq, hs], writes=[hr])
    k.op("dve", lambda e: e.tensor_scalar(out=hr.t[:, :], in0=hr.t[:, :], scalar1=64e-5, scalar2=None, op0=ALU.add), reads=[hr], writes=[hr])
    k.op("act", lambda e: e.activation(out=hr.t[:, :], in_=hr.t[:, :], func=AF.Sqrt), reads=[hr], writes=[hr])
    k.op("dve", lambda e: e.reciprocal(out=hr.t[:, :], in_=hr.t[:, :]), reads=[hr], writes=[hr])
    k.op("dve", lambda e: e.tensor_tensor(out=h3(y), in0=h3(y), in1=b64(hm), op=ALU.subtract), reads=[y, hm], writes=[y])
    k.op("dve", lambda e: e.tensor_tensor(out=h3(y), in0=h3(y), in1=b64(hr), op=ALU.mult), reads=[y, hr], writes=[y])
    lnw = bc16(k, "wlnw", prm["rwkv_lnx_w"][0:1, :], 4096); lnb = bc16(k, "wlnb", prm["rwkv_lnx_b"][0:1, :], 4096)
    k.op("pool", lambda e: e.tensor_tensor(out=y.t[:, :], in0=y.t[:, :], in1=lnw.t[:, :], op=ALU.mult), reads=[y, lnw], writes=[y])
    k.op("pool", lambda e: e.tensor_tensor(out=y.t[:, :], in0=y.t[:, :], in1=lnb.t[:, :], op=ALU.add), reads=[y, lnb], writes=[y])
    k.op("dve", lambda e: e.tensor_tensor(out=h3(tA), in0=h3(v), in1=b64(bsum), op=ALU.mult), reads=[v, bsum, tA], writes=[tA])
    k.op("dve", lambda e: e.tensor_tensor(out=y.t[:, :], in0=y.t[:, :], in1=tA.t[:, :], op=ALU.add), reads=[y, tA], writes=[y])
    g = k.sb("wg", [NS, 4096], F32)
    k.dma("sp", g.t[:, :], P0[TP:T, GB:GB + 4096], reads=[], writes=[g])
    k.op("act", lambda e: e.activation(out=g.t[:, :], in_=g.t[:, :], func=AF.Silu), reads=[g], writes=[g])
    y16 = k.sb("wy16", [NS, 4096], BF16)
    k.op("dve", lambda e: e.tensor_tensor(out=y16.t[:, :], in0=y.t[:, :], in1=g.t[:, :], op=ALU.mult), reads=[y, g], writes=[y16])
    yT = k.sb("wyT", [128, 32, NS], BF16)
    to_fm(k, y16, 0, 32, yT.t[:, :, :], psbf, ident_bf); yT.w = k.last("dve")
    k.dma("sp", YT[TP // 128, :, 32:64, 0:NS], yT.t[:, :, :], reads=[yT])


def phase_ret_s(k, P1, prm, C, RC, st_ret, YT, out_ret):
    ident, ident_bf = C["ident"], C["ident_bf"]
    true_outer = k.es
    psA = k.ps("rpsA", [128, 512], F32); psB = k.ps("rpsB", [128, 512], F32)
    psr = [k.ps(f"rpsr{i}", [128, 512], F32) for i in range(2)]
    psbf = k.ps("rpsbf", [128, 1024], BF16)
    qT = k.sb("rqT", [128, 32, NS], F32); kT = k.sb("rkT", [128, 32, NS], F32)
    Y = k.sb("rY", [NS, 8192], F32)
    gdec = k.sb("rgdec", [128, 32], F32)
    for h in range(16):
        k.op("pool", lambda e: e.memset(gdec.t[:, 2 * h:2 * h + 2], float(np.exp(RET_LOG_G[h]))), writes=[gdec])
    k.op("pool", lambda e: e.memset(Y.t[:, :], 0.0), writes=[Y])
    with contextlib.ExitStack() as esc:
        k.es = esc
        X = k.sb("rX", [NS, 8192], F32); ro = k.sb("rro", [NS, 8192], F32); tmp = k.sb("rtmp_s", [NS, 32, 128], F32)
        rope = k.sb("rrope", [NS, 256], F32)
        k.dma("sp", X.t[:, :], P1[TP:T, 0:8192], writes=[X])
        k.dma("sp", rope.t[:, :], RC["rope"][TP:T, :], writes=[rope])
        X3 = X.t[:, :].rearrange("p (a d) -> p a d", d=256); R3 = ro.t[:, :].rearrange("p (a d) -> p a d", d=256)
        cosb = rope.t[:, 0:128].unsqueeze(1).to_broadcast([NS, 32, 128]); sinb = rope.t[:, 128:256].unsqueeze(1).to_broadcast([NS, 32, 128])
        k.op("dve", lambda e: e.tensor_tensor(out=R3[:, :, 0:128], in0=X3[:, :, 0:128], in1=cosb, op=ALU.mult), reads=[X, rope], writes=[ro])
        k.op("pool", lambda e: e.tensor_tensor(out=tmp.t[:, :, :], in0=X3[:, :, 128:256], in1=sinb, op=ALU.mult), reads=[X, rope], writes=[tmp])
        k.op("dve", lambda e: e.tensor_tensor(out=R3[:, :, 0:128], in0=R3[:, :, 0:128], in1=tmp.t[:, :, :], op=ALU.subtract), reads=[ro, tmp], writes=[ro])
        k.op("dve", lambda e: e.tensor_tensor(out=R3[:, :, 128:256], in0=X3[:, :, 0:128], in1=sinb, op=ALU.mult), reads=[X, rope], writes=[ro])
        k.op("pool", lambda e: e.tensor_tensor(out=tmp.t[:, :, :], in0=X3[:, :, 128:256], in1=cosb, op=ALU.mult), reads=[X, rope, tmp, ro], writes=[tmp])
        k.op("dve", lambda e: e.tensor_tensor(out=R3[:, :, 128:256], in0=R3[:, :, 128:256], in1=tmp.t[:, :, :], op=ALU.add), reads=[ro, tmp], writes=[ro])
        k.op("dve", lambda e: e.tensor_scalar(out=ro.t[:, 0:4096], in0=ro.t[:, 0:4096], scalar1=1.0 / 16, scalar2=None, op0=ALU.mult), reads=[ro], writes=[ro])
        to_fm(k, ro, 0, 32, qT.t[:, :, :], psA, ident); qT.w = k.last("dve")
        to_fm(k, ro, 4096, 32, kT.t[:, :, :], psB, ident); kT.w = k.last("dve")
        k.barrier()
    with contextlib.ExitStack() as esc:
        k.es = esc
        St = [k.sb(f"rSt{i}", [128, 16, 512], F32) for i in range(2)]
        vb = [k.sb(f"rvb{i}", [128, 8, 512], F32) for i in range(2)]
        tt = k.sb("rtt", [128, 16, 512], F32)
        u = 0
        for s_ in range(NS):
            for hf in range(2):
                S_ = St[u % 2]; V_ = vb[u % 2]; u += 1
                j0 = hf * 16
                k.dma("sp", S_.t[:, :, :], st_ret[s_, hf * 2048:(hf + 1) * 2048, :].rearrange("(j p) v -> p j v", p=128), writes=[S_])
                k.dma("act", V_.t[:, :, :], P1[TP + s_:TP + s_ + 1, 8192 + hf * 4096:8192 + (hf + 1) * 4096].rearrange("o (h v) -> o h v", v=512).to_broadcast([128, 8, 512]), writes=[V_])
                kcol = kT.t[:, j0:j0 + 16, s_].rearrange("p (h c) -> p h c", c=2).unsqueeze(3).to_broadcast([128, 8, 2, 512])
                k.op("pool", lambda e: e.tensor_tensor(out=tt.t[:, :, :].rearrange("p (h c) v -> p h c v", c=2), in0=V_.t[:, :, :].unsqueeze(2).to_broadcast([128, 8, 2, 512]), in1=kcol, op=ALU.mult),
                     reads=[V_, kT, tt], writes=[tt])
                k.op("dve", lambda e: e.tensor_tensor(out=S_.t[:, :, :], in0=S_.t[:, :, :], in1=gdec.t[:, j0:j0 + 16].unsqueeze(2).to_broadcast([128, 16, 512]), op=ALU.mult),
                     reads=[S_, gdec], writes=[S_])
                k.op("dve", lambda e: e.tensor_tensor(out=S_.t[:, :, :], in0=S_.t[:, :, :], in1=tt.t[:, :, :], op=ALU.add), reads=[S_, tt], writes=[S_])
                k.dma("sp", out_ret[s_, hf * 2048:(hf + 1) * 2048, :].rearrange("(j p) v -> p j v", p=128), S_.t[:, :, :], reads=[S_])
                for hl in range(8):
                    h = hf * 8 + hl
                    pr = psr[hl % 2]
                    k.mm(pr, pr.t[:NS, :], [(qT.t[:, 2 * h, :], S_.t[:, 2 * hl, :]), (qT.t[:, 2 * h + 1, :], S_.t[:, 2 * hl + 1, :])], reads=[qT, S_])
                    k.op("dve", lambda e: e.scalar_tensor_tensor(out=Y.t[:, h * 512:(h + 1) * 512], in0=pr.t[:NS, :], scalar=ident.t[:NS, s_:s_ + 1], in1=Y.t[:, h * 512:(h + 1) * 512],
                                                               op0=ALU.mult, op1=ALU.add), reads=[pr, ident, Y], writes=[Y])
        k.barrier()
    k.es = true_outer
    h3 = lambda t_: t_.t[:, :].rearrange("p (h d) -> p h d", h=16)
    b16 = lambda t_: t_.t[:, :].unsqueeze(2).to_broadcast([NS, 16, 512])
    sq = k.sb("rsq", [NS, 8192], F32)
    hs = k.sb("rhs", [NS, 16], F32); hq = k.sb("rhq", [NS, 16], F32); hm = k.sb("rhm", [NS, 16], F32); hr = k.sb("rhr", [NS, 16], F32)
    k.op("dve", lambda e: e.tensor_reduce(out=hs.t[:, :], in_=h3(Y), axis=AX.X, op=ALU.add), reads=[Y], writes=[hs])
    k.op("pool", lambda e: e.tensor_tensor(out=sq.t[:, :], in0=Y.t[:, :], in1=Y.t[:, :], op=ALU.mult), reads=[Y], writes=[sq])
    k.op("dve", lambda e: e.tensor_reduce(out=hq.t[:, :], in_=h3(sq), axis=AX.X, op=ALU.add), reads=[sq], writes=[hq])
    k.op("dve", lambda e: e.tensor_scalar(out=hm.t[:, :], in0=hs.t[:, :], scalar1=1.0 / 512, scalar2=None, op0=ALU.mult), reads=[hs], writes=[hm])
    k.op("dve", lambda e: e.tensor_tensor(out=hs.t[:, :], in0=hm.t[:, :], in1=hm.t[:, :], op=ALU.mult), reads=[hm], writes=[hs])
    k.op("dve", lambda e: e.scalar_tensor_tensor(out=hr.t[:, :], in0=hq.t[:, :], scalar=1.0 / 512, in1=hs.t[:, :], op0=ALU.mult, op1=ALU.subtract), reads=[hq, hs], writes=[hr])
    k.op("dve", lambda e: e.tensor_scalar(out=hr.t[:, :], in0=hr.t[:, :], scalar1=1e-6, scalar2=None, op0=ALU.add), reads=[hr], writes=[hr])
    k.op("act", lambda e: e.activation(out=hr.t[:, :], in_=hr.t[:, :], func=AF.Sqrt), reads=[hr], writes=[hr])
    k.op("dve", lambda e: e.reciprocal(out=hr.t[:, :], in_=hr.t[:, :]), reads=[hr], writes=[hr])
    k.op("dve", lambda e: e.tensor_tensor(out=h3(Y), in0=h3(Y), in1=b16(hm), op=ALU.subtract), reads=[Y, hm], writes=[Y])
    k.op("dve", lambda e: e.tensor_tensor(out=h3(Y), in0=h3(Y), in1=b16(hr), op=ALU.mult), reads=[Y, hr], writes=[Y])
    gnw = bc16(k, "rgnw", prm["ret_gn_w"][0:1, :], 8192)
    k.op("pool", lambda e: e.tensor_tensor(out=Y.t[:, :], in0=Y.t[:, :], in1=gnw.t[:, :], op=ALU.mult), reads=[Y, gnw], writes=[Y])
    k.dma("sp", sq.t[:, :], P1[TP:T, 16384:24576], writes=[sq])
    k.op("act", lambda e: e.activation(out=sq.t[:, :], in_=sq.t[:, :], func=AF.Silu), reads=[sq], writes=[sq])
    y16 = k.sb("ry16", [NS, 8192], BF16)
    k.op("dve", lambda e: e.tensor_tensor(out=y16.t[:, :], in0=Y.t[:, :], in1=sq.t[:, :], op=ALU.mult), reads=[Y, sq], writes=[y16])
    yT = k.sb("ryT", [128, 64, NS], BF16)
    to_fm(k, y16, 0, 32, yT.t[:, 0:32, :], psbf, ident_bf); yT.w = k.last("dve")
    to_fm(k, y16, 4096, 32, yT.t[:, 32:64, :], psbf, ident_bf); yT.w = k.last("dve")
    k.dma("sp", YT[TP // 128, :, 0:64, 0:NS], yT.t[:, :, :], reads=[yT])


USED_INPUTS = []

PARAM_SPECS = [
    ("ssd_conv_w", [4, 6144]), ("ssd_conv_b", [1, 6144]), ("ssd_dt_bias", [1, 64]), ("ssd_a_log", [1, 64]),
    ("ssd_d", [1, 64]), ("ssd_norm_w", [1, 4096]), ("rwkv_mu", [1, 12544]), ("rwkv_w0", [1, 4096]),
    ("rwkv_w_up", [128, 4096]), ("rwkv_a0", [1, 4096]), ("rwkv_a_up", [128, 4096]), ("rwkv_k_k", [1, 4096]),
    ("rwkv_k_a", [1, 4096]), ("rwkv_r_k", [1, 4096]), ("rwkv_lnx_w", [1, 4096]), ("rwkv_lnx_b", [1, 4096]),
    ("ab_ln_w", [1, 4096]), ("ab_ln_b", [1, 4096]), ("ret_gn_w", [1, 8192]), ("ret_ln_w", [1, 4096]),
    ("ret_ln_b", [1, 4096]),
]


def load_consts(k, cin):
    C = {}
    names = ["ident", "tri_le", "tri_gt", "ones"]
    call = k.sb("c_all", [128, 9, 128], F32)
    k.dma("sp", call.t[:, :, :], cin[:, :, :], writes=[call])
    for i, n in enumerate(names):
        C[n] = Buf(call.t[:, i, :], n, call)
    ib = k.sb("ident_bf", [128, 128], BF16)
    k.op("dve", lambda e: e.tensor_copy(out=ib.t[:, :], in_=call.t[:, 0, :]), reads=[call], writes=[ib])
    C["ident_bf"] = ib
    C["all"] = call
    return C


def build_program(dev=None):
    _, plan = _build(dev, None)
    nc, _ = _build(dev, plan)
    return nc


def _build(dev, plan):
    dev = dev or {}
    nc = bass.Bass("TRN2", target_bir_lowering=False)
    specs = dict(IN_SPECS + PARAM_SPECS + [("consts", [128, 9, 128]), ("innerT", [16, 128, 128]), ("qdec", [128, 16]), ("kdec", [128, 16]), ("rope", [T, 256])])

    class LazyIns(dict):
        def __missing__(self, n):
            v = nc.dram_tensor(n, specs[n], F32, kind="ExternalInput").ap()
            self[n] = v
            return v

    ins = LazyIns()
    USED_INPUTS.clear()
    outs = {n: nc.dram_tensor(n, s, F32, kind="ExternalOutput").ap() for n, s in OUT_SPECS}

    def scratch(name, shape, dt):
        kind = "Internal"
        if name in dev.get("as_input", ()):
            kind = "ExternalInput"
        elif name in dev.get("as_output", ()):
            kind = "ExternalOutput"
        return nc.dram_tensor(name, shape, dt, kind=kind).ap()

    P0 = scratch("P0", [T, AB_IN], F32)
    YT0 = scratch("YT0", [TP // 128 + 1, 128, 64, 128], BF16)
    stages = dev.get("stages", ["xT", "gemm0", "raw0", "ssd", "ssd_s", "rwkv", "rwkv_s", "out0", "ln0", "gemm1", "ret", "ret_s", "out1", "ln1"])
    with contextlib.ExitStack() as es:
        k = KB(nc, es, plan)
        C = load_consts(k, ins["consts"])
        if "xT" in stages or "gemm0" in stages:
            with contextlib.ExitStack() as es1:
                k.es = es1
                xT = k.sb("xT", [128, KC, T], BF16)
                with contextlib.ExitStack() as es2:
                    k.es = es2
                    phase_xT(k, ins["xp"], ins["xs"], xT, C["ident"])
                    k.barrier()
                with contextlib.ExitStack() as es2:
                    k.es = es2
                    wbufs = [k.sb(f"wb{i}", [128, KC, 512], BF16) for i in range(2)]
                    psb = [k.ps(f"psg{i}", [128, 512], F32) for i in range(8)]
                    stg = [k.sb(f"stg{i}", [128, 512], F32) for i in range(4)]
                    phase_gemm(k, xT, ins["ab_w_in"], AB_IN, P0, wbufs, psb, stg)
                    k.barrier()
        k.es = es
        if "raw0" in stages:
            k.dma("sp", outs["prompt_conv"][:, :], P0[TP - 3:TP, 4096:10240])
            k.dma("sp", outs["prompt_shift"][:, :], P0[TP - 1:TP, 10304:22848])
            k.dma("sp", outs["sample_shift"][:, :], P0[TP:T, 10304:22848])
            k.dma("sp", outs["sample_conv"][:, 2, :], P0[TP:T, 4096:10240])
            k.dma("sp", outs["sample_conv"][:, 0:2, :], ins["st_conv"][:, 1:3, :])
        if "ssd" in stages:
            with contextlib.ExitStack() as es1:
                k.es = es1
                phase_ssd(k, P0, ins, C, YT0, outs["prompt_ssm"], nchunks=dev.get("nchunks", TP // 128))
                k.barrier()
        SBC = scratch("SBC", [NS, 2048], F32)
        if "ssd_s" in stages:
            with contextlib.ExitStack() as es1:
                k.es = es1
                phase_ssd_s(k, P0, ins, C, ins["st_conv"], ins["st_ssm"], YT0, outs["sample_ssm"], SBC)
                k.barrier()
        if "rwkv" in stages:
            with contextlib.ExitStack() as es1:
                k.es = es1
                phase_rwkv(k, P0, ins, C, YT0, outs["prompt_wkv"], nchunks=dev.get("nchunks", TP // 128), ngroups=dev.get("ngroups", 8))
                k.barrier()
        SV = scratch("SV", [5, NS, 4096], F32)
        if "rwkv_s" in stages:
            with contextlib.ExitStack() as es1:
                k.es = es1
                phase_rwkv_s(k, P0, ins, C, ins["st_shift"], ins["st_wkv"], YT0, outs["sample_wkv"], SV)
                k.barrier()
        H0 = scratch("H0", [T, D], F32)
        X1 = scratch("X1", [T, D], F32)
        if "out0" in stages:
            with contextlib.ExitStack() as es1:
                k.es = es1
                phase_outproj(k, YT0, ins["ab_w_out"], H0)
                k.barrier()
        es_x1 = contextlib.ExitStack()
        x1T = None
        if "ln0" in stages or "gemm1" in stages:
            k.es = es_x1
            x1T = k.sb("x1T", [128, KC, T], BF16)
        if "ln0" in stages:
            if True:
                xin0 = lambda t: ins["xp"][t * 128:(t + 1) * 128, :] if t < TP // 128 else ins["xs"][:, :]
                dst0 = lambda t: X1[t * 128:(t + 1) * 128, :] if t < TP // 128 else X1[TP:T, :]
                with contextlib.ExitStack() as es2:
                    k.es = es2
                    phase_ln(k, H0, xin0, ins["ab_ln_w"][0:1, :], ins["ab_ln_b"][0:1, :], C, dst0, xT=x1T)
                    k.barrier()
        P1 = scratch("P1", [T, RET_IN], F32)
        YT1 = scratch("YT1", [TP // 128 + 1, 128, 64, 128], BF16)
        H1 = scratch("H1", [T, D], F32)
        if "gemm1" in stages:
            with contextlib.ExitStack() as es2:
                k.es = es2
                wbufs = [k.sb(f"wc{i}", [128, KC, 512], BF16) for i in range(2)]
                psb = [k.ps(f"psh{i}", [128, 512], F32) for i in range(8)]
                stg = [k.sb(f"sth{i}", [128, 512], F32) for i in range(4)]
                phase_gemm(k, x1T, ins["ret_w_in"], RET_IN, P1, wbufs, psb, stg)
                k.barrier()
        es_x1.close()
        k.es = es
        if "ret" in stages:
            with contextlib.ExitStack() as es1:
                k.es = es1
                phase_ret(k, P1, ins, C, ins, YT1, outs["prompt_ret"], nchunks=dev.get("nchunks", TP // 128), nheads=dev.get("nheads", 16))
                k.barrier()
        if "ret_s" in stages:
            with contextlib.ExitStack() as es1:
                k.es = es1
                phase_ret_s(k, P1, ins, C, ins, ins["st_ret"], YT1, outs["sample_ret"])
                k.barrier()
        if "out1" in stages:
            with contextlib.ExitStack() as es1:
                k.es = es1
                phase_outproj(k, YT1, ins["ret_w_out"], H1)
                k.barrier()
        if "ln1" in stages:
            with contextlib.ExitStack() as es1:
                k.es = es1
                xin1 = lambda t: X1[t * 128:(t + 1) * 128, :] if t < TP // 128 else X1[TP:T, :]
                dst1 = lambda t: outs["y_prompt"][t * 128:(t + 1) * 128, :] if t < TP // 128 else outs["y_sample"][:, :]
                phase_ln(k, H1, xin1, ins["ret_ln_w"][0:1, :], ins["ret_ln_b"][0:1, :], C, dst1, xT=None)
                k.barrier()
        k.es = es
        k.finish()
        print("instructions emitted:", k.ninstr, "sems:", k.nsem, "compute incs:", k.ninc)
        needed = set(k.needed)
    USED_INPUTS.extend(ins.keys())
    return nc, needed


def make_consts():
    j = np.arange(128)
    c = np.zeros((128, 9, 128), np.float32)
    le = (j[:, None] <= j[None, :]).astype(np.float32)
    lt = (j[:, None] < j[None, :]).astype(np.float32)
    c[:, 0, :] = np.eye(128, dtype=np.float32)
    c[:, 1, :] = le
    c[:, 2, :] = (j[:, None] > j[None, :]).astype(np.float32)
    c[:, 3, :] = 1.0
    c[:, 4, :] = lt
    c[:, 5, :] = lt
    c[:, 6, :] = le
    c[:, 7, :] = lt
    c[:, 8, :] = le
    return c


def make_in_map(inputs, c, shared=None):
    f = lambda a: np.ascontiguousarray(np.asarray(a, dtype=np.float32))
    if shared is None:
        shared = {}
        for n, shp in PARAM_SPECS:
            shared[n] = f(inputs[n][0]).reshape(shp)
        for n in ("ab_w_in", "ab_w_out", "ret_w_in", "ret_w_out"):
            shared[n] = f(inputs[n][0])
        shared["consts"] = make_consts()
        shared.update(make_ret_consts())
        shared["ident"] = np.eye(128, dtype=np.float32)
    b = c // 2
    sl = slice(c * NS, (c + 1) * NS)
    m = dict(shared)
    m["xp"] = f(inputs["x_prompt"][b])
    m["xs"] = f(inputs["x_sample"][sl, 0, :])
    m["st_conv"] = f(inputs["state_conv"][0, sl])
    m["st_ssm"] = f(inputs["state_ssm"][0, sl]).reshape(NS, 4096, 128)
    m["st_shift"] = f(inputs["state_shift"][0, sl, 0, :])
    m["st_wkv"] = f(inputs["state_wkv"][0, sl]).reshape(NS, 4096, 64)
    m["st_ret"] = f(inputs["state_ret"][0, sl]).reshape(NS, 4096, 512)
    return m, shared


_NC_CACHE = {}


def kernel(**inputs):
    if "nc" not in _NC_CACHE:
        _NC_CACHE["nc"] = build_program()
    nc = _NC_CACHE["nc"]
    in_maps = []
    shared = None
    for c in range(NCORES):
        m, shared = make_in_map(inputs, c, shared)
        in_maps.append({n: m[n] for n in USED_INPUTS})
    res = run_bass_kernel_spmd(nc, in_maps, core_ids=list(range(NCORES)))
    R = res.results
    pc = lambda name: [R[2 * b][name] for b in range(4)]
    sc = lambda name: [R[c][name] for c in range(NCORES)]
    y_prompt = np.stack(pc("y_prompt")).reshape(4, TP, D)
    y_sample = np.concatenate(sc("y_sample")).reshape(128, 1, D)
    prompt_conv = np.stack(pc("prompt_conv")).reshape(1, 4, 3, 6144)
    prompt_ssm = np.stack(pc("prompt_ssm")).reshape(1, 4, 64, 64, 128)
    prompt_shift = np.stack(pc("prompt_shift")).reshape(1, 4, 1, 12544)
    prompt_wkv = np.stack(pc("prompt_wkv")).reshape(1, 4, 64, 64, 64)
    prompt_ret = np.stack(pc("prompt_ret")).reshape(1, 4, 16, 256, 512)
    sample_conv = np.concatenate(sc("sample_conv")).reshape(1, 128, 3, 6144)
    sample_ssm = np.concatenate(sc("sample_ssm")).reshape(1, 128, 64, 64, 128)
    sample_shift = np.concatenate(sc("sample_shift")).reshape(1, 128, 1, 12544)
    sample_wkv = np.concatenate(sc("sample_wkv")).reshape(1, 128, 64, 64, 64)
    sample_ret = np.concatenate(sc("sample_ret")).reshape(1, 128, 16, 256, 512)
    return tuple(np.ascontiguousarray(a, dtype=np.float32) for a in (
        y_prompt, y_sample, prompt_conv, prompt_ssm, prompt_shift, prompt_wkv, prompt_ret,
        sample_conv, sample_ssm, sample_shift, sample_wkv, sample_ret))
```

```python
import contextlib
import numpy as np
import concourse.bass as bass
import concourse.mybir as mybir
from concourse.bass_utils import run_bass_kernel_spmd

F32 = mybir.dt.float32
BF16 = mybir.dt.bfloat16
AF = mybir.ActivationFunctionType
ALU = mybir.AluOpType
AX = mybir.AxisListType

D = 4096
TP = 2048
NS = 16
T = TP + NS
KC = D // 128
AB_IN = 26944
RET_IN = 24576
NCORES = 8
SEM_LIMIT = 60000


class Buf:
    def __init__(self, t, name="", parent=None, excl=False):
        self.t = t
        self.name = name
        self.parent = parent
        self.excl = excl or (parent is not None and parent.excl)
        self._w = None
        self._r = {}

    @property
    def w(self):
        return self.parent.w if self.parent is not None else self._w

    @w.setter
    def w(self, v):
        if self.parent is not None:
            self.parent.w = v
        else:
            self._w = v

    @property
    def r(self):
        return self.parent.r if self.parent is not None else self._r

    @r.setter
    def r(self, v):
        if self.parent is not None:
            self.parent.r = v
        else:
            self._r = v

    def __getitem__(self, idx):
        return self.t[idx]


class KB:
    def __init__(self, nc, es, plan=None):
        self.nc = nc
        self.es = es
        self.sem_es = es
        self.plan = plan
        self.needed = set()
        self.engs = {"pe": nc.tensor, "dve": nc.vector, "act": nc.scalar,
                     "pool": nc.gpsimd, "sp": nc.sync}
        self.csem = {}
        self.shadow = {k: 0 for k in self.engs}
        self.real = {}
        self.waited = {k: {} for k in self.engs}
        self.dsem = {}
        self.dnext = {}
        self.nsem = 0
        self.all_sems = []
        for k in self.engs:
            self.csem[k] = [self._newsem(), 0]
        for k in ("sp", "act", "pool"):
            self.dsem[k] = [[self._newsem(), 0] for _ in range(8)]
            self.dnext[k] = 0
        self.ninstr = 0
        self.ninc = 0

    def _newsem(self):
        s = self.sem_es.enter_context(self.nc.semaphore(f"s{self.nsem}"))
        self.nsem += 1
        self.all_sems.append(s)
        return s

    def _uniq(self, name):
        self.nalloc = getattr(self, "nalloc", 0) + 1
        return f"{name}_{self.nalloc}"

    def sb(self, name, shape, dt):
        name = self._uniq(name)
        return Buf(self.es.enter_context(self.nc.sbuf_tensor(name, shape, dt)), name)

    def ps(self, name, shape, dt=F32):
        name = self._uniq(name)
        return Buf(self.es.enter_context(self.nc.psum_tensor(name, shape, dt)), name, excl=True)

    def last(self, en):
        return ("c", en, self.shadow[en])

    @staticmethod
    def _kv(ev):
        if ev[0] == "c":
            return ("c", ev[1]), ev[2]
        return ev[0], ev[1]

    def _wait(self, en, evs):
        need = {}
        for ev in evs:
            if ev is None:
                continue
            key, v = self._kv(ev)
            if v > need.get(key, 0):
                need[key] = v
        wd = self.waited[en]
        for key, v in need.items():
            if wd.get(key, 0) >= v:
                continue
            if isinstance(key, tuple):
                self.needed.add((key[1], v))
                s, rv = self.real[(key[1], v)]
            else:
                s, rv = key, v
            self.engs[en].wait_ge(s, rv)
            wd[key] = v
            self.ninstr += 1

    def _deps(self, en, reads, writes):
        evs = []
        own = ("c", en)
        for b in reads:
            evs.append(b.w)
            if b.excl:
                for key, v in b.r.items():
                    if key != own:
                        evs.append(key + (v,) if isinstance(key, tuple) else (key, v))
        for b in writes:
            evs.append(b.w)
            for key, v in b.r.items():
                if key == own:
                    continue
                evs.append(key + (v,) if isinstance(key, tuple) else (key, v))
        return evs

    def _record(self, ev, reads, writes):
        key, v = self._kv(ev)
        for b in reads:
            if v > b.r.get(key, 0):
                b.r[key] = v
        for b in writes:
            b.w = ev
            b.r = {}

    def _cinc(self, en, ins):
        self.shadow[en] += 1
        n = self.shadow[en]
        if self.plan is None or (en, n) in self.plan:
            cs = self.csem[en]
            if cs[1] >= SEM_LIMIT:
                cs[0] = self._newsem()
                cs[1] = 0
            cs[1] += 1
            ins.then_inc(cs[0], 1)
            self.real[(en, n)] = (cs[0], cs[1])
            self.ninc += 1
        return ("c", en, n)

    def op(self, en, fn, reads=(), writes=()):
        self._wait(en, self._deps(en, reads, writes))
        ins = fn(self.engs[en])
        ev = self._cinc(en, ins)
        self._record(ev, reads, writes)
        self.ninstr += 1
        return ev

    def mm(self, out_buf, out_ap, pairs, reads, transpose=False):
        return self.mm_multi(out_buf, [(out_ap, pairs)], reads, transpose)

    def mm_multi(self, out_buf, items, reads, transpose=False):
        self._wait("pe", self._deps("pe", reads, [out_buf]))
        ins = None
        for out_ap, pairs in items:
            n = len(pairs)
            for i, (l, r) in enumerate(pairs):
                if transpose:
                    ins = self.nc.tensor.transpose(out_ap, l, r)
                else:
                    ins = self.nc.tensor.matmul(out_ap, l, r, start=(i == 0), stop=(i == n - 1))
                self.ninstr += 1
        ev = self._cinc("pe", ins)
        self._record(ev, reads, [out_buf])
        return ev

    def dma(self, en, out_ap, in_ap, reads=(), writes=()):
        ring = self.dsem[en]
        i = self.dnext[en]
        self.dnext[en] = (i + 1) % len(ring)
        slot = ring[i]
        if slot[1] + 16 > SEM_LIMIT:
            old = (slot[0], slot[1])
            slot[0] = self._newsem()
            slot[1] = 0
            self._wait(en, [old])
        evs = self._deps(en, reads, writes)
        if slot[1] > 0:
            evs.append((slot[0], slot[1]))
        self._wait(en, evs)
        ins = self.engs[en].dma_start(out=out_ap, in_=in_ap)
        slot[1] += 16
        ins.then_inc(slot[0], 16)
        ev = (slot[0], slot[1])
        self._record(ev, reads, writes)
        self.ninstr += 1
        return ev

    def barrier(self):
        evs = []
        for en in self.engs:
            if self.shadow[en] > 0:
                evs.append(("c", en, self.shadow[en]))
        for k, ring in self.dsem.items():
            for slot in ring:
                if slot[1] > 0:
                    evs.append((slot[0], slot[1]))
        for en in self.engs:
            self._wait(en, evs)

    def finish(self):
        self.barrier()


OUT_SPECS = [
    ("y_prompt", [TP, D]),
    ("y_sample", [NS, D]),
    ("prompt_conv", [3, 6144]),
    ("prompt_ssm", [64 * 64, 128]),
    ("prompt_shift", [1, 12544]),
    ("prompt_wkv", [64 * 64, 64]),
    ("prompt_ret", [16 * 256, 512]),
    ("sample_conv", [NS, 3, 6144]),
    ("sample_ssm", [NS, 64 * 64, 128]),
    ("sample_shift", [NS, 12544]),
    ("sample_wkv", [NS, 64 * 64, 64]),
    ("sample_ret", [NS, 16 * 256, 512]),
]

IN_SPECS = [
    ("xp", [TP, D]), ("xs", [NS, D]),
    ("st_conv", [NS, 3, 6144]), ("st_ssm", [NS, 4096, 128]), ("st_shift", [NS, 12544]),
    ("st_wkv", [NS, 4096, 64]), ("st_ret", [NS, 4096, 512]),
    ("ab_w_in", [D, AB_IN]), ("ab_w_out", [8192, D]),
    ("ret_w_in", [D, RET_IN]), ("ret_w_out", [8192, D]),
    ("ident", [128, 128]),
]


def phase_xT(k, x_p, x_s, xT, ident):
    nc = k.nc
    xin = [k.sb(f"xin{i}", [128, D], F32) for i in range(2)]
    pst = [k.ps(f"pst{i}", [128, 512], F32) for i in range(2)]
    ntile = TP // 128 + 1
    q = 0
    for t in range(ntile):
        m = 128 if t < TP // 128 else NS
        xb = xin[t % 2]
        src = x_p[t * 128:(t + 1) * 128, :] if t < TP // 128 else x_s[:, :]
        k.dma("sp", xb.t[:m, :], src, writes=[xb])
        for g in range(KC // 4):
            pb = pst[q % 2]
            items = []
            for j in range(4):
                kc = g * 4 + j
                items.append((pb.t[:, j * 128:j * 128 + m],
                              [(xb.t[:m, kc * 128:(kc + 1) * 128], ident.t[:m, :m])]))
            k.mm_multi(pb, items, reads=[xb, ident], transpose=True)
            src_ap = pb.t[:, :].rearrange("p (j m) -> p j m", j=4)[:, :, :m]
            dst_ap = xT.t[:, g * 4:(g + 1) * 4, t * 128:t * 128 + m]
            en = "dve" if q % 2 == 0 else "act"
            if en == "dve":
                k.op("dve", lambda e: e.tensor_copy(out=dst_ap, in_=src_ap), reads=[pb], writes=[xT])
            else:
                k.op("act", lambda e: e.copy(out=dst_ap, in_=src_ap), reads=[pb], writes=[xT])
            q += 1


def phase_gemm(k, xT, W, ncols, P, wbufs, psb, stg):
    Wv = W.rearrange("(kc p) n -> p kc n", p=128)
    ngrp = (ncols + 511) // 512
    ntile = TP // 128 + 1
    q = 0

    def load(g):
        c0 = g * 512
        n = min(512, ncols - c0)
        wb = wbufs[g % 2]
        for h in range(4):
            k.dma("pool", wb.t[:, h * 8:(h + 1) * 8, :n], Wv[:, h * 8:(h + 1) * 8, c0:c0 + n], writes=[wb])

    load(0)
    for g in range(ngrp):
        if g + 1 < ngrp:
            load(g + 1)
        c0 = g * 512
        n = min(512, ncols - c0)
        wb = wbufs[g % 2]
        for t in range(ntile):
            m = 128 if t < TP // 128 else NS
            pb = psb[q % len(psb)]
            pairs = [(xT.t[:, kc, t * 128:t * 128 + m], wb.t[:, kc, :n]) for kc in range(KC)]
            k.mm(pb, pb.t[:m, :n], pairs, reads=[xT, wb])
            sg = stg[q % len(stg)]
            if q % 2 == 0:
                k.op("dve", lambda e: e.tensor_copy(out=sg.t[:m, :n], in_=pb.t[:m, :n]), reads=[pb], writes=[sg])
            else:
                k.op("act", lambda e: e.copy(out=sg.t[:m, :n], in_=pb.t[:m, :n]), reads=[pb], writes=[sg])
            k.dma("sp", P[t * 128:t * 128 + m, c0:c0 + n], sg.t[:m, :n], reads=[sg])
            q += 1


def load_bc(k, name, src_row_ap, n, dt=F32, en="sp"):
    t = k.sb(name, [128, n], dt)
    k.dma(en, t.t[:, :], src_row_ap.to_broadcast([128, n]), writes=[t])
    return t


def phase_ssd(k, P0, prm, C, YT, out_ssm, nchunks=TP // 128):
    ident, tri_le, tri_gt, ones, ident_bf = C["ident"], C["tri_le"], C["tri_gt"], C["ones"], C["ident_bf"]
    a_bc = load_bc(k, "a_bc", prm["ssd_a_log"][0:1, :], 64)
    k.op("act", lambda e: e.activation(out=a_bc.t[:, :], in_=a_bc.t[:, :], func=AF.Exp), reads=[a_bc], writes=[a_bc])
    k.op("dve", lambda e: e.tensor_scalar(out=a_bc.t[:, :], in0=a_bc.t[:, :], scalar1=-1.0, scalar2=None, op0=ALU.mult),
         reads=[a_bc], writes=[a_bc])
    d_bc = load_bc(k, "d_bc", prm["ssd_d"][0:1, :], 64)
    dtb_bc = load_bc(k, "dtb_bc", prm["ssd_dt_bias"][0:1, :], 64)
    cw = k.sb("cw", [128, 4, 768], F32)
    cb = k.sb("cb", [128, 768], F32)
    nw = k.sb("nw", [128, 512], F32)
    S = k.sb("S", [128, 512], F32)
    Sb = k.sb("Sb", [128, 512], BF16)
    yTs = k.sb("yTs", [128, 4, TP], BF16)
    U = [k.sb(f"U{i}", [128, 4, 768], F32) for i in range(2)]
    zt = [k.sb(f"zt{i}", [128, 512], F32) for i in range(2)]
    dtr = [k.sb(f"dtr{i}", [128, 8], F32) for i in range(2)]
    def db(name, shape, dt):
        return [k.sb(f"{name}{i}", shape, dt) for i in range(2)]
    DB = dict(
        a1=db("a1", [128, 768], F32), t1=db("t1c", [128, 768], F32), a2=db("a2", [128, 768], F32), t2=db("t2c", [128, 768], F32),
        xa=db("xa", [128, 768], F32), bc16=db("bc16", [128, 256], BF16), bcT=db("bcT", [128, 256], BF16),
        dt=db("dt", [128, 8], F32), adt=db("adt", [128, 8], F32), dtt=db("dtt", [128, 8], F32),
        ein=db("ein", [128, 24], F32), st=db("st", [128, 24], F32), rhsA=db("rhsA", [128, 8, 128], F32),
        eseg=db("eseg", [128, 8, 128], F32), cbm=db("cbm", [128, 128], F32), M=db("M", [128, 8, 128], BF16),
        xdt=db("xdt", [128, 8, 64], BF16), xdtt=db("xdtt", [128, 8, 64], BF16), yc=db("yc", [128, 512], F32),
        y3=db("y3", [128, 512], F32), sz=db("sz", [128, 512], F32), junk=db("junk", [128, 512], F32),
        ss=db("ss", [128, 1], F32), rstd=db("rstd", [128, 1], F32), ya=db("ya", [128, 512], BF16))
    sto = k.sb("sto", [128, 512], F32)
    psSeg = [k.ps(f"psSeg{i}", [128, 512], F32) for i in range(2)]
    psY = k.ps("psY", [128, 512], F32)
    psI = k.ps("psI", [128, 512], F32)
    psS = k.ps("psS", [128, 512], F32)
    psM = k.ps("psM", [128, 512], F32)
    psT = k.ps("psT", [128, 1024], BF16)
    XB = 4096
    u = 0
    for g in range(8):
        cols = [(XB + g * 512, 512, 0), (XB + 4096 + g * 128, 128, 512), (XB + 5120 + g * 128, 128, 640)]
        for (c0, n, o) in cols:
            cwsrc = prm["ssd_conv_w"][:, c0 - XB:c0 - XB + n]
            for kk in range(4):
                k.dma("act", cw.t[:, kk, o:o + n], cwsrc[kk:kk + 1, :].to_broadcast([128, n]), writes=[cw])
            k.dma("act", cb.t[:, o:o + n], prm["ssd_conv_b"][0:1, c0 - XB:c0 - XB + n].to_broadcast([128, n]), writes=[cb])
        k.dma("act", nw.t[:, :], prm["ssd_norm_w"][0:1, g * 512:(g + 1) * 512].to_broadcast([128, 512]), writes=[nw])
        k.op("pool", lambda e: e.memset(S.t[:, :], 0.0), writes=[S])
        k.op("pool", lambda e: e.memset(Sb.t[:, :], 0.0), writes=[Sb])
        for c in range(nchunks):
            t0 = c * 128
            Ub = U[u % 2]; zb = zt[u % 2]; db = dtr[u % 2]
            (a1, t1, a2, t2, xa, bc16, bcT, dt, adt, dtt, ein, st, rhsA, eseg, cbm, M, xdt, xdtt, yc, y3, sz, junk, ss, rstd, ya) = [
                DB[n][u % 2] for n in ("a1", "t1", "a2", "t2", "xa", "bc16", "bcT", "dt", "adt", "dtt", "ein", "st", "rhsA", "eseg", "cbm", "M",
                                       "xdt", "xdtt", "yc", "y3", "sz", "junk", "ss", "rstd", "ya")]
            u += 1
            if c == 0:
                k.op("pool", lambda e: e.memset(Ub.t[:, :, :], 0.0), writes=[Ub])
            for kk in range(4):
                r0 = t0 - 3 + kk
                p0 = max(0, -r0)
                for (c0, n, o) in cols:
                    k.dma("sp", Ub.t[p0:128, kk, o:o + n], P0[r0 + p0:r0 + 128, c0:c0 + n], writes=[Ub])
            k.dma("act", zb.t[:, :], P0[t0:t0 + 128, g * 512:(g + 1) * 512], writes=[zb])
            k.dma("act", db.t[:, :], P0[t0:t0 + 128, 10240 + g * 8:10240 + g * 8 + 8], writes=[db])
            k.op("dve", lambda e: e.tensor_tensor(out=a1.t[:, :], in0=Ub.t[:, 0, :], in1=cw.t[:, 0, :], op=ALU.mult), reads=[Ub, cw], writes=[a1])
            k.op("dve", lambda e: e.tensor_tensor(out=t1.t[:, :], in0=Ub.t[:, 1, :], in1=cw.t[:, 1, :], op=ALU.mult), reads=[Ub, cw], writes=[t1])
            k.op("pool", lambda e: e.tensor_tensor(out=a2.t[:, :], in0=Ub.t[:, 2, :], in1=cw.t[:, 2, :], op=ALU.mult), reads=[Ub, cw], writes=[a2])
            k.op("pool", lambda e: e.tensor_tensor(out=t2.t[:, :], in0=Ub.t[:, 3, :], in1=cw.t[:, 3, :], op=ALU.mult), reads=[Ub, cw], writes=[t2])
            k.op("pool", lambda e: e.tensor_tensor(out=a2.t[:, :], in0=a2.t[:, :], in1=t2.t[:, :], op=ALU.add), reads=[a2, t2], writes=[a2])
            k.op("pool", lambda e: e.tensor_tensor(out=a2.t[:, :], in0=a2.t[:, :], in1=cb.t[:, :], op=ALU.add), reads=[a2, cb], writes=[a2])
            k.op("dve", lambda e: e.tensor_tensor(out=a1.t[:, :], in0=a1.t[:, :], in1=t1.t[:, :], op=ALU.add), reads=[a1, t1], writes=[a1])
            k.op("dve", lambda e: e.tensor_tensor(out=a1.t[:, :], in0=a1.t[:, :], in1=a2.t[:, :], op=ALU.add), reads=[a1, a2], writes=[a1])
            k.op("act", lambda e: e.activation(out=xa.t[:, :], in_=a1.t[:, :], func=AF.Silu), reads=[a1], writes=[xa])
            k.op("dve", lambda e: e.tensor_tensor(out=dt.t[:, :], in0=db.t[:, :], in1=dtb_bc.t[:, g * 8:g * 8 + 8], op=ALU.add), reads=[db, dtb_bc], writes=[dt])
            k.op("act", lambda e: e.activation(out=dt.t[:, :], in_=dt.t[:, :], func=AF.Exp), reads=[dt], writes=[dt])
            k.op("act", lambda e: e.activation(out=dt.t[:, :], in_=dt.t[:, :], func=AF.Ln, bias=1.0), reads=[dt], writes=[dt])
            k.op("dve", lambda e: e.tensor_tensor(out=adt.t[:, :], in0=dt.t[:, :], in1=a_bc.t[:, g * 8:g * 8 + 8], op=ALU.mult), reads=[dt, a_bc], writes=[adt])
            k.op("dve", lambda e: e.tensor_tensor(out=rhsA.t[:, :, :], in0=tri_le.t[:, :].unsqueeze(1).to_broadcast([128, 8, 128]),
                                                  in1=adt.t[:, :].unsqueeze(2).to_broadcast([128, 8, 128]), op=ALU.mult),
                 reads=[tri_le, adt], writes=[rhsA])
            for hh in range(2):
                k.mm(psSeg[hh], psSeg[hh].t[:, :], [(tri_gt.t[:, :], rhsA.t[:, hh * 4:(hh + 1) * 4, :].rearrange("p h l -> p (h l)"))], reads=[tri_gt, rhsA])
            k.mm_multi(psM, [(psM.t[:, 128:136], [(tri_le.t[:, :], adt.t[:, :])]),
                             (psM.t[:, 136:144], [(ones.t[:, :], adt.t[:, :])])], reads=[tri_le, ones, adt])
            k.op("dve", lambda e: e.tensor_copy(out=ein.t[:, 0:16], in_=psM.t[:, 128:144]), reads=[psM], writes=[ein])
            k.op("dve", lambda e: e.tensor_tensor(out=ein.t[:, 16:24], in0=ein.t[:, 8:16], in1=ein.t[:, 0:8], op=ALU.subtract), reads=[ein], writes=[ein])
            k.op("act", lambda e: e.activation(out=st.t[:, :], in_=ein.t[:, :], func=AF.Exp), reads=[ein], writes=[st])
            k.op("dve", lambda e: e.tensor_tensor(out=dtt.t[:, :], in0=dt.t[:, :], in1=st.t[:, 16:24], op=ALU.mult), reads=[dt, st], writes=[dtt])
            k.op("act", lambda e: e.copy(out=bc16.t[:, :], in_=xa.t[:, 512:768]), reads=[xa], writes=[bc16])
            k.mm_multi(psT, [(psT.t[:, 0:128], [(bc16.t[:, 0:128], ident_bf.t[:, :])]),
                             (psT.t[:, 128:256], [(bc16.t[:, 128:256], ident_bf.t[:, :])])], reads=[bc16, ident_bf], transpose=True)
            k.op("act", lambda e: e.copy(out=bcT.t[:, :], in_=psT.t[:, 0:256]), reads=[psT], writes=[bcT])
            k.mm(psM, psM.t[:, 0:128], [(bcT.t[:, 0:128], bcT.t[:, 128:256])], reads=[bcT])
            k.op("dve", lambda e: e.tensor_tensor(out=cbm.t[:, :], in0=psM.t[:, 0:128], in1=tri_le.t[:, :], op=ALU.mult), reads=[psM, tri_le], writes=[cbm])
            for hh in range(2):
                k.op("act", lambda e: e.activation(out=eseg.t[:, hh * 4:(hh + 1) * 4, :].rearrange("p h l -> p (h l)"), in_=psSeg[hh].t[:, :], func=AF.Exp),
                     reads=[psSeg[hh]], writes=[eseg])
            k.op("dve", lambda e: e.tensor_tensor(out=M.t[:, :, :], in0=eseg.t[:, :, :], in1=cbm.t[:, :].unsqueeze(1).to_broadcast([128, 8, 128]), op=ALU.mult),
                 reads=[eseg, cbm], writes=[M])
            xs3 = xa.t[:, 0:512].rearrange("p (h d) -> p h d", h=8)
            k.op("dve", lambda e: e.tensor_tensor(out=xdt.t[:, :, :], in0=xs3, in1=dt.t[:, :].unsqueeze(2).to_broadcast([128, 8, 64]), op=ALU.mult),
                 reads=[xa, dt], writes=[xdt])
            k.op("pool", lambda e: e.tensor_tensor(out=xdtt.t[:, :, :], in0=xs3, in1=dtt.t[:, :].unsqueeze(2).to_broadcast([128, 8, 64]), op=ALU.mult),
                 reads=[xa, dtt], writes=[xdtt])
            k.mm_multi(psY, [(psY.t[:, h * 64:(h + 1) * 64], [(M.t[:, h, :], xdt.t[:, h, :])]) for h in range(8)], reads=[M, xdt])
            k.mm(psI, psI.t[:, :], [(bcT.t[:, 128:256], Sb.t[:, :])], reads=[bcT, Sb])
            k.op("dve", lambda e: e.tensor_tensor(out=yc.t[:, :].rearrange("p (h d) -> p h d", h=8), in0=psI.t[:, :].rearrange("p (h d) -> p h d", h=8),
                                                  in1=st.t[:, 0:8].unsqueeze(2).to_broadcast([128, 8, 64]), op=ALU.mult), reads=[psI, st], writes=[yc])
            k.op("dve", lambda e: e.tensor_tensor(out=yc.t[:, :], in0=yc.t[:, :], in1=psY.t[:, :], op=ALU.add), reads=[yc, psY], writes=[yc])
            k.op("pool", lambda e: e.tensor_tensor(out=y3.t[:, :].rearrange("p (h d) -> p h d", h=8), in0=xs3,
                                                   in1=d_bc.t[:, g * 8:g * 8 + 8].unsqueeze(2).to_broadcast([128, 8, 64]), op=ALU.mult), reads=[xa, d_bc], writes=[y3])
            k.op("dve", lambda e: e.tensor_tensor(out=yc.t[:, :], in0=yc.t[:, :], in1=y3.t[:, :], op=ALU.add), reads=[yc, y3], writes=[yc])
            k.op("act", lambda e: e.activation(out=sz.t[:, :], in_=zb.t[:, :], func=AF.Silu), reads=[zb], writes=[sz])
            k.op("dve", lambda e: e.tensor_tensor(out=yc.t[:, :], in0=yc.t[:, :], in1=sz.t[:, :], op=ALU.mult), reads=[yc, sz], writes=[yc])
            k.op("act", lambda e: e.activation(out=junk.t[:, :], in_=yc.t[:, :], func=AF.Square, accum_out=ss.t[:, :]), reads=[yc], writes=[junk, ss])
            k.op("dve", lambda e: e.tensor_scalar(out=rstd.t[:, :], in0=ss.t[:, :], scalar1=1.0 / 512, scalar2=1e-5, op0=ALU.mult, op1=ALU.add), reads=[ss], writes=[rstd])
            k.op("act", lambda e: e.activation(out=rstd.t[:, :], in_=rstd.t[:, :], func=AF.Sqrt), reads=[rstd], writes=[rstd])
            k.op("dve", lambda e: e.reciprocal(out=rstd.t[:, :], in_=rstd.t[:, :]), reads=[rstd], writes=[rstd])
            k.op("dve", lambda e: e.scalar_tensor_tensor(out=ya.t[:, :], in0=yc.t[:, :], scalar=rstd.t[:, 0:1], in1=nw.t[:, :], op0=ALU.mult, op1=ALU.mult),
                 reads=[yc, rstd, nw], writes=[ya])
            k.mm_multi(psT, [(psT.t[:, 256 + j * 128:256 + (j + 1) * 128], [(ya.t[:, j * 128:(j + 1) * 128], ident_bf.t[:, :])]) for j in range(4)],
                       reads=[ya, ident_bf], transpose=True)
            k.op("act", lambda e: e.copy(out=yTs.t[:, :, t0:t0 + 128], in_=psT.t[:, 256:768].rearrange("p (j t) -> p j t", j=4)), reads=[psT], writes=[yTs])
            k.mm(psS, psS.t[:, :], [(bc16.t[:, 0:128], xdtt.t[:, :, :].rearrange("p h d -> p (h d)"))], reads=[bc16, xdtt])
            k.op("dve", lambda e: e.tensor_tensor(out=S.t[:, :].rearrange("p (h d) -> p h d", h=8), in0=S.t[:, :].rearrange("p (h d) -> p h d", h=8),
                                                  in1=st.t[:, 8:16].unsqueeze(2).to_broadcast([128, 8, 64]), op=ALU.mult), reads=[S, st], writes=[S])
            k.op("dve", lambda e: e.tensor_tensor(out=S.t[:, :], in0=S.t[:, :], in1=psS.t[:, :], op=ALU.add), reads=[S, psS], writes=[S])
            k.op("act", lambda e: e.copy(out=Sb.t[:, :], in_=S.t[:, :]), reads=[S], writes=[Sb])
        for j in range(4):
            k.dma("sp", YT[0:nchunks, :, g * 4 + j, :].rearrange("t p k -> p t k"), yTs.t[:, j, 0:nchunks * 128].rearrange("p (t k) -> p t k", k=128), reads=[yTs])
        k.mm_multi(psI, [(psI.t[:, j * 128:(j + 1) * 128], [(S.t[:, j * 128:(j + 1) * 128], ident.t[:, :])]) for j in range(4)], reads=[S, ident], transpose=True)
        k.op("dve", lambda e: e.tensor_copy(out=sto.t[:, :], in_=psI.t[:, :]), reads=[psI], writes=[sto])
        k.dma("sp", out_ssm[g * 512:(g + 1) * 512, :].rearrange("(j p) n -> p j n", p=128), sto.t[:, :].rearrange("p (j n) -> p j n", j=4), reads=[sto])


PB = 10304
GB = 22848
C1 = 0.6065306597126334


def phase_rwkv(k, P0, prm, C, YT, out_wkv, nchunks=TP // 128, ngroups=8):
    ident, tri_le, tri_gt, ones, ident_bf = C["ident"], C["tri_le"], C["tri_gt"], C["ones"], C["ident_bf"]
    call = C["all"]
    maskA = Buf(call.t[:, 5:9, :].rearrange("p a b -> p (a b)"), "maskA", call)
    e127 = Buf(call.t[:, 0, 127:128], "e127", call)
    loraT = k.sb("loraT", [128, 2, TP], BF16)
    mu_wa = load_bc(k, "mu_wa", prm["rwkv_mu"][0:1, 12288:12544], 256)
    pw = [k.sb(f"pw{i}", [128, 256], F32) for i in range(2)]
    pwp = [k.sb(f"pwp{i}", [128, 256], F32) for i in range(2)]
    dd = k.sb("dd", [128, 256], F32)
    lw16 = k.sb("lw16", [128, 256], BF16)
    psTr = k.ps("psTr", [128, 1024], BF16)
    for c in range(nchunks):
        t0 = c * 128
        a = pw[c % 2]; b = pwp[c % 2]
        k.dma("sp", a.t[:, :], P0[t0:t0 + 128, PB + 12288:PB + 12544], writes=[a])
        if c == 0:
            k.op("pool", lambda e: e.memset(b.t[:, :], 0.0), writes=[b])
            k.dma("sp", b.t[1:128, :], P0[0:127, PB + 12288:PB + 12544], writes=[b])
        else:
            k.dma("sp", b.t[:, :], P0[t0 - 1:t0 + 127, PB + 12288:PB + 12544], writes=[b])
        k.op("dve", lambda e: e.tensor_tensor(out=dd.t[:, :], in0=b.t[:, :], in1=a.t[:, :], op=ALU.subtract), reads=[a, b], writes=[dd])
        k.op("dve", lambda e: e.tensor_tensor(out=dd.t[:, :], in0=dd.t[:, :], in1=mu_wa.t[:, :], op=ALU.mult), reads=[dd, mu_wa], writes=[dd])
        k.op("dve", lambda e: e.tensor_tensor(out=dd.t[:, :], in0=dd.t[:, :], in1=a.t[:, :], op=ALU.add), reads=[dd, a], writes=[dd])
        k.op("act", lambda e: e.activation(out=lw16.t[:, 0:128], in_=dd.t[:, 0:128], func=AF.Tanh), reads=[dd], writes=[lw16])
        k.op("act", lambda e: e.copy(out=lw16.t[:, 128:256], in_=dd.t[:, 128:256]), reads=[dd], writes=[lw16])
        k.mm_multi(psTr, [(psTr.t[:, j * 128:(j + 1) * 128], [(lw16.t[:, j * 128:(j + 1) * 128], ident_bf.t[:, :])]) for j in range(2)],
                   reads=[lw16, ident_bf], transpose=True)
        k.op("dve", lambda e: e.tensor_copy(out=loraT.t[:, :, t0:t0 + 128], in_=psTr.t[:, 0:256].rearrange("p (j t) -> p j t", j=2)), reads=[psTr], writes=[loraT])
    def sbf(name, dt=F32, n=512):
        return k.sb(name, [128, n], dt)
    mu_r = sbf("mu_r"); mu_k = sbf("mu_k"); mu_v = sbf("mu_v"); w0 = sbf("w0"); a0 = sbf("a0")
    k_k = sbf("k_k"); k_a = sbf("k_a"); r_k = sbf("r_k"); lnw = sbf("lnw"); lnb = sbf("lnb")
    wup = sbf("wup", BF16); aup = sbf("aup", BF16)
    ST = k.sb("ST", [64, 8, 64], F32); STb = k.sb("STb", [64, 8, 64], BF16)
    yTs = k.sb("yTsr", [128, 4, TP], BF16)
    ld = [[sbf(f"ld{n}{i}") for n in ("r", "k", "v", "rp", "kp", "vp", "g")] for i in range(2)]
    r = sbf("r_"); kt = sbf("k_"); v = sbf("v_"); tA = sbf("tA"); tB = sbf("tB")
    sig = sbf("sig"); aa = sbf("aa"); cumI = sbf("cumI"); G = sbf("G"); Ginv = sbf("Ginv"); GX = sbf("GX"); GLr = sbf("GLr")
    kk = sbf("kk"); kp = sbf("kprime"); kka = sbf("kka"); sg = sbf("sg")
    At = k.sb("At", [128, 4, 512], BF16)
    Bh = sbf("Bh", BF16); Kh = sbf("Kh", BF16); Vb = sbf("Vb", BF16)
    ssq = k.sb("ssq", [128, 8], F32); bsum = k.sb("bsum", [128, 8], F32)
    hs = k.sb("hs", [128, 8], F32); hq = k.sb("hq", [128, 8], F32); hm = k.sb("hm", [128, 8], F32); hr = k.sb("hr", [128, 8], F32)
    glT = k.sb("glT", [64, 8], F32)
    FT = k.sb("FT", [64, 8, 4, 128], BF16)
    Am = [k.sb(f"Am{i}", [128, 512], F32) for i in range(8)]
    Abf = [k.sb(f"Abf{i}", [128, 3, 128], BF16) for i in range(8)]
    Mb = [[k.sb(f"Mb{i}_{j}", [128, 256], BF16) for j in range(2)] for i in range(8)]
    Qb = [[k.sb(f"Qb{i}_{j}", [128, 128], F32) for j in range(2)] for i in range(8)]
    Qh = [[k.sb(f"Qh{i}_{j}", [128, 128], BF16) for j in range(2)] for i in range(8)]
    Xs = k.sb("Xs", [128, 512], F32)
    Ub = k.sb("Ub", [128, 512], BF16)
    yo = sbf("yo"); yb = sbf("yb", BF16)
    sto = k.sb("stow", [64, 8, 64], F32)
    psL = [k.ps(f"psL{i}", [128, 512], F32) for i in range(2)]
    psA = k.ps("psA", [128, 512], F32)
    psInv = [k.ps(f"psInv{i}", [128, 512], F32) for i in range(2)]
    psY = k.ps("psYr", [128, 512], F32)
    psMt = k.ps("psMisc", [128, 512], F32)
    psN = Buf(psMt.t[:, 0:128], "psN", psMt); psX = Buf(psMt.t[:, 128:192], "psX", psMt); psU = Buf(psMt.t[:, 192:256], "psU", psMt)
    psSt = Buf(psMt.t[:64, 256:320], "psSt", psMt); psG = Buf(psMt.t[:64, 320:328], "psG", psMt)
    psW = Buf(psMt.t[:64, 0:512].rearrange("p (h v) -> p h v", h=8), "psW", psMt)
    sq_banks = [psInv[0], psInv[1], psL[0], psL[1]]
    sqv = [Buf(sq_banks[h // 2].t[:, (h % 2) * 256:(h % 2 + 1) * 256], f"sqv{h}", sq_banks[h // 2]) for h in range(8)]
    mq_banks = [psA, psY]
    mqv = [Buf(mq_banks[h // 4].t[:, (h % 4) * 128:(h % 4 + 1) * 128], f"mqv{h}", mq_banks[h // 4]) for h in range(8)]
    psNv = [Buf(psInv[i].t[:, 0:128], f"psNv{i}", psInv[i]) for i in range(2)]
    u = 0
    for hg in range(ngroups):
        cs = hg * 512
        for (tile, nm, off) in ((mu_r, "rwkv_mu", 0), (mu_k, "rwkv_mu", 4096), (mu_v, "rwkv_mu", 8192), (w0, "rwkv_w0", 0),
                                (a0, "rwkv_a0", 0), (k_k, "rwkv_k_k", 0), (k_a, "rwkv_k_a", 0), (r_k, "rwkv_r_k", 0),
                                (lnw, "rwkv_lnx_w", 0), (lnb, "rwkv_lnx_b", 0)):
            k.dma("act", tile.t[:, :], prm[nm][0:1, off + cs:off + cs + 512].to_broadcast([128, 512]), writes=[tile])
        k.dma("pool", wup.t[:, :], prm["rwkv_w_up"][:, cs:cs + 512], writes=[wup])
        k.dma("pool", aup.t[:, :], prm["rwkv_a_up"][:, cs:cs + 512], writes=[aup])
        k.op("pool", lambda e: e.memset(ST.t[:, :, :], 0.0), writes=[ST])
        k.op("pool", lambda e: e.memset(STb.t[:, :, :], 0.0), writes=[STb])
        for c in range(nchunks):
            t0 = c * 128
            L = ld[u % 2]
            u += 1
            for i, off in enumerate((0, 4096, 8192)):
                k.dma("sp", L[i].t[:, :], P0[t0:t0 + 128, PB + off + cs:PB + off + cs + 512], writes=[L[i]])
                if c == 0:
                    k.op("pool", lambda e: e.memset(L[3 + i].t[:, :], 0.0), writes=[L[3 + i]])
                    k.dma("act", L[3 + i].t[1:128, :], P0[0:127, PB + off + cs:PB + off + cs + 512], writes=[L[3 + i]])
                else:
                    k.dma("act", L[3 + i].t[:, :], P0[t0 - 1:t0 + 127, PB + off + cs:PB + off + cs + 512], writes=[L[3 + i]])
            k.dma("sp", L[6].t[:, :], P0[t0:t0 + 128, GB + cs:GB + cs + 512], writes=[L[6]])
            for i, (dst, mu) in enumerate(((r, mu_r), (kt, mu_k), (v, mu_v))):
                en = "dve" if i != 1 else "pool"
                k.op(en, lambda e: e.tensor_tensor(out=dst.t[:, :], in0=L[3 + i].t[:, :], in1=L[i].t[:, :], op=ALU.subtract), reads=[L[3 + i], L[i]], writes=[dst])
                k.op(en, lambda e: e.tensor_tensor(out=dst.t[:, :], in0=dst.t[:, :], in1=mu.t[:, :], op=ALU.mult), reads=[dst, mu], writes=[dst])
                k.op(en, lambda e: e.tensor_tensor(out=dst.t[:, :], in0=dst.t[:, :], in1=L[i].t[:, :], op=ALU.add), reads=[dst, L[i]], writes=[dst])
            k.mm(psL[0], psL[0].t[:, :], [(loraT.t[:, 0, t0:t0 + 128], wup.t[:, :])], reads=[loraT, wup])
            k.mm(psL[1], psL[1].t[:, :], [(loraT.t[:, 1, t0:t0 + 128], aup.t[:, :])], reads=[loraT, aup])
            k.op("dve", lambda e: e.tensor_tensor(out=tA.t[:, :], in0=psL[0].t[:, :], in1=w0.t[:, :], op=ALU.add), reads=[psL[0], w0], writes=[tA])
            k.op("act", lambda e: e.activation(out=sig.t[:, :], in_=tA.t[:, :], func=AF.Sigmoid), reads=[tA], writes=[sig])
            k.op("dve", lambda e: e.tensor_tensor(out=tB.t[:, :], in0=psL[1].t[:, :], in1=a0.t[:, :], op=ALU.add), reads=[psL[1], a0], writes=[tB])
            k.op("act", lambda e: e.activation(out=aa.t[:, :], in_=tB.t[:, :], func=AF.Sigmoid), reads=[tB], writes=[aa])
            k.mm(psL[0], psL[0].t[:, :], [(tri_le.t[:, :], sig.t[:, :])], reads=[tri_le, sig])
            k.mm(psL[1], psL[1].t[:, :], [(ones.t[:, :], sig.t[:, :])], reads=[ones, sig])
            k.op("act", lambda e: e.copy(out=cumI.t[:, :], in_=psL[0].t[:, :]), reads=[psL[0]], writes=[cumI])
            k.op("act", lambda e: e.activation(out=G.t[:, :], in_=cumI.t[:, :], func=AF.Exp, scale=-C1), reads=[cumI], writes=[G])
            k.op("act", lambda e: e.activation(out=Ginv.t[:, :], in_=cumI.t[:, :], func=AF.Exp, scale=C1), reads=[cumI], writes=[Ginv])
            k.op("dve", lambda e: e.tensor_tensor(out=tA.t[:, :], in0=cumI.t[:, :], in1=sig.t[:, :], op=ALU.subtract), reads=[cumI, sig], writes=[tA])
            k.op("act", lambda e: e.activation(out=GX.t[:, :], in_=tA.t[:, :], func=AF.Exp, scale=-C1), reads=[tA], writes=[GX])
            k.op("dve", lambda e: e.tensor_tensor(out=tB.t[:, :], in0=psL[1].t[:, :], in1=cumI.t[:, :], op=ALU.subtract), reads=[psL[1], cumI], writes=[tB])
            k.op("act", lambda e: e.activation(out=GLr.t[:, :], in_=tB.t[:, :], func=AF.Exp, scale=-C1), reads=[tB], writes=[GLr])
            k3 = lambda t_: t_.t[:, :].rearrange("p (h d) -> p h d", h=8)
            bc8 = lambda t_: t_.t[:, :].unsqueeze(2).to_broadcast([128, 8, 64])
            k.op("pool", lambda e: e.tensor_tensor(out=kk.t[:, :], in0=kt.t[:, :], in1=k_k.t[:, :], op=ALU.mult), reads=[kt, k_k], writes=[kk])
            k.op("pool", lambda e: e.tensor_tensor(out=tA.t[:, :], in0=kk.t[:, :], in1=kk.t[:, :], op=ALU.mult), reads=[kk, tA], writes=[tA])
            k.op("dve", lambda e: e.tensor_reduce(out=ssq.t[:, :], in_=k3(tA), axis=AX.X, op=ALU.add), reads=[tA], writes=[ssq])
            k.op("dve", lambda e: e.tensor_scalar(out=ssq.t[:, :], in0=ssq.t[:, :], scalar1=1e-24, scalar2=None, op0=ALU.max), reads=[ssq], writes=[ssq])
            k.op("act", lambda e: e.activation(out=ssq.t[:, :], in_=ssq.t[:, :], func=AF.Sqrt), reads=[ssq], writes=[ssq])
            k.op("dve", lambda e: e.reciprocal(out=ssq.t[:, :], in_=ssq.t[:, :]), reads=[ssq], writes=[ssq])
            k.op("dve", lambda e: e.tensor_tensor(out=k3(kk), in0=k3(kk), in1=bc8(ssq), op=ALU.mult), reads=[kk, ssq], writes=[kk])
            k.op("dve", lambda e: e.scalar_tensor_tensor(out=tB.t[:, :], in0=aa.t[:, :], scalar=-1.0, in1=k_a.t[:, :], op0=ALU.add, op1=ALU.mult), reads=[aa, k_a, tB], writes=[tB])
            k.op("dve", lambda e: e.scalar_tensor_tensor(out=kp.t[:, :], in0=tB.t[:, :], scalar=1.0, in1=kt.t[:, :], op0=ALU.add, op1=ALU.mult), reads=[tB, kt], writes=[kp])
            k.op("dve", lambda e: e.tensor_tensor(out=kka.t[:, :], in0=kk.t[:, :], in1=aa.t[:, :], op=ALU.mult), reads=[kk, aa], writes=[kka])
            k.op("dve", lambda e: e.tensor_tensor(out=At.t[:, 0, :], in0=kk.t[:, :], in1=GX.t[:, :], op=ALU.mult), reads=[kk, GX], writes=[At])
            k.op("pool", lambda e: e.tensor_tensor(out=At.t[:, 1, :], in0=r.t[:, :], in1=G.t[:, :], op=ALU.mult), reads=[r, G, At], writes=[At])
            k.op("dve", lambda e: e.tensor_tensor(out=At.t[:, 2, :], in0=kka.t[:, :], in1=Ginv.t[:, :], op=ALU.mult), reads=[kka, Ginv], writes=[At])
            k.op("pool", lambda e: e.tensor_tensor(out=At.t[:, 3, :], in0=kp.t[:, :], in1=Ginv.t[:, :], op=ALU.mult), reads=[kp, Ginv, At], writes=[At])
            k.op("dve", lambda e: e.tensor_tensor(out=Bh.t[:, :], in0=kka.t[:, :], in1=GLr.t[:, :], op=ALU.mult), reads=[kka, GLr], writes=[Bh])
            k.op("pool", lambda e: e.tensor_tensor(out=Kh.t[:, :], in0=kp.t[:, :], in1=GLr.t[:, :], op=ALU.mult), reads=[kp, GLr], writes=[Kh])
            k.op("act", lambda e: e.copy(out=Vb.t[:, :], in_=v.t[:, :]), reads=[v], writes=[Vb])
            k.op("pool", lambda e: e.tensor_tensor(out=tA.t[:, :], in0=r.t[:, :], in1=kp.t[:, :], op=ALU.mult), reads=[r, kp, tA], writes=[tA])
            k.op("pool", lambda e: e.tensor_tensor(out=tA.t[:, :], in0=tA.t[:, :], in1=r_k.t[:, :], op=ALU.mult), reads=[tA, r_k], writes=[tA])
            k.op("dve", lambda e: e.tensor_reduce(out=bsum.t[:, :], in_=k3(tA), axis=AX.X, op=ALU.add), reads=[tA], writes=[bsum])
            k.mm_multi(psG, [(psG.t[:, h:h + 1], [(G.t[:, h * 64:(h + 1) * 64], e127.t)]) for h in range(8)], reads=[G, e127])
            k.op("dve", lambda e: e.tensor_copy(out=glT.t[:, :], in_=psG.t[:, :]), reads=[psG], writes=[glT])
            for h in range(8):
                k.mm_multi(psTr, [(psTr.t[:64, q * 128:(q + 1) * 128], [(At.t[:, q, h * 64:(h + 1) * 64], ident_bf.t[:, :])]) for q in range(4)],
                           reads=[At, ident_bf], transpose=True)
                en = "act" if h % 2 == 0 else "dve"
                if en == "act":
                    k.op("act", lambda e: e.copy(out=FT.t[:, h, :, :], in_=psTr.t[:64, 0:512].rearrange("p (q t) -> p q t", q=4)), reads=[psTr], writes=[FT])
                else:
                    k.op("dve", lambda e: e.tensor_copy(out=FT.t[:, h, :, :], in_=psTr.t[:64, 0:512].rearrange("p (q t) -> p q t", q=4)), reads=[psTr], writes=[FT])
            for h in range(8):
                pa = psA if h % 2 == 0 else psY
                pn = psNv[h % 2]
                rhsAR = FT.t[:, h, 0:2, :].rearrange("p q t -> p (q t)")
                k.mm_multi(pa, [(pa.t[:, 0:256], [(FT.t[:, h, 2, :], rhsAR)]), (pa.t[:, 256:512], [(FT.t[:, h, 3, :], rhsAR)])], reads=[FT])
                k.mm(pn, pn.t, [(FT.t[:, h, 0, :], FT.t[:, h, 2, :])], reads=[FT])
                k.op("dve", lambda e: e.tensor_tensor(out=Am[h].t[:, :], in0=pa.t[:, :], in1=maskA.t, op=ALU.mult), reads=[pa, maskA], writes=[Am[h]])
                k.op("act", lambda e: e.copy(out=Abf[h].t[:, :, :].rearrange("p a b -> p (a b)"), in_=Am[h].t[:, 128:512]), reads=[Am[h]], writes=[Abf[h]])
                k.op("pool", lambda e: e.tensor_copy(out=Mb[h][0].t[:, 0:128], in_=Am[h].t[:, 0:128]), reads=[Am[h], Mb[h][0]], writes=[Mb[h][0]])
                k.op("dve", lambda e: e.tensor_tensor(out=Mb[h][0].t[:, 128:256], in0=pn.t, in1=tri_gt.t[:, :], op=ALU.mult), reads=[pn, tri_gt, Mb[h][0]], writes=[Mb[h][0]])
                k.op("pool", lambda e: e.tensor_tensor(out=Qb[h][0].t[:, :], in0=ident.t[:, :], in1=Am[h].t[:, 0:128], op=ALU.subtract), reads=[ident, Am[h]], writes=[Qb[h][0]])
                k.op("pool", lambda e: e.tensor_copy(out=Qh[h][0].t[:, :], in_=Qb[h][0].t[:, :]), reads=[Qb[h][0]], writes=[Qh[h][0]])
            NR = 7
            for rd in range(NR):
                for h in range(8):
                    Mc = Mb[h][rd % 2]; Qc = Qb[h][rd % 2]
                    items = []
                    if rd < NR - 2:
                        items.append((sqv[h].t[:, 0:128], [(Mc.t[:, 128:256], Mc.t[:, 0:128])]))
                    if rd < NR - 1:
                        items.append((sqv[h].t[:, 128:256], [(Mc.t[:, 0:128], Mc.t[:, 128:256])]))
                    if items:
                        k.mm_multi(sqv[h], items, reads=[Mc])
                    if rd > 0:
                        k.mm(mqv[h], mqv[h].t, [(Mc.t[:, 128:256], Qh[h][rd % 2].t[:, :])], reads=[Mc, Qh[h][rd % 2]])
                for h in range(8):
                    Qc = Qb[h][rd % 2]
                    Mn = Mb[h][(rd + 1) % 2]; Qn = Qb[h][(rd + 1) % 2]
                    if rd < NR - 1:
                        lo = 0 if rd < NR - 2 else 128
                        k.op("act", lambda e: e.copy(out=Mn.t[:, lo:256], in_=sqv[h].t[:, lo:256]), reads=[sqv[h]], writes=[Mn])
                    Qhn = Qh[h][(rd + 1) % 2]
                    if rd > 0:
                        k.op("dve", lambda e: e.tensor_tensor(out=Qn.t[:, :], in0=Qc.t[:, :], in1=mqv[h].t, op=ALU.add), reads=[Qc, mqv[h]], writes=[Qn])
                    else:
                        k.op("pool", lambda e: e.tensor_copy(out=Qn.t[:, :], in_=Qc.t[:, :]), reads=[Qc], writes=[Qn])
                    if rd < NR - 1:
                        k.op("pool", lambda e: e.tensor_copy(out=Qhn.t[:, :], in_=Qn.t[:, :]), reads=[Qn, Qhn], writes=[Qhn])
            hcs = [slice(h * 64, (h + 1) * 64) for h in range(8)]
            k.mm_multi(psMt, [(psMt.t[:, hcs[h]], [(FT.t[:, h, 0, :], STb.t[:, h, :]), (Abf[h].t[:, 1, :], Vb.t[:, hcs[h]])]) for h in range(8)],
                       reads=[FT, STb, Vb] + Abf)
            k.op("dve", lambda e: e.tensor_copy(out=Xs.t[:, :], in_=psMt.t[:, :]), reads=[psMt], writes=[Xs])
            Ws = [Qb[h][NR % 2] for h in range(8)]
            k.mm_multi(psInv[0], [(psInv[0].t[:, hcs[h]], [(Ws[h].t[:, :], Xs.t[:, hcs[h]])]) for h in range(8)], reads=[Xs] + Ws)
            k.op("dve", lambda e: e.tensor_scalar(out=Ub.t[:, :], in0=psInv[0].t[:, :], scalar1=-1.0, scalar2=None, op0=ALU.mult), reads=[psInv[0]], writes=[Ub])
            k.mm_multi(psY, [(psY.t[:, hcs[h]], [(FT.t[:, h, 1, :], STb.t[:, h, :]), (Abf[h].t[:, 0, :], Ub.t[:, hcs[h]]), (Abf[h].t[:, 2, :], Vb.t[:, hcs[h]])]) for h in range(8)],
                       reads=[FT, STb, Ub, Vb] + Abf)
            k.mm_multi(psInv[1], [(psInv[1].t[:64, hcs[h]], [(Bh.t[:, hcs[h]], Ub.t[:, hcs[h]]), (Kh.t[:, hcs[h]], Vb.t[:, hcs[h]])]) for h in range(8)],
                       reads=[Bh, Kh, Ub, Vb])
            k.op("dve", lambda e: e.tensor_tensor(out=ST.t[:, :, :], in0=ST.t[:, :, :], in1=glT.t[:, :].unsqueeze(2).to_broadcast([64, 8, 64]), op=ALU.mult), reads=[ST, glT], writes=[ST])
            k.op("dve", lambda e: e.tensor_tensor(out=ST.t[:, :, :], in0=ST.t[:, :, :], in1=psInv[1].t[:64, :].rearrange("p (h v) -> p h v", h=8), op=ALU.add), reads=[ST, psInv[1]], writes=[ST])
            k.op("act", lambda e: e.copy(out=STb.t[:, :, :], in_=ST.t[:, :, :]), reads=[ST], writes=[STb])
            k.op("act", lambda e: e.copy(out=yo.t[:, :], in_=psY.t[:, :]), reads=[psY], writes=[yo])
            k.op("dve", lambda e: e.tensor_reduce(out=hs.t[:, :], in_=k3(yo), axis=AX.X, op=ALU.add), reads=[yo], writes=[hs])
            k.op("pool", lambda e: e.tensor_tensor(out=tA.t[:, :], in0=yo.t[:, :], in1=yo.t[:, :], op=ALU.mult), reads=[yo, tA], writes=[tA])
            k.op("dve", lambda e: e.tensor_reduce(out=hq.t[:, :], in_=k3(tA), axis=AX.X, op=ALU.add), reads=[tA], writes=[hq])
            k.op("dve", lambda e: e.tensor_scalar(out=hm.t[:, :], in0=hs.t[:, :], scalar1=1.0 / 64, scalar2=None, op0=ALU.mult), reads=[hs], writes=[hm])
            k.op("dve", lambda e: e.tensor_tensor(out=hs.t[:, :], in0=hm.t[:, :], in1=hm.t[:, :], op=ALU.mult), reads=[hm], writes=[hs])
            k.op("dve", lambda e: e.scalar_tensor_tensor(out=hr.t[:, :], in0=hq.t[:, :], scalar=1.0 / 64, in1=hs.t[:, :], op0=ALU.mult, op1=ALU.subtract), reads=[hq, hs], writes=[hr])
            k.op("dve", lambda e: e.tensor_scalar(out=hr.t[:, :], in0=hr.t[:, :], scalar1=64e-5, scalar2=None, op0=ALU.add), reads=[hr], writes=[hr])
            k.op("act", lambda e: e.activation(out=hr.t[:, :], in_=hr.t[:, :], func=AF.Sqrt), reads=[hr], writes=[hr])
            k.op("dve", lambda e: e.reciprocal(out=hr.t[:, :], in_=hr.t[:, :]), reads=[hr], writes=[hr])
            k.op("dve", lambda e: e.tensor_tensor(out=k3(yo), in0=k3(yo), in1=bc8(hm), op=ALU.subtract), reads=[yo, hm], writes=[yo])
            k.op("dve", lambda e: e.tensor_tensor(out=k3(yo), in0=k3(yo), in1=bc8(hr), op=ALU.mult), reads=[yo, hr], writes=[yo])
            k.op("pool", lambda e: e.tensor_tensor(out=yo.t[:, :], in0=yo.t[:, :], in1=lnw.t[:, :], op=ALU.mult), reads=[yo, lnw], writes=[yo])
            k.op("pool", lambda e: e.tensor_tensor(out=yo.t[:, :], in0=yo.t[:, :], in1=lnb.t[:, :], op=ALU.add), reads=[yo, lnb], writes=[yo])
            k.op("dve", lambda e: e.tensor_tensor(out=k3(tA), in0=k3(v), in1=bc8(bsum), op=ALU.mult), reads=[v, bsum, tA], writes=[tA])
            k.op("dve", lambda e: e.tensor_tensor(out=yo.t[:, :], in0=yo.t[:, :], in1=tA.t[:, :], op=ALU.add), reads=[yo, tA], writes=[yo])
            k.op("act", lambda e: e.activation(out=sg.t[:, :], in_=L[6].t[:, :], func=AF.Silu), reads=[L[6]], writes=[sg])
            k.op("dve", lambda e: e.tensor_tensor(out=yb.t[:, :], in0=yo.t[:, :], in1=sg.t[:, :], op=ALU.mult), reads=[yo, sg], writes=[yb])
            k.mm_multi(psTr, [(psTr.t[:, 512 + j * 128:512 + (j + 1) * 128], [(yb.t[:, j * 128:(j + 1) * 128], ident_bf.t[:, :])]) for j in range(4)],
                       reads=[yb, ident_bf], transpose=True)
            k.op("act", lambda e: e.copy(out=yTs.t[:, :, t0:t0 + 128], in_=psTr.t[:, 512:1024].rearrange("p (j t) -> p j t", j=4)), reads=[psTr], writes=[yTs])
        for j in range(4):
            k.dma("sp", YT[0:nchunks, :, 32 + hg * 4 + j, :].rearrange("t p k -> p t k"), yTs.t[:, j, 0:nchunks * 128].rearrange("p (t k) -> p t k", k=128), reads=[yTs])
        k.mm_multi(psW, [(psW.t[:, h, :], [(ST.t[:, h, :], ident.t[:64, :64])]) for h in range(8)], reads=[ST, ident], transpose=True)
        k.op("dve", lambda e: e.tensor_copy(out=sto.t[:, :, :], in_=psW.t), reads=[psW], writes=[sto])
        k.dma("sp", out_wkv[cs:cs + 512, :].rearrange("(h v) n -> v h n", v=64), sto.t[:, :, :], reads=[sto])


ALPHA = (2 * 2) ** 0.25


def phase_outproj(k, YT, W, H):
    EC = 64
    Wv = W.rearrange("(ec p) n -> p ec n", p=128)
    wb = [k.sb(f"wo{i}", [128, EC, 512], BF16) for i in range(2)]
    yt = [k.sb(f"yt{i}", [128, EC, 128], BF16) for i in range(2)]
    psb = [k.ps(f"pso{i}", [128, 512], F32) for i in range(4)]
    stg = [k.sb(f"stgo{i}", [128, 512], F32) for i in range(2)]
    ntile = TP // 128 + 1

    def loadw(g):
        for h in range(8):
            k.dma("pool", wb[g % 2].t[:, h * 8:(h + 1) * 8, :], Wv[:, h * 8:(h + 1) * 8, g * 512:(g + 1) * 512], writes=[wb[g % 2]])

    q = 0
    loadw(0)
    for g in range(8):
        if g + 1 < 8:
            loadw(g + 1)
        for t in range(ntile):
            m = 128 if t < TP // 128 else NS
            yb = yt[q % 2]
            for h in range(2):
                k.dma("sp" if h == 0 else "act", yb.t[:, h * 32:(h + 1) * 32, :m], YT[t, :, h * 32:(h + 1) * 32, 0:m], writes=[yb])
            pb = psb[q % 4]
            k.mm(pb, pb.t[:m, :], [(yb.t[:, ec, :m], wb[g % 2].t[:, ec, :]) for ec in range(EC)], reads=[yb, wb[g % 2]])
            sg = stg[q % 2]
            if q % 2 == 0:
                k.op("dve", lambda e: e.tensor_copy(out=sg.t[:m, :], in_=pb.t[:m, :]), reads=[pb], writes=[sg])
            else:
                k.op("act", lambda e: e.copy(out=sg.t[:m, :], in_=pb.t[:m, :]), reads=[pb], writes=[sg])
            k.dma("sp", H[t * 128:t * 128 + m, g * 512:(g + 1) * 512], sg.t[:m, :], reads=[sg])
            q += 1


def phase_ln(k, H, xin, ln_w, ln_b, C, dst, xT=None):
    ident_bf = C["ident_bf"]
    w_bc = load_bc(k, "lnw_bc", ln_w, D)
    b_bc = load_bc(k, "lnb_bc", ln_b, D)
    ht = k.sb("ht", [128, D], F32); xt = k.sb("xt_", [128, D], F32)
    y16 = k.sb("y16", [128, D], BF16)
    s1 = k.sb("s1", [128, 1], F32); s2 = k.sb("s2", [128, 1], F32); mean = k.sb("mean", [128, 1], F32); rstd = k.sb("rstd_", [128, 1], F32)
    pst = [k.ps(f"psl{i}", [128, 1024], BF16) for i in range(2)]
    ntile = TP // 128 + 1
    q = 0
    for t in range(ntile):
        m = 128 if t < TP // 128 else NS
        k.dma("sp", ht.t[:m, :], H[t * 128:t * 128 + m, :], writes=[ht])
        k.dma("act", xt.t[:m, :], xin(t), writes=[xt])
        k.op("dve", lambda e: e.scalar_tensor_tensor(out=ht.t[:m, :], in0=xt.t[:m, :], scalar=ALPHA, in1=ht.t[:m, :], op0=ALU.mult, op1=ALU.add), reads=[xt, ht], writes=[ht])
        k.op("act", lambda e: e.activation(out=xt.t[:m, :], in_=ht.t[:m, :], func=AF.Identity, accum_out=s1.t[:m, :]), reads=[ht], writes=[xt, s1])
        k.op("act", lambda e: e.activation(out=xt.t[:m, :], in_=ht.t[:m, :], func=AF.Square, accum_out=s2.t[:m, :]), reads=[ht], writes=[xt, s2])
        k.op("dve", lambda e: e.tensor_scalar(out=mean.t[:m, :], in0=s1.t[:m, :], scalar1=1.0 / D, scalar2=None, op0=ALU.mult), reads=[s1], writes=[mean])
        k.op("dve", lambda e: e.tensor_tensor(out=s1.t[:m, :], in0=mean.t[:m, :], in1=mean.t[:m, :], op=ALU.mult), reads=[mean], writes=[s1])
        k.op("dve", lambda e: e.scalar_tensor_tensor(out=rstd.t[:m, :], in0=s2.t[:m, :], scalar=1.0 / D, in1=s1.t[:m, :], op0=ALU.mult, op1=ALU.subtract), reads=[s2, s1], writes=[rstd])
        k.op("dve", lambda e: e.tensor_scalar(out=rstd.t[:m, :], in0=rstd.t[:m, :], scalar1=1e-5, scalar2=None, op0=ALU.add), reads=[rstd], writes=[rstd])
        k.op("act", lambda e: e.activation(out=rstd.t[:m, :], in_=rstd.t[:m, :], func=AF.Sqrt), reads=[rstd], writes=[rstd])
        k.op("dve", lambda e: e.reciprocal(out=rstd.t[:m, :], in_=rstd.t[:m, :]), reads=[rstd], writes=[rstd])
        k.op("dve", lambda e: e.tensor_scalar(out=ht.t[:m, :], in0=ht.t[:m, :], scalar1=mean.t[:m, 0:1], scalar2=rstd.t[:m, 0:1], op0=ALU.subtract, op1=ALU.mult),
             reads=[ht, mean, rstd], writes=[ht])
        k.op("pool", lambda e: e.tensor_tensor(out=ht.t[:m, :], in0=ht.t[:m, :], in1=w_bc.t[:m, :], op=ALU.mult), reads=[ht, w_bc], writes=[ht])
        k.op("dve", lambda e: e.tensor_tensor(out=ht.t[:m, :], in0=ht.t[:m, :], in1=b_bc.t[:m, :], op=ALU.add), reads=[ht, b_bc], writes=[ht])
        k.dma("sp", dst(t), ht.t[:m, :], reads=[ht])
        if xT is not None:
            k.op("act", lambda e: e.copy(out=y16.t[:m, :], in_=ht.t[:m, :]), reads=[ht], writes=[y16])
            for g in range(KC // 8):
                pb = pst[q % 2]
                k.mm_multi(pb, [(pb.t[:, j * 128:j * 128 + m], [(y16.t[:m, (g * 8 + j) * 128:(g * 8 + j + 1) * 128], ident_bf.t[:m, :m])]) for j in range(8)],
                           reads=[y16, ident_bf], transpose=True)
                src_ap = pb.t[:, :].rearrange("p (j m) -> p j m", j=8)[:, :, :m]
                dst_ap = xT.t[:, g * 8:(g + 1) * 8, t * 128:t * 128 + m]
                if q % 2 == 0:
                    k.op("dve", lambda e: e.tensor_copy(out=dst_ap, in_=src_ap), reads=[pb], writes=[xT])
                else:
                    k.op("act", lambda e: e.copy(out=dst_ap, in_=src_ap), reads=[pb], writes=[xT])
                q += 1


RET_LOG_G = [float(np.log1p(-np.exp2(-5.0 - h))) for h in range(16)]


def phase_ret(k, P1, prm, C, RC, YT, out_ret, nchunks=TP // 128, nheads=16):
    ident_bf = C["ident_bf"]
    rope = k.sb("rope_sb", [128, 16, 256], F32)
    k.dma("sp", rope.t[:, :, :], RC["rope"][0:TP, :].rearrange("(c p) f -> p c f", p=128), writes=[rope])
    qdec = k.sb("qdec_sb", [128, 16], F32); kdec = k.sb("kdec_sb", [128, 16], F32)
    k.dma("sp", qdec.t[:, :], RC["qdec"][:, :], writes=[qdec])
    k.dma("sp", kdec.t[:, :], RC["kdec"][:, :], writes=[kdec])
    innerT = k.sb("innerT_sb", [128, 128], F32)
    gnw = k.sb("gnw", [128, 512], F32)
    S = k.sb("Sr", [128, 2, 512], F32); Sb = k.sb("Srb", [128, 2, 512], BF16)
    yTs = k.sb("yTs1", [128, 4, TP], BF16)
    ld = [[k.sb(f"rl{n}{i}", [128, 512], F32) for n in ("qk", "v", "g")] for i in range(2)]
    def db(name, shape, dt):
        return [k.sb(f"{name}{i}", shape, dt) for i in range(2)]
    DB = dict(
        ro=db("ro", [128, 2, 256], F32), tmp=db("rtmp", [128, 2, 128], F32), tmp2=db("rtmp2", [128, 2, 128], F32),
        qk16=db("qk16", [128, 512], BF16), kd16=db("kd16", [128, 256], BF16), v16=db("v16", [128, 512], BF16),
        qkT=db("qkT", [128, 4, 128], BF16), scT=db("scT", [128, 128], BF16), yi=db("yi", [128, 512], F32),
        yo=db("yor", [128, 512], F32), sg=db("sgr", [128, 512], F32), junk=db("junkr", [128, 512], F32), yb=db("ybr", [128, 512], BF16),
        s1=db("rs1", [128, 1], F32), s2=db("rs2", [128, 1], F32), mean=db("rmean", [128, 1], F32), rstd=db("rrstd", [128, 1], F32))
    psT = k.ps("psTq", [128, 1024], BF16)
    psSc = k.ps("psSc", [128, 512], F32)
    psYi = k.ps("psYi", [128, 512], F32); psYe = k.ps("psYe", [128, 512], F32)
    psS = [k.ps(f"psSr{i}", [128, 512], F32) for i in range(2)]
    u = 0
    for h in range(nheads):
        k.dma("sp", innerT.t[:, :], RC["innerT"][h, :, :], writes=[innerT])
        k.dma("sp", gnw.t[:, :], prm["ret_gn_w"][0:1, h * 512:(h + 1) * 512].to_broadcast([128, 512]), writes=[gnw])
        k.op("pool", lambda e: e.memset(S.t[:, :, :], 0.0), writes=[S])
        k.op("pool", lambda e: e.memset(Sb.t[:, :, :], 0.0), writes=[Sb])
        cd = float(np.exp(128.0 * RET_LOG_G[h]))
        for c in range(nchunks):
            t0 = c * 128
            L = ld[u % 2]
            (ro, tmp, tmp2, qk16, kd16, v16, qkT, scT, yi, yo, sg, junk, yb, s1, s2, mean, rstd) = [
                DB[n][u % 2] for n in ("ro", "tmp", "tmp2", "qk16", "kd16", "v16", "qkT", "scT", "yi", "yo", "sg", "junk", "yb", "s1", "s2", "mean", "rstd")]
            u += 1
            k.dma("sp", L[0].t[:, 0:256], P1[t0:t0 + 128, h * 256:(h + 1) * 256], writes=[L[0]])
            k.dma("sp", L[0].t[:, 256:512], P1[t0:t0 + 128, 4096 + h * 256:4096 + (h + 1) * 256], writes=[L[0]])
            k.dma("act", L[1].t[:, :], P1[t0:t0 + 128, 8192 + h * 512:8192 + (h + 1) * 512], writes=[L[1]])
            k.dma("act", L[2].t[:, :], P1[t0:t0 + 128, 16384 + h * 512:16384 + (h + 1) * 512], writes=[L[2]])
            X = L[0].t[:, :].rearrange("p (a d) -> p a d", a=2)
            cosb = rope.t[:, c, 0:128].unsqueeze(1).to_broadcast([128, 2, 128])
            sinb = rope.t[:, c, 128:256].unsqueeze(1).to_broadcast([128, 2, 128])
            k.op("dve", lambda e: e.tensor_tensor(out=ro.t[:, :, 0:128], in0=X[:, :, 0:128], in1=cosb, op=ALU.mult), reads=[L[0], rope], writes=[ro])
            k.op("pool", lambda e: e.tensor_tensor(out=tmp.t[:, :, :], in0=X[:, :, 128:256], in1=sinb, op=ALU.mult), reads=[L[0], rope], writes=[tmp])
            k.op("dve", lambda e: e.tensor_tensor(out=ro.t[:, :, 0:128], in0=ro.t[:, :, 0:128], in1=tmp.t[:, :, :], op=ALU.subtract), reads=[ro, tmp], writes=[ro])
            k.op("dve", lambda e: e.tensor_tensor(out=ro.t[:, :, 128:256], in0=X[:, :, 0:128], in1=sinb, op=ALU.mult), reads=[L[0], rope], writes=[ro])
            k.op("pool", lambda e: e.tensor_tensor(out=tmp2.t[:, :, :], in0=X[:, :, 128:256], in1=cosb, op=ALU.mult), reads=[L[0], rope], writes=[tmp2])
            k.op("dve", lambda e: e.tensor_tensor(out=ro.t[:, :, 128:256], in0=ro.t[:, :, 128:256], in1=tmp2.t[:, :, :], op=ALU.add), reads=[ro, tmp2], writes=[ro])
            k.op("act", lambda e: e.copy(out=qk16.t[:, :], in_=ro.t[:, :, :].rearrange("p a d -> p (a d)")), reads=[ro], writes=[qk16])
            k.op("dve", lambda e: e.tensor_scalar(out=kd16.t[:, :], in0=ro.t[:, 1, :], scalar1=kdec.t[:, h:h + 1], scalar2=None, op0=ALU.mult), reads=[ro, kdec], writes=[kd16])
            k.op("pool", lambda e: e.tensor_copy(out=v16.t[:, :], in_=L[1].t[:, :]), reads=[L[1]], writes=[v16])
            k.mm_multi(psT, [(psT.t[:, j * 128:(j + 1) * 128], [(qk16.t[:, j * 128:(j + 1) * 128], ident_bf.t[:, :])]) for j in range(4)], reads=[qk16, ident_bf], transpose=True)
            k.op("act", lambda e: e.copy(out=qkT.t[:, :, :], in_=psT.t[:, 0:512].rearrange("p (j t) -> p j t", j=4)), reads=[psT], writes=[qkT])
            k.mm(psSc, psSc.t[:, 0:128], [(qkT.t[:, 2, :], qkT.t[:, 0, :]), (qkT.t[:, 3, :], qkT.t[:, 1, :])], reads=[qkT])
            k.op("dve", lambda e: e.tensor_tensor(out=scT.t[:, :], in0=psSc.t[:, 0:128], in1=innerT.t[:, :], op=ALU.mult), reads=[psSc, innerT], writes=[scT])
            k.mm(psYi, psYi.t[:, :], [(scT.t[:, :], v16.t[:, :])], reads=[scT, v16])
            k.mm(psYe, psYe.t[:, :], [(qkT.t[:, 0, :], Sb.t[:, 0, :]), (qkT.t[:, 1, :], Sb.t[:, 1, :])], reads=[qkT, Sb])
            k.op("act", lambda e: e.copy(out=yi.t[:, :], in_=psYi.t[:, :]), reads=[psYi], writes=[yi])
            k.op("dve", lambda e: e.scalar_tensor_tensor(out=yo.t[:, :], in0=psYe.t[:, :], scalar=qdec.t[:, h:h + 1], in1=yi.t[:, :], op0=ALU.mult, op1=ALU.add),
                 reads=[psYe, qdec, yi], writes=[yo])
            for dc in range(2):
                k.mm(psS[dc], psS[dc].t[:, :], [(kd16.t[:, dc * 128:(dc + 1) * 128], v16.t[:, :])], reads=[kd16, v16])
                k.op("dve", lambda e: e.scalar_tensor_tensor(out=S.t[:, dc, :], in0=S.t[:, dc, :], scalar=cd, in1=psS[dc].t[:, :], op0=ALU.mult, op1=ALU.add),
                     reads=[S, psS[dc]], writes=[S])
            k.op("act", lambda e: e.copy(out=Sb.t[:, :, :], in_=S.t[:, :, :]), reads=[S], writes=[Sb])
            k.op("act", lambda e: e.activation(out=junk.t[:, :], in_=yo.t[:, :], func=AF.Identity, accum_out=s1.t[:, :]), reads=[yo], writes=[junk, s1])
            k.op("act", lambda e: e.activation(out=junk.t[:, :], in_=yo.t[:, :], func=AF.Square, accum_out=s2.t[:, :]), reads=[yo], writes=[junk, s2])
            k.op("dve", lambda e: e.tensor_scalar(out=mean.t[:, :], in0=s1.t[:, :], scalar1=1.0 / 512, scalar2=None, op0=ALU.mult), reads=[s1], writes=[mean])
            k.op("dve", lambda e: e.tensor_tensor(out=s1.t[:, :], in0=mean.t[:, :], in1=mean.t[:, :], op=ALU.mult), reads=[mean], writes=[s1])
            k.op("dve", lambda e: e.scalar_tensor_tensor(out=rstd.t[:, :], in0=s2.t[:, :], scalar=1.0 / 512, in1=s1.t[:, :], op0=ALU.mult, op1=ALU.subtract), reads=[s2, s1], writes=[rstd])
            k.op("dve", lambda e: e.tensor_scalar(out=rstd.t[:, :], in0=rstd.t[:, :], scalar1=1e-6, scalar2=None, op0=ALU.add), reads=[rstd], writes=[rstd])
            k.op("act", lambda e: e.activation(out=rstd.t[:, :], in_=rstd.t[:, :], func=AF.Sqrt), reads=[rstd], writes=[rstd])
            k.op("dve", lambda e: e.reciprocal(out=rstd.t[:, :], in_=rstd.t[:, :]), reads=[rstd], writes=[rstd])
            k.op("dve", lambda e: e.tensor_scalar(out=yo.t[:, :], in0=yo.t[:, :], scalar1=mean.t[:, 0:1], scalar2=rstd.t[:, 0:1], op0=ALU.subtract, op1=ALU.mult),
                 reads=[yo, mean, rstd], writes=[yo])
            k.op("pool", lambda e: e.tensor_tensor(out=yo.t[:, :], in0=yo.t[:, :], in1=gnw.t[:, :], op=ALU.mult), reads=[yo, gnw], writes=[yo])
            k.op("act", lambda e: e.activation(out=sg.t[:, :], in_=L[2].t[:, :], func=AF.Silu), reads=[L[2]], writes=[sg])
            k.op("dve", lambda e: e.tensor_tensor(out=yb.t[:, :], in0=yo.t[:, :], in1=sg.t[:, :], op=ALU.mult), reads=[yo, sg], writes=[yb])
            k.mm_multi(psT, [(psT.t[:, 512 + j * 128:512 + (j + 1) * 128], [(yb.t[:, j * 128:(j + 1) * 128], ident_bf.t[:, :])]) for j in range(4)],
                       reads=[yb, ident_bf], transpose=True)
            k.op("act", lambda e: e.copy(out=yTs.t[:, :, t0:t0 + 128], in_=psT.t[:, 512:1024].rearrange("p (j t) -> p j t", j=4)), reads=[psT], writes=[yTs])
        for j in range(4):
            k.dma("sp", YT[0:nchunks, :, h * 4 + j, :].rearrange("t p k -> p t k"), yTs.t[:, j, 0:nchunks * 128].rearrange("p (t k) -> p t k", k=128), reads=[yTs])
        k.dma("sp", out_ret[h * 256:(h + 1) * 256, :].rearrange("(dc p) v -> p dc v", p=128), S.t[:, :, :], reads=[S])


def make_ret_consts():
    lg = np.array(RET_LOG_G, np.float64)
    i = np.arange(128, dtype=np.float64)
    rel_ = i[None, :] - i[:, None]
    innerT = np.where(rel_[None] >= 0, np.exp(rel_[None] * lg[:, None, None]), 0.0) / 16.0
    qdec = np.exp((i[:, None] + 1.0) * lg[None]) / 16.0
    kdec = np.exp((127.0 - i)[:, None] * lg[None])
    half = 128
    freq = (10000.0 ** (-np.arange(half, dtype=np.float32) / half)).astype(np.float32)
    pos = np.concatenate([np.arange(TP), np.full(NS, 16384)]).astype(np.float32)
    ang = pos[:, None] * freq[None]
    rope = np.concatenate([np.cos(ang), np.sin(ang)], 1)
    return {"innerT": innerT.astype(np.float32), "qdec": qdec.astype(np.float32), "kdec": kdec.astype(np.float32), "rope": rope.astype(np.float32)}


def to_fm(k, src, c0, n, dst, ps, ident, bf=False):
    idt = ident.t[:NS, :NS]
    k.mm_multi(ps, [(ps.t[:, j * NS:(j + 1) * NS], [(src.t[:NS, c0 + j * 128:c0 + (j + 1) * 128], idt)]) for j in range(n)],
               reads=[src, ident], transpose=True)
    k.op("dve", lambda e: e.tensor_copy(out=dst, in_=ps.t[:, 0:n * NS].rearrange("p (j s) -> p j s", s=NS)), reads=[ps], writes=[])


def to_tm(k, srcfm, n, dst, c0, pss, ident):
    for g in range((n + 3) // 4):
        ps = pss[g % len(pss)]
        k.mm_multi(ps, [(ps.t[:NS, jj * 128:(jj + 1) * 128], [(srcfm.t[:, g * 4 + jj, :], ident.t[:, :])]) for jj in range(4)],
                   reads=[srcfm, ident], transpose=True)
        k.op("act", lambda e: e.copy(out=dst.t[:NS, c0 + g * 512:c0 + (g + 1) * 512], in_=ps.t[:NS, :]), reads=[ps], writes=[dst])


def bc16(k, name, row_ap, n, en="act"):
    t = k.sb(name, [NS, n], F32)
    k.dma(en, t.t[:, :], row_ap.to_broadcast([NS, n]), writes=[t])
    return t


def phase_ssd_s(k, P0, prm, C, st_conv, st_ssm, YT, out_ssm, SBC):
    ident, ident_bf = C["ident"], C["ident_bf"]
    XB = 4096
    xa = k.sb("sxa", [NS, 6144], F32)
    outer_es = k.es
    with contextlib.ExitStack() as esc:
        k.es = esc
        CW = 1536
        cw = k.sb("scw", [NS, 4, CW], F32); cb = k.sb("scb", [NS, CW], F32)
        cbuf = k.sb("scbuf", [NS, 3, CW], F32); xr = k.sb("sxr", [NS, CW], F32)
        acc = k.sb("sacc", [NS, CW], F32); tmp = k.sb("stmp", [NS, CW], F32)
        for q in range(6144 // CW):
            c0 = q * CW
            for kk in range(4):
                k.dma("act", cw.t[:, kk, :], prm["ssd_conv_w"][kk:kk + 1, c0:c0 + CW].to_broadcast([NS, CW]), writes=[cw])
            k.dma("act", cb.t[:, :], prm["ssd_conv_b"][0:1, c0:c0 + CW].to_broadcast([NS, CW]), writes=[cb])
            k.dma("sp", cbuf.t[:, :, :], st_conv[:, :, c0:c0 + CW], writes=[cbuf])
            k.dma("sp", xr.t[:, :], P0[TP:T, XB + c0:XB + c0 + CW], writes=[xr])
            k.op("dve", lambda e: e.tensor_tensor(out=acc.t[:, :], in0=xr.t[:, :], in1=cw.t[:, 3, :], op=ALU.mult), reads=[xr, cw], writes=[acc])
            for kk in range(3):
                k.op("pool", lambda e: e.tensor_tensor(out=tmp.t[:, :], in0=cbuf.t[:, kk, :], in1=cw.t[:, kk, :], op=ALU.mult), reads=[cbuf, cw, tmp], writes=[tmp])
                k.op("dve", lambda e: e.tensor_tensor(out=acc.t[:, :], in0=acc.t[:, :], in1=tmp.t[:, :], op=ALU.add), reads=[acc, tmp], writes=[acc])
            k.op("dve", lambda e: e.tensor_tensor(out=acc.t[:, :], in0=acc.t[:, :], in1=cb.t[:, :], op=ALU.add), reads=[acc, cb], writes=[acc])
            k.op("act", lambda e: e.activation(out=xa.t[:, c0:c0 + CW], in_=acc.t[:, :], func=AF.Silu), reads=[acc], writes=[xa])
        k.barrier()
    k.es = outer_es
    k.dma("sp", SBC[:, :], xa.t[:, 4096:6144], reads=[xa])
    a_bc = bc16(k, "sa_bc", prm["ssd_a_log"][0:1, :], 64)
    k.op("act", lambda e: e.activation(out=a_bc.t[:, :], in_=a_bc.t[:, :], func=AF.Exp), reads=[a_bc], writes=[a_bc])
    d_bc = bc16(k, "sd_bc", prm["ssd_d"][0:1, :], 64)
    dtb = bc16(k, "sdtb", prm["ssd_dt_bias"][0:1, :], 64)
    dt = k.sb("sdt", [NS, 64], F32); ee = k.sb("see", [NS, 64], F32)
    k.dma("sp", dt.t[:, :], P0[TP:T, 10240:10304], writes=[dt])
    k.op("dve", lambda e: e.tensor_tensor(out=dt.t[:, :], in0=dt.t[:, :], in1=dtb.t[:, :], op=ALU.add), reads=[dt, dtb], writes=[dt])
    k.op("act", lambda e: e.activation(out=dt.t[:, :], in_=dt.t[:, :], func=AF.Exp), reads=[dt], writes=[dt])
    k.op("act", lambda e: e.activation(out=dt.t[:, :], in_=dt.t[:, :], func=AF.Ln, bias=1.0), reads=[dt], writes=[dt])
    k.op("dve", lambda e: e.tensor_tensor(out=ee.t[:, :], in0=dt.t[:, :], in1=a_bc.t[:, :], op=ALU.mult), reads=[dt, a_bc], writes=[ee])
    k.op("act", lambda e: e.activation(out=ee.t[:, :], in_=ee.t[:, :], func=AF.Exp, scale=-1.0), reads=[ee], writes=[ee])
    h3 = lambda t_: t_.t[:, 0:4096].rearrange("p (h d) -> p h d", h=64)
    b64 = lambda t_: t_.t[:, :].unsqueeze(2).to_broadcast([NS, 64, 64])
    xdt = k.sb("sxdt", [NS, 4096], F32); er = k.sb("ser", [NS, 4096], F32)
    k.op("dve", lambda e: e.tensor_tensor(out=h3(xdt), in0=h3(xa), in1=b64(dt), op=ALU.mult), reads=[xa, dt], writes=[xdt])
    k.op("dve", lambda e: e.tensor_copy(out=h3(er), in_=b64(ee)), reads=[ee], writes=[er])
    psA = k.ps("spsA", [128, 512], F32); psB = k.ps("spsB", [128, 512], F32)
    pstm = [k.ps(f"spstm{i}", [128, 512], F32) for i in range(2)]
    psbf = k.ps("spsbf", [128, 1024], BF16)
    xdtT = k.sb("sxdtT", [128, 32, NS], F32); eT = k.sb("seT", [128, 32, NS], F32)
    to_fm(k, xdt, 0, 32, xdtT.t[:, :, :], psA, ident); xdtT.w = k.last("dve")
    to_fm(k, er, 0, 32, eT.t[:, :, :], psB, ident); eT.w = k.last("dve")
    Yfm = k.sb("sYfm", [128, 32, NS], F32)
    St = [k.sb(f"sSt{i}", [128, 32, 128], F32) for i in range(2)]
    BCb = [k.sb(f"sBCb{i}", [128, 16, 128], F32) for i in range(2)]
    tm = k.sb("stm", [128, 32, 128], F32)
    g4 = lambda ap: ap.rearrange("p (g i) n -> p g i n", g=8)
    for s_ in range(NS):
        Sb_ = St[s_ % 2]; Bb = BCb[s_ % 2]
        k.dma("sp", Sb_.t[:, :, :], st_ssm[s_].rearrange("(j p) n -> p j n", p=128), writes=[Sb_])
        k.dma("act", Bb.t[:, :, :], SBC[s_:s_ + 1, :].rearrange("o (g n) -> o g n", n=128).to_broadcast([128, 16, 128]), writes=[Bb])
        xcol = xdtT.t[:, :, s_].rearrange("p (g i) -> p g i", g=8).unsqueeze(3).to_broadcast([128, 8, 4, 128])
        k.op("pool", lambda e: e.tensor_tensor(out=g4(tm.t[:, :, :]), in0=Bb.t[:, 0:8, :].unsqueeze(2).to_broadcast([128, 8, 4, 128]), in1=xcol, op=ALU.mult),
             reads=[Bb, xdtT, tm], writes=[tm])
        k.op("dve", lambda e: e.tensor_tensor(out=Sb_.t[:, :, :], in0=Sb_.t[:, :, :], in1=eT.t[:, :, s_].unsqueeze(2).to_broadcast([128, 32, 128]), op=ALU.mult),
             reads=[Sb_, eT], writes=[Sb_])
        k.op("dve", lambda e: e.tensor_tensor(out=Sb_.t[:, :, :], in0=Sb_.t[:, :, :], in1=tm.t[:, :, :], op=ALU.add), reads=[Sb_, tm], writes=[Sb_])
        k.dma("sp", out_ssm[s_].rearrange("(j p) n -> p j n", p=128), Sb_.t[:, :, :], reads=[Sb_])
        k.op("pool", lambda e: e.tensor_tensor(out=g4(tm.t[:, :, :]), in0=g4(Sb_.t[:, :, :]), in1=Bb.t[:, 8:16, :].unsqueeze(2).to_broadcast([128, 8, 4, 128]), op=ALU.mult),
             reads=[Sb_, Bb, tm], writes=[tm])
        k.op("dve", lambda e: e.tensor_reduce(out=Yfm.t[:, :, s_], in_=tm.t[:, :, :], axis=AX.X, op=ALU.add), reads=[tm], writes=[Yfm])
    y = k.sb("sy", [NS, 4096], F32)
    to_tm(k, Yfm, 32, y, 0, pstm, ident)
    k.op("dve", lambda e: e.tensor_tensor(out=h3(xdt), in0=h3(xa), in1=b64(d_bc), op=ALU.mult), reads=[xa, d_bc], writes=[xdt])
    k.op("dve", lambda e: e.tensor_tensor(out=y.t[:, :], in0=y.t[:, :], in1=xdt.t[:, :], op=ALU.add), reads=[y, xdt], writes=[y])
    z = k.sb("sz_", [NS, 4096], F32)
    k.dma("sp", z.t[:, :], P0[TP:T, 0:4096], writes=[z])
    k.op("act", lambda e: e.activation(out=z.t[:, :], in_=z.t[:, :], func=AF.Silu), reads=[z], writes=[z])
    k.op("dve", lambda e: e.tensor_tensor(out=y.t[:, :], in0=y.t[:, :], in1=z.t[:, :], op=ALU.mult), reads=[y, z], writes=[y])
    k.op("pool", lambda e: e.tensor_tensor(out=er.t[:, :], in0=y.t[:, :], in1=y.t[:, :], op=ALU.mult), reads=[y, er], writes=[er])
    ssq = k.sb("sssq", [NS, 8], F32)
    k.op("dve", lambda e: e.tensor_reduce(out=ssq.t[:, :], in_=er.t[:, :].rearrange("p (g c) -> p g c", g=8), axis=AX.X, op=ALU.add), reads=[er], writes=[ssq])
    k.op("dve", lambda e: e.tensor_scalar(out=ssq.t[:, :], in0=ssq.t[:, :], scalar1=1.0 / 512, scalar2=1e-5, op0=ALU.mult, op1=ALU.add), reads=[ssq], writes=[ssq])
    k.op("act", lambda e: e.activation(out=ssq.t[:, :], in_=ssq.t[:, :], func=AF.Sqrt), reads=[ssq], writes=[ssq])
    k.op("dve", lambda e: e.reciprocal(out=ssq.t[:, :], in_=ssq.t[:, :]), reads=[ssq], writes=[ssq])
    k.op("dve", lambda e: e.tensor_tensor(out=y.t[:, :].rearrange("p (g c) -> p g c", g=8), in0=y.t[:, :].rearrange("p (g c) -> p g c", g=8),
                                          in1=ssq.t[:, :].unsqueeze(2).to_broadcast([NS, 8, 512]), op=ALU.mult), reads=[y, ssq], writes=[y])
    nw = bc16(k, "snw", prm["ssd_norm_w"][0:1, :], 4096)
    y16 = k.sb("sy16", [NS, 4096], BF16)
    k.op("dve", lambda e: e.tensor_tensor(out=y16.t[:, :], in0=y.t[:, :], in1=nw.t[:, :], op=ALU.mult), reads=[y, nw], writes=[y16])
    yT = k.sb("syT", [128, 32, NS], BF16)
    to_fm(k, y16, 0, 32, yT.t[:, :, :], psbf, ident_bf); yT.w = k.last("dve")
    k.dma("sp", YT[TP // 128, :, 0:32, 0:NS], yT.t[:, :, :], reads=[yT])


def phase_rwkv_s(k, P0, prm, C, st_shift, st_wkv, YT, out_wkv, SV):
    ident, ident_bf = C["ident"], C["ident_bf"]
    true_outer = k.es
    v = k.sb("wv", [NS, 4096], F32); tA = k.sb("wtA", [NS, 4096], F32); bsum = k.sb("wbsum", [NS, 64], F32)
    psbf = k.ps("wpsbf", [128, 1024], BF16)
    psA = k.ps("wpsA", [128, 512], F32)
    pstm = [k.ps(f"wpstm{i}", [128, 512], F32) for i in range(2)]
    psL = [k.ps(f"wpsL{i}", [128, 512], F32) for i in range(2)]
    esA = contextlib.ExitStack()
    k.es = esA
    outer_es = esA
    r = k.sb("wr", [NS, 4096], F32); kt = k.sb("wk", [NS, 4096], F32)
    lw16 = k.sb("wlw16", [NS, 256], BF16)
    with contextlib.ExitStack() as esc:
        k.es = esc
        CW = 3136
        p = k.sb("wp", [NS, CW], F32); pp = k.sb("wpp", [NS, CW], F32); mu = k.sb("wmu", [NS, CW], F32)
        for q in range(4):
            c0 = q * CW
            k.dma("sp", p.t[:, :], P0[TP:T, PB + c0:PB + c0 + CW], writes=[p])
            k.dma("sp", pp.t[:, :], st_shift[:, c0:c0 + CW], writes=[pp])
            k.dma("act", mu.t[:, :], prm["rwkv_mu"][0:1, c0:c0 + CW].to_broadcast([NS, CW]), writes=[mu])
            k.op("dve", lambda e: e.tensor_tensor(out=pp.t[:, :], in0=pp.t[:, :], in1=p.t[:, :], op=ALU.subtract), reads=[pp, p], writes=[pp])
            k.op("dve", lambda e: e.tensor_tensor(out=pp.t[:, :], in0=pp.t[:, :], in1=mu.t[:, :], op=ALU.mult), reads=[pp, mu], writes=[pp])
            k.op("dve", lambda e: e.tensor_tensor(out=pp.t[:, :], in0=pp.t[:, :], in1=p.t[:, :], op=ALU.add), reads=[pp, p], writes=[pp])
            for (dst, d0) in ((r, 0), (kt, 4096), (v, 8192)):
                lo = max(c0, d0); hi = min(c0 + CW, d0 + 4096)
                if lo < hi:
                    k.op("act", lambda e: e.copy(out=dst.t[:, lo - d0:hi - d0], in_=pp.t[:, lo - c0:hi - c0]), reads=[pp], writes=[dst])
            if c0 + CW > 12288:
                o = 12288 - c0
                k.op("act", lambda e: e.activation(out=lw16.t[:, 0:128], in_=pp.t[:, o:o + 128], func=AF.Tanh), reads=[pp], writes=[lw16])
                k.op("act", lambda e: e.copy(out=lw16.t[:, 128:256], in_=pp.t[:, o + 128:o + 256]), reads=[pp], writes=[lw16])
        k.barrier()
    k.es = outer_es
    lT = k.sb("wlT", [128, 2, NS], BF16)
    to_fm(k, lw16, 0, 2, lT.t[:, :, :], psbf, ident_bf); lT.w = k.last("dve")
    sig = k.sb("wsig", [NS, 4096], F32); aa = k.sb("waa", [NS, 4096], F32)
    with contextlib.ExitStack() as esc:
        k.es = esc
        wup = k.sb("wwup", [128, 4096], BF16); aup = k.sb("waup", [128, 4096], BF16)
        k.dma("pool", wup.t[:, :], prm["rwkv_w_up"][:, :], writes=[wup])
        k.dma("pool", aup.t[:, :], prm["rwkv_a_up"][:, :], writes=[aup])
        w0 = bc16(k, "ww0", prm["rwkv_w0"][0:1, :], 4096); a0 = bc16(k, "wa0", prm["rwkv_a0"][0:1, :], 4096)
        for cg in range(8):
            cs = slice(cg * 512, (cg + 1) * 512)
            k.mm(psL[0], psL[0].t[:NS, :], [(lT.t[:, 0, :], wup.t[:, cs])], reads=[lT, wup])
            k.mm(psL[1], psL[1].t[:NS, :], [(lT.t[:, 1, :], aup.t[:, cs])], reads=[lT, aup])
            k.op("dve", lambda e: e.tensor_tensor(out=sig.t[:, cs], in0=psL[0].t[:NS, :], in1=w0.t[:, cs], op=ALU.add), reads=[psL[0], w0], writes=[sig])
            k.op("dve", lambda e: e.tensor_tensor(out=aa.t[:, cs], in0=psL[1].t[:NS, :], in1=a0.t[:, cs], op=ALU.add), reads=[psL[1], a0], writes=[aa])
        k.op("act", lambda e: e.activation(out=sig.t[:, :], in_=sig.t[:, :], func=AF.Sigmoid), reads=[sig], writes=[sig])
        k.op("act", lambda e: e.activation(out=aa.t[:, :], in_=aa.t[:, :], func=AF.Sigmoid), reads=[aa], writes=[aa])
        k.op("act", lambda e: e.activation(out=sig.t[:, :], in_=sig.t[:, :], func=AF.Exp, scale=-C1), reads=[sig], writes=[sig])
        k.barrier()
    k.es = outer_es
    h3 = lambda t_: t_.t[:, 0:4096].rearrange("p (h d) -> p h d", h=64)
    b64 = lambda t_: t_.t[:, :].unsqueeze(2).to_broadcast([NS, 64, 64])
    kk = k.sb("wkk", [NS, 4096], F32); kp = k.sb("wkp", [NS, 4096], F32)
    ssq = k.sb("wssq", [NS, 64], F32)
    with contextlib.ExitStack() as esc:
        k.es = esc
        k_k = bc16(k, "wk_k", prm["rwkv_k_k"][0:1, :], 4096); k_a = bc16(k, "wk_a", prm["rwkv_k_a"][0:1, :], 4096)
        r_k = bc16(k, "wr_k", prm["rwkv_r_k"][0:1, :], 4096)
        k.op("dve", lambda e: e.tensor_tensor(out=kk.t[:, :], in0=kt.t[:, :], in1=k_k.t[:, :], op=ALU.mult), reads=[kt, k_k], writes=[kk])
        k.op("pool", lambda e: e.tensor_tensor(out=tA.t[:, :], in0=kk.t[:, :], in1=kk.t[:, :], op=ALU.mult), reads=[kk, tA], writes=[tA])
        k.op("dve", lambda e: e.tensor_reduce(out=ssq.t[:, :], in_=h3(tA), axis=AX.X, op=ALU.add), reads=[tA], writes=[ssq])
        k.op("dve", lambda e: e.tensor_scalar(out=ssq.t[:, :], in0=ssq.t[:, :], scalar1=1e-24, scalar2=None, op0=ALU.max), reads=[ssq], writes=[ssq])
        k.op("act", lambda e: e.activation(out=ssq.t[:, :], in_=ssq.t[:, :], func=AF.Sqrt), reads=[ssq], writes=[ssq])
        k.op("dve", lambda e: e.reciprocal(out=ssq.t[:, :], in_=ssq.t[:, :]), reads=[ssq], writes=[ssq])
        k.op("dve", lambda e: e.tensor_tensor(out=h3(kk), in0=h3(kk), in1=b64(ssq), op=ALU.mult), reads=[kk, ssq], writes=[kk])
        k.op("dve", lambda e: e.scalar_tensor_tensor(out=tA.t[:, :], in0=aa.t[:, :], scalar=-1.0, in1=k_a.t[:, :], op0=ALU.add, op1=ALU.mult), reads=[aa, k_a, tA], writes=[tA])
        k.op("dve", lambda e: e.scalar_tensor_tensor(out=kp.t[:, :], in0=tA.t[:, :], scalar=1.0, in1=kt.t[:, :], op0=ALU.add, op1=ALU.mult), reads=[tA, kt], writes=[kp])
        k.op("dve", lambda e: e.tensor_tensor(out=aa.t[:, :], in0=kk.t[:, :], in1=aa.t[:, :], op=ALU.mult), reads=[kk, aa], writes=[aa])
        k.op("pool", lambda e: e.tensor_tensor(out=tA.t[:, :], in0=r.t[:, :], in1=kp.t[:, :], op=ALU.mult), reads=[r, kp, tA], writes=[tA])
        k.op("pool", lambda e: e.tensor_tensor(out=tA.t[:, :], in0=tA.t[:, :], in1=r_k.t[:, :], op=ALU.mult), reads=[tA, r_k], writes=[tA])
        k.op("dve", lambda e: e.tensor_reduce(out=bsum.t[:, :], in_=h3(tA), axis=AX.X, op=ALU.add), reads=[tA], writes=[bsum])
        k.barrier()
    k.es = outer_es
    for q, t_ in enumerate((kk, sig, aa, kp, r)):
        k.dma("sp", SV[q, :, :], t_.t[:, :], reads=[t_])
    k.barrier()
    esA.close()
    k.es = true_outer
    outer_es = true_outer
    vT = k.sb("wvT", [128, 32, NS], F32)
    to_fm(k, v, 0, 32, vT.t[:, :, :], psA, ident); vT.w = k.last("dve")
    Yfm = k.sb("wYfm", [128, 32, NS], F32)
    esB = contextlib.ExitStack()
    k.es = esB
    St = [k.sb(f"wSt{i}", [128, 32, 64], F32) for i in range(2)]
    BC = [k.sb(f"wBC{i}", [128, 5, 32, 64], F32) for i in range(2)]
    t1 = k.sb("wt1", [128, 32, 64], F32); t2 = k.sb("wt2", [128, 32, 64], F32)
    sk = k.sb("wsk", [128, 32], F32)
    for s_ in range(NS):
        S_ = St[s_ % 2]; B_ = BC[s_ % 2]
        k.dma("sp", S_.t[:, :, :], st_wkv[s_].rearrange("(j p) n -> p j n", p=128), writes=[S_])
        for q in range(5):
            src = SV[q, s_:s_ + 1, :].rearrange("o (j hh n) -> o hh j n", hh=2, n=64)
            for hh in range(2):
                k.dma("act" if hh == 0 else "sp", B_.t[hh * 64:(hh + 1) * 64, q, :, :], src[:, hh, :, :].to_broadcast([64, 32, 64]), writes=[B_])
        bcj = lambda col: col.unsqueeze(2).to_broadcast([128, 32, 64])
        k.op("dve", lambda e: e.tensor_tensor(out=t1.t[:, :, :], in0=S_.t[:, :, :], in1=B_.t[:, 0, :, :], op=ALU.mult), reads=[S_, B_], writes=[t1])
        k.op("dve", lambda e: e.tensor_reduce(out=sk.t[:, :], in_=t1.t[:, :, :], axis=AX.X, op=ALU.add), reads=[t1], writes=[sk])
        k.op("pool", lambda e: e.tensor_tensor(out=S_.t[:, :, :], in0=S_.t[:, :, :], in1=B_.t[:, 1, :, :], op=ALU.mult), reads=[S_, B_, t1], writes=[S_])
        k.op("dve", lambda e: e.tensor_tensor(out=t1.t[:, :, :], in0=B_.t[:, 2, :, :], in1=bcj(sk.t[:, :]), op=ALU.mult), reads=[B_, sk], writes=[t1])
        k.op("pool", lambda e: e.tensor_tensor(out=t2.t[:, :, :], in0=B_.t[:, 3, :, :], in1=bcj(vT.t[:, :, s_]), op=ALU.mult), reads=[B_, vT, t2], writes=[t2])
        k.op("dve", lambda e: e.tensor_tensor(out=S_.t[:, :, :], in0=S_.t[:, :, :], in1=t1.t[:, :, :], op=ALU.subtract), reads=[S_, t1], writes=[S_])
        k.op("dve", lambda e: e.tensor_tensor(out=S_.t[:, :, :], in0=S_.t[:, :, :], in1=t2.t[:, :, :], op=ALU.add), reads=[S_, t2], writes=[S_])
        k.dma("sp", out_wkv[s_].rearrange("(j p) n -> p j n", p=128), S_.t[:, :, :], reads=[S_])
        k.op("pool", lambda e: e.tensor_tensor(out=t2.t[:, :, :], in0=S_.t[:, :, :], in1=B_.t[:, 4, :, :], op=ALU.mult), reads=[S_, B_, t2], writes=[t2])
        k.op("dve", lambda e: e.tensor_reduce(out=Yfm.t[:, :, s_], in_=t2.t[:, :, :], axis=AX.X, op=ALU.add), reads=[t2], writes=[Yfm])
    k.barrier()
    esB.close()
    k.es = true_outer
    y = k.sb("wy", [NS, 4096], F32)
    to_tm(k, Yfm, 32, y, 0, pstm, ident)
    hs = k.sb("whs", [NS, 64], F32); hq = k.sb("whq", [NS, 64], F32); hm = k.sb("whm", [NS, 64], F32); hr = k.sb("whr", [NS, 64], F32)
    k.op("dve", lambda e: e.tensor_reduce(out=hs.t[:, :], in_=h3(y), axis=AX.X, op=ALU.add), reads=[y], writes=[hs])
    k.op("pool", lambda e: e.tensor_tensor(out=tA.t[:, :], in0=y.t[:, :], in1=y.t[:, :], op=ALU.mult), reads=[y, tA], writes=[tA])
    k.op("dve", lambda e: e.tensor_reduce(out=hq.t[:, :], in_=h3(tA), axis=AX.X, op=ALU.add), reads=[tA], writes=[hq])
    k.op("dve", lambda e: e.tensor_scalar(out=hm.t[:, :], in0=hs.t[:, :], scalar1=1.0 / 64, scalar2=None, op0=ALU.mult), reads=[hs], writes=[hm])
    k.op("dve", lambda e: e.tensor_tensor(out=hs.t[:, :], in0=hm.t[:, :], in1=hm.t[:, :], op=ALU.mult), reads=[hm], writes=[hs])
    k.op("dve", lambda e: e.scalar_tensor_tensor(out=hr.t[:, :], in0=hq.t[:, :], scalar=1.0 / 64, in1=hs.t[:, :], op0=ALU.mult, op1=ALU.subtract), reads=[hq, hs], writes=[hr])
    k.op("dve", lambda e: e.tensor_scalar(out=hr.t[:, :], in0=hr.t[:, :], scalar1=64e-5, scalar2=None, op0=ALU.add), reads=[hr], writes=[hr])
    k.op("act", lambda e: e.activation(out=hr.t[:, :], in_=hr.t[:, :], func=AF.Sqrt), reads=[hr], writes=[hr])
    k.op("dve", lambda e: e.reciprocal(out=hr.t[:, :], in_=hr.t[:, :]), reads=[hr], writes=[hr])
    k.op("dve", lambda e: e.tensor_tensor(out=h3(y), in0=h3(y), in1=b64(hm), op=ALU.subtract), reads=[y, hm], writes=[y])
    k.op("dve", lambda e: e.tensor_tensor(out=h3(y), in0=h3(y), in1=b64(hr), op=ALU.mult), reads=[y, hr], writes=[y])
    lnw = bc16(k, "wlnw", prm["rwkv_lnx_w"][0:1, :], 4096); lnb = bc16(k, "wlnb", prm["rwkv_lnx_b"][0:1, :], 4096)
    k.op("pool", lambda e: e.tensor_tensor(out=y.t[:, :], in0=y.t[:, :], in1=lnw.t[:, :], op=ALU.mult), reads=[y, lnw], writes=[y])
    k.op("pool", lambda e: e.tensor_tensor(out=y.t[:, :], in0=y.t[:, :], in1=lnb.t[:, :], op=ALU.add), reads=[y, lnb], writes=[y])
    k.op("dve", lambda e: e.tensor_tensor(out=h3(tA), in0=h3(v), in1=b64(bsum), op=ALU.mult), reads=[v, bsum, tA], writes=[tA])
    k.op("dve", lambda e: e.tensor_tensor(out=y.t[:, :], in0=y.t[:, :], in1=tA.t[:, :], op=ALU.add), reads=[y, tA], writes=[y])
    g = k.sb("wg", [NS, 4096], F32)
    k.dma("sp", g.t[:, :], P0[TP:T, GB:GB + 4096], reads=[], writes=[g])
    k.op("act", lambda e: e.activation(out=g.t[:, :], in_=g.t[:, :], func=AF.Silu), reads=[g], writes=[g])
    y16 = k.sb("wy16", [NS, 4096], BF16)
    k.op("dve", lambda e: e.tensor_tensor(out=y16.t[:, :], in0=y.t[:, :], in1=g.t[:, :], op=ALU.mult), reads=[y, g], writes=[y16])
    yT = k.sb("wyT", [128, 32, NS], BF16)
    to_fm(k, y16, 0, 32, yT.t[:, :, :], psbf, ident_bf); yT.w = k.last("dve")
    k.dma("sp", YT[TP // 128, :, 32:64, 0:NS], yT.t[:, :, :], reads=[yT])


def phase_ret_s(k, P1, prm, C, RC, st_ret, YT, out_ret):
    ident, ident_bf = C["ident"], C["ident_bf"]
    true_outer = k.es
    psA = k.ps("rpsA", [128, 512], F32); psB = k.ps("rpsB", [128, 512], F32)
    psr = [k.ps(f"rpsr{i}", [128, 512], F32) for i in range(2)]
    psbf = k.ps("rpsbf", [128, 1024], BF16)
    qT = k.sb("rqT", [128, 32, NS], F32); kT = k.sb("rkT", [128, 32, NS], F32)
    Y = k.sb("rY", [NS, 8192], F32)
    gdec = k.sb("rgdec", [128, 32], F32)
    for h in range(16):
        k.op("pool", lambda e: e.memset(gdec.t[:, 2 * h:2 * h + 2], float(np.exp(RET_LOG_G[h]))), writes=[gdec])
    k.op("pool", lambda e: e.memset(Y.t[:, :], 0.0), writes=[Y])
    with contextlib.ExitStack() as esc:
        k.es = esc
        X = k.sb("rX", [NS, 8192], F32); ro = k.sb("rro", [NS, 8192], F32); tmp = k.sb("rtmp_s", [NS, 32, 128], F32)
        rope = k.sb("rrope", [NS, 256], F32)
        k.dma("sp", X.t[:, :], P1[TP:T, 0:8192], writes=[X])
        k.dma("sp", rope.t[:, :], RC["rope"][TP:T, :], writes=[rope])
        X3 = X.t[:, :].rearrange("p (a d) -> p a d", d=256); R3 = ro.t[:, :].rearrange("p (a d) -> p a d", d=256)
        cosb = rope.t[:, 0:128].unsqueeze(1).to_broadcast([NS, 32, 128]); sinb = rope.t[:, 128:256].unsqueeze(1).to_broadcast([NS, 32, 128])
        k.op("dve", lambda e: e.tensor_tensor(out=R3[:, :, 0:128], in0=X3[:, :, 0:128], in1=cosb, op=ALU.mult), reads=[X, rope], writes=[ro])
        k.op("pool", lambda e: e.tensor_tensor(out=tmp.t[:, :, :], in0=X3[:, :, 128:256], in1=sinb, op=ALU.mult), reads=[X, rope], writes=[tmp])
        k.op("dve", lambda e: e.tensor_tensor(out=R3[:, :, 0:128], in0=R3[:, :, 0:128], in1=tmp.t[:, :, :], op=ALU.subtract), reads=[ro, tmp], writes=[ro])
        k.op("dve", lambda e: e.tensor_tensor(out=R3[:, :, 128:256], in0=X3[:, :, 0:128], in1=sinb, op=ALU.mult), reads=[X, rope], writes=[ro])
        k.op("pool", lambda e: e.tensor_tensor(out=tmp.t[:, :, :], in0=X3[:, :, 128:256], in1=cosb, op=ALU.mult), reads=[X, rope, tmp, ro], writes=[tmp])
        k.op("dve", lambda e: e.tensor_tensor(out=R3[:, :, 128:256], in0=R3[:, :, 128:256], in1=tmp.t[:, :, :], op=ALU.add), reads=[ro, tmp], writes=[ro])
        k.op("dve", lambda e: e.tensor_scalar(out=ro.t[:, 0:4096], in0=ro.t[:, 0:4096], scalar1=1.0 / 16, scalar2=None, op0=ALU.mult), reads=[ro], writes=[ro])
        to_fm(k, ro, 0, 32, qT.t[:, :, :], psA, ident); qT.w = k.last("dve")
        to_fm(k, ro, 4096, 32, kT.t[:, :, :], psB, ident); kT.w = k.last("dve")
        k.barrier()
    with contextlib.ExitStack() as esc:
        k.es = esc
        St = [k.sb(f"rSt{i}", [128, 16, 512], F32) for i in range(2)]
        vb = [k.sb(f"rvb{i}", [128, 8, 512], F32) for i in range(2)]
        tt = k.sb("rtt", [128, 16, 512], F32)
        u = 0
        for s_ in range(NS):
            for hf in range(2):
                S_ = St[u % 2]; V_ = vb[u % 2]; u += 1
                j0 = hf * 16
                k.dma("sp", S_.t[:, :, :], st_ret[s_, hf * 2048:(hf + 1) * 2048, :].rearrange("(j p) v -> p j v", p=128), writes=[S_])
                k.dma("act", V_.t[:, :, :], P1[TP + s_:TP + s_ + 1, 8192 + hf * 4096:8192 + (hf + 1) * 4096].rearrange("o (h v) -> o h v", v=512).to_broadcast([128, 8, 512]), writes=[V_])
                kcol = kT.t[:, j0:j0 + 16, s_].rearrange("p (h c) -> p h c", c=2).unsqueeze(3).to_broadcast([128, 8, 2, 512])
                k.op("pool", lambda e: e.tensor_tensor(out=tt.t[:, :, :].rearrange("p (h c) v -> p h c v", c=2), in0=V_.t[:, :, :].unsqueeze(2).to_broadcast([128, 8, 2, 512]), in1=kcol, op=ALU.mult),
                     reads=[V_, kT, tt], writes=[tt])
                k.op("dve", lambda e: e.tensor_tensor(out=S_.t[:, :, :], in0=S_.t[:, :, :], in1=gdec.t[:, j0:j0 + 16].unsqueeze(2).to_broadcast([128, 16, 512]), op=ALU.mult),
                     reads=[S_, gdec], writes=[S_])
                k.op("dve", lambda e: e.tensor_tensor(out=S_.t[:, :, :], in0=S_.t[:, :, :], in1=tt.t[:, :, :], op=ALU.add), reads=[S_, tt], writes=[S_])
                k.dma("sp", out_ret[s_, hf * 2048:(hf + 1) * 2048, :].rearrange("(j p) v -> p j v", p=128), S_.t[:, :, :], reads=[S_])
                for hl in range(8):
                    h = hf * 8 + hl
                    pr = psr[hl % 2]
                    k.mm(pr, pr.t[:NS, :], [(qT.t[:, 2 * h, :], S_.t[:, 2 * hl, :]), (qT.t[:, 2 * h + 1, :], S_.t[:, 2 * hl + 1, :])], reads=[qT, S_])
                    k.op("dve", lambda e: e.scalar_tensor_tensor(out=Y.t[:, h * 512:(h + 1) * 512], in0=pr.t[:NS, :], scalar=ident.t[:NS, s_:s_ + 1], in1=Y.t[:, h * 512:(h + 1) * 512],
                                                               op0=ALU.mult, op1=ALU.add), reads=[pr, ident, Y], writes=[Y])
        k.barrier()
    k.es = true_outer
    h3 = lambda t_: t_.t[:, :].rearrange("p (h d) -> p h d", h=16)
    b16 = lambda t_: t_.t[:, :].unsqueeze(2).to_broadcast([NS, 16, 512])
    sq = k.sb("rsq", [NS, 8192], F32)
    hs = k.sb("rhs", [NS, 16], F32); hq = k.sb("rhq", [NS, 16], F32); hm = k.sb("rhm", [NS, 16], F32); hr = k.sb("rhr", [NS, 16], F32)
    k.op("dve", lambda e: e.tensor_reduce(out=hs.t[:, :], in_=h3(Y), axis=AX.X, op=ALU.add), reads=[Y], writes=[hs])
    k.op("pool", lambda e: e.tensor_tensor(out=sq.t[:, :], in0=Y.t[:, :], in1=Y.t[:, :], op=ALU.mult), reads=[Y], writes=[sq])
    k.op("dve", lambda e: e.tensor_reduce(out=hq.t[:, :], in_=h3(sq), axis=AX.X, op=ALU.add), reads=[sq], writes=[hq])
    k.op("dve", lambda e: e.tensor_scalar(out=hm.t[:, :], in0=hs.t[:, :], scalar1=1.0 / 512, scalar2=None, op0=ALU.mult), reads=[hs], writes=[hm])
    k.op("dve", lambda e: e.tensor_tensor(out=hs.t[:, :], in0=hm.t[:, :], in1=hm.t[:, :], op=ALU.mult), reads=[hm], writes=[hs])
    k.op("dve", lambda e: e.scalar_tensor_tensor(out=hr.t[:, :], in0=hq.t[:, :], scalar=1.0 / 512, in1=hs.t[:, :], op0=ALU.mult, op1=ALU.subtract), reads=[hq, hs], writes=[hr])
    k.op("dve", lambda e: e.tensor_scalar(out=hr.t[:, :], in0=hr.t[:, :], scalar1=1e-6, scalar2=None, op0=ALU.add), reads=[hr], writes=[hr])
    k.op("act", lambda e: e.activation(out=hr.t[:, :], in_=hr.t[:, :], func=AF.Sqrt), reads=[hr], writes=[hr])
    k.op("dve", lambda e: e.reciprocal(out=hr.t[:, :], in_=hr.t[:, :]), reads=[hr], writes=[hr])
    k.op("dve", lambda e: e.tensor_tensor(out=h3(Y), in0=h3(Y), in1=b16(hm), op=ALU.subtract), reads=[Y, hm], writes=[Y])
    k.op("dve", lambda e: e.tensor_tensor(out=h3(Y), in0=h3(Y), in1=b16(hr), op=ALU.mult), reads=[Y, hr], writes=[Y])
    gnw = bc16(k, "rgnw", prm["ret_gn_w"][0:1, :], 8192)
    k.op("pool", lambda e: e.tensor_tensor(out=Y.t[:, :], in0=Y.t[:, :], in1=gnw.t[:, :], op=ALU.mult), reads=[Y, gnw], writes=[Y])
    k.dma("sp", sq.t[:, :], P1[TP:T, 16384:24576], writes=[sq])
    k.op("act", lambda e: e.activation(out=sq.t[:, :], in_=sq.t[:, :], func=AF.Silu), reads=[sq], writes=[sq])
    y16 = k.sb("ry16", [NS, 8192], BF16)
    k.op("dve", lambda e: e.tensor_tensor(out=y16.t[:, :], in0=Y.t[:, :], in1=sq.t[:, :], op=ALU.mult), reads=[Y, sq], writes=[y16])
    yT = k.sb("ryT", [128, 64, NS], BF16)
    to_fm(k, y16, 0, 32, yT.t[:, 0:32, :], psbf, ident_bf); yT.w = k.last("dve")
    to_fm(k, y16, 4096, 32, yT.t[:, 32:64, :], psbf, ident_bf); yT.w = k.last("dve")
    k.dma("sp", YT[TP // 128, :, 0:64, 0:NS], yT.t[:, :, :], reads=[yT])


USED_INPUTS = []

PARAM_SPECS = [
    ("ssd_conv_w", [4, 6144]), ("ssd_conv_b", [1, 6144]), ("ssd_dt_bias", [1, 64]), ("ssd_a_log", [1, 64]),
    ("ssd_d", [1, 64]), ("ssd_norm_w", [1, 4096]), ("rwkv_mu", [1, 12544]), ("rwkv_w0", [1, 4096]),
    ("rwkv_w_up", [128, 4096]), ("rwkv_a0", [1, 4096]), ("rwkv_a_up", [128, 4096]), ("rwkv_k_k", [1, 4096]),
    ("rwkv_k_a", [1, 4096]), ("rwkv_r_k", [1, 4096]), ("rwkv_lnx_w", [1, 4096]), ("rwkv_lnx_b", [1, 4096]),
    ("ab_ln_w", [1, 4096]), ("ab_ln_b", [1, 4096]), ("ret_gn_w", [1, 8192]), ("ret_ln_w", [1, 4096]),
    ("ret_ln_b", [1, 4096]),
]


def load_consts(k, cin):
    C = {}
    names = ["ident", "tri_le", "tri_gt", "ones"]
    call = k.sb("c_all", [128, 9, 128], F32)
    k.dma("sp", call.t[:, :, :], cin[:, :, :], writes=[call])
    for i, n in enumerate(names):
        C[n] = Buf(call.t[:, i, :], n, call)
    ib = k.sb("ident_bf", [128, 128], BF16)
    k.op("dve", lambda e: e.tensor_copy(out=ib.t[:, :], in_=call.t[:, 0, :]), reads=[call], writes=[ib])
    C["ident_bf"] = ib
    C["all"] = call
    return C


def build_program(dev=None):
    _, plan = _build(dev, None)
    nc, _ = _build(dev, plan)
    return nc


def _build(dev, plan):
    dev = dev or {}
    nc = bass.Bass("TRN2", target_bir_lowering=False)
    specs = dict(IN_SPECS + PARAM_SPECS + [("consts", [128, 9, 128]), ("innerT", [16, 128, 128]), ("qdec", [128, 16]), ("kdec", [128, 16]), ("rope", [T, 256])])

    class LazyIns(dict):
        def __missing__(self, n):
            v = nc.dram_tensor(n, specs[n], F32, kind="ExternalInput").ap()
            self[n] = v
            return v

    ins = LazyIns()
    USED_INPUTS.clear()
    outs = {n: nc.dram_tensor(n, s, F32, kind="ExternalOutput").ap() for n, s in OUT_SPECS}

    def scratch(name, shape, dt):
        kind = "Internal"
        if name in dev.get("as_input", ()):
            kind = "ExternalInput"
        elif name in dev.get("as_output", ()):
            kind = "ExternalOutput"
        return nc.dram_tensor(name, shape, dt, kind=kind).ap()

    P0 = scratch("P0", [T, AB_IN], F32)
    YT0 = scratch("YT0", [TP // 128 + 1, 128, 64, 128], BF16)
    stages = dev.get("stages", ["xT", "gemm0", "raw0", "ssd", "ssd_s", "rwkv", "rwkv_s", "out0", "ln0", "gemm1", "ret", "ret_s", "out1", "ln1"])
    with contextlib.ExitStack() as es:
        k = KB(nc, es, plan)
        C = load_consts(k, ins["consts"])
        if "xT" in stages or "gemm0" in stages:
            with contextlib.ExitStack() as es1:
                k.es = es1
                xT = k.sb("xT", [128, KC, T], BF16)
                with contextlib.ExitStack() as es2:
                    k.es = es2
                    phase_xT(k, ins["xp"], ins["xs"], xT, C["ident"])
                    k.barrier()
                with contextlib.ExitStack() as es2:
                    k.es = es2
                    wbufs = [k.sb(f"wb{i}", [128, KC, 512], BF16) for i in range(2)]
                    psb = [k.ps(f"psg{i}", [128, 512], F32) for i in range(8)]
                    stg = [k.sb(f"stg{i}", [128, 512], F32) for i in range(4)]
                    phase_gemm(k, xT, ins["ab_w_in"], AB_IN, P0, wbufs, psb, stg)
                    k.barrier()
        k.es = es
        if "raw0" in stages:
            k.dma("sp", outs["prompt_conv"][:, :], P0[TP - 3:TP, 4096:10240])
            k.dma("sp", outs["prompt_shift"][:, :], P0[TP - 1:TP, 10304:22848])
            k.dma("sp", outs["sample_shift"][:, :], P0[TP:T, 10304:22848])
            k.dma("sp", outs["sample_conv"][:, 2, :], P0[TP:T, 4096:10240])
            k.dma("sp", outs["sample_conv"][:, 0:2, :], ins["st_conv"][:, 1:3, :])
        if "ssd" in stages:
            with contextlib.ExitStack() as es1:
                k.es = es1
                phase_ssd(k, P0, ins, C, YT0, outs["prompt_ssm"], nchunks=dev.get("nchunks", TP // 128))
                k.barrier()
        SBC = scratch("SBC", [NS, 2048], F32)
        if "ssd_s" in stages:
            with contextlib.ExitStack() as es1:
                k.es = es1
                phase_ssd_s(k, P0, ins, C, ins["st_conv"], ins["st_ssm"], YT0, outs["sample_ssm"], SBC)
                k.barrier()
        if "rwkv" in stages:
            with contextlib.ExitStack() as es1:
                k.es = es1
                phase_rwkv(k, P0, ins, C, YT0, outs["prompt_wkv"], nchunks=dev.get("nchunks", TP // 128), ngroups=dev.get("ngroups", 8))
                k.barrier()
        SV = scratch("SV", [5, NS, 4096], F32)
        if "rwkv_s" in stages:
            with contextlib.ExitStack() as es1:
                k.es = es1
                phase_rwkv_s(k, P0, ins, C, ins["st_shift"], ins["st_wkv"], YT0, outs["sample_wkv"], SV)
                k.barrier()
        H0 = scratch("H0", [T, D], F32)
        X1 = scratch("X1", [T, D], F32)
        if "out0" in stages:
            with contextlib.ExitStack() as es1:
                k.es = es1
                phase_outproj(k, YT0, ins["ab_w_out"], H0)
                k.barrier()
        es_x1 = contextlib.ExitStack()
        x1T = None
        if "ln0" in stages or "gemm1" in stages:
            k.es = es_x1
            x1T = k.sb("x1T", [128, KC, T], BF16)
        if "ln0" in stages:
            if True:
                xin0 = lambda t: ins["xp"][t * 128:(t + 1) * 128, :] if t < TP // 128 else ins["xs"][:, :]
                dst0 = lambda t: X1[t * 128:(t + 1) * 128, :] if t < TP // 128 else X1[TP:T, :]
                with contextlib.ExitStack() as es2:
                    k.es = es2
                    phase_ln(k, H0, xin0, ins["ab_ln_w"][0:1, :], ins["ab_ln_b"][0:1, :], C, dst0, xT=x1T)
                    k.barrier()
        P1 = scratch("P1", [T, RET_IN], F32)
        YT1 = scratch("YT1", [TP // 128 + 1, 128, 64, 128], BF16)
        H1 = scratch("H1", [T, D], F32)
        if "gemm1" in stages:
            with contextlib.ExitStack() as es2:
                k.es = es2
                wbufs = [k.sb(f"wc{i}", [128, KC, 512], BF16) for i in range(2)]
                psb = [k.ps(f"psh{i}", [128, 512], F32) for i in range(8)]
                stg = [k.sb(f"sth{i}", [128, 512], F32) for i in range(4)]
                phase_gemm(k, x1T, ins["ret_w_in"], RET_IN, P1, wbufs, psb, stg)
                k.barrier()
        es_x1.close()
        k.es = es
        if "ret" in stages:
            with contextlib.ExitStack() as es1:
                k.es = es1
                phase_ret(k, P1, ins, C, ins, YT1, outs["prompt_ret"], nchunks=dev.get("nchunks", TP // 128), nheads=dev.get("nheads", 16))
                k.barrier()
        if "ret_s" in stages:
            with contextlib.ExitStack() as es1:
                k.es = es1
                phase_ret_s(k, P1, ins, C, ins, ins["st_ret"], YT1, outs["sample_ret"])
                k.barrier()
        if "out1" in stages:
            with contextlib.ExitStack() as es1:
                k.es = es1
                phase_outproj(k, YT1, ins["ret_w_out"], H1)
                k.barrier()
        if "ln1" in stages:
            with contextlib.ExitStack() as es1:
                k.es = es1
                xin1 = lambda t: X1[t * 128:(t + 1) * 128, :] if t < TP // 128 else X1[TP:T, :]
                dst1 = lambda t: outs["y_prompt"][t * 128:(t + 1) * 128, :] if t < TP // 128 else outs["y_sample"][:, :]
                phase_ln(k, H1, xin1, ins["ret_ln_w"][0:1, :], ins["ret_ln_b"][0:1, :], C, dst1, xT=None)
                k.barrier()
        k.es = es
        k.finish()
        print("instructions emitted:", k.ninstr, "sems:", k.nsem, "compute incs:", k.ninc)
        needed = set(k.needed)
    USED_INPUTS.extend(ins.keys())
    return nc, needed


def make_consts():
    j = np.arange(128)
    c = np.zeros((128, 9, 128), np.float32)
    le = (j[:, None] <= j[None, :]).astype(np.float32)
    lt = (j[:, None] < j[None, :]).astype(np.float32)
    c[:, 0, :] = np.eye(128, dtype=np.float32)
    c[:, 1, :] = le
    c[:, 2, :] = (j[:, None] > j[None, :]).astype(np.float32)
    c[:, 3, :] = 1.0
    c[:, 4, :] = lt
    c[:, 5, :] = lt
    c[:, 6, :] = le
    c[:, 7, :] = lt
    c[:, 8, :] = le
    return c


def make_in_map(inputs, c, shared=None):
    f = lambda a: np.ascontiguousarray(np.asarray(a, dtype=np.float32))
    if shared is None:
        shared = {}
        for n, shp in PARAM_SPECS:
            shared[n] = f(inputs[n][0]).reshape(shp)
        for n in ("ab_w_in", "ab_w_out", "ret_w_in", "ret_w_out"):
            shared[n] = f(inputs[n][0])
        shared["consts"] = make_consts()
        shared.update(make_ret_consts())
        shared["ident"] = np.eye(128, dtype=np.float32)
    b = c // 2
    sl = slice(c * NS, (c + 1) * NS)
    m = dict(shared)
    m["xp"] = f(inputs["x_prompt"][b])
    m["xs"] = f(inputs["x_sample"][sl, 0, :])
    m["st_conv"] = f(inputs["state_conv"][0, sl])
    m["st_ssm"] = f(inputs["state_ssm"][0, sl]).reshape(NS, 4096, 128)
    m["st_shift"] = f(inputs["state_shift"][0, sl, 0, :])
    m["st_wkv"] = f(inputs["state_wkv"][0, sl]).reshape(NS, 4096, 64)
    m["st_ret"] = f(inputs["state_ret"][0, sl]).reshape(NS, 4096, 512)
    return m, shared


_NC_CACHE = {}


def kernel(**inputs):
    if "nc" not in _NC_CACHE:
        _NC_CACHE["nc"] = build_program()
    nc = _NC_CACHE["nc"]
    in_maps = []
    shared = None
    for c in range(NCORES):
        m, shared = make_in_map(inputs, c, shared)
        in_maps.append({n: m[n] for n in USED_INPUTS})
    res = run_bass_kernel_spmd(nc, in_maps, core_ids=list(range(NCORES)))
    R = res.results
    pc = lambda name: [R[2 * b][name] for b in range(4)]
    sc = lambda name: [R[c][name] for c in range(NCORES)]
    y_prompt = np.stack(pc("y_prompt")).reshape(4, TP, D)
    y_sample = np.concatenate(sc("y_sample")).reshape(128, 1, D)
    prompt_conv = np.stack(pc("prompt_conv")).reshape(1, 4, 3, 6144)
    prompt_ssm = np.stack(pc("prompt_ssm")).reshape(1, 4, 64, 64, 128)
    prompt_shift = np.stack(pc("prompt_shift")).reshape(1, 4, 1, 12544)
    prompt_wkv = np.stack(pc("prompt_wkv")).reshape(1, 4, 64, 64, 64)
    prompt_ret = np.stack(pc("prompt_ret")).reshape(1, 4, 16, 256, 512)
    sample_conv = np.concatenate(sc("sample_conv")).reshape(1, 128, 3, 6144)
    sample_ssm = np.concatenate(sc("sample_ssm")).reshape(1, 128, 64, 64, 128)
    sample_shift = np.concatenate(sc("sample_shift")).reshape(1, 128, 1, 12544)
    sample_wkv = np.concatenate(sc("sample_wkv")).reshape(1, 128, 64, 64, 64)
    sample_ret = np.concatenate(sc("sample_ret")).reshape(1, 128, 16, 256, 512)
    return tuple(np.ascontiguousarray(a, dtype=np.float32) for a in (
        y_prompt, y_sample, prompt_conv, prompt_ssm, prompt_shift, prompt_wkv, prompt_ret,
        sample_conv, sample_ssm, sample_shift, sample_wkv, sample_ret))
```
